# Optimizing a Trainium2 kernel written in Bass

```python
import jax, jax.numpy as jnp
from jax import lax
import numpy as np

D_MODEL = 1024
BATCH = 16
SEQ = 2048
DEPTH = 1

GRID_W = 64
D_FF = 2816
EPS = 1e-6
GLA_HEADS = 4
GLA_DK = 128
GLA_DV = 256
GLA_RANK = 16
GLA_TAU = 16.0
GLA_CHUNK = 64
ATT_Q_HEADS = 8
ATT_KV_HEADS = 2
ATT_HEAD_DIM = 128
ATT_BLOCK = 128
ROPE_THETA = 10000.0
GLA_KEY_W = GLA_HEADS * GLA_DK
GLA_VAL_W = GLA_HEADS * GLA_DV
ATT_Q_W = ATT_Q_HEADS * ATT_HEAD_DIM
ATT_KV_W = ATT_KV_HEADS * ATT_HEAD_DIM
IN_SPLITS = (GLA_KEY_W, GLA_KEY_W, GLA_VAL_W, GLA_VAL_W, GLA_RANK, GLA_RANK,
             ATT_Q_W, ATT_KV_W, ATT_KV_W, D_MODEL, D_MODEL)
N_IN = sum(IN_SPLITS)
IN_OFFSETS = tuple(int(s) for s in np.cumsum(IN_SPLITS)[:-1])

kernel_name = "hybrid_gla_axial_gqa_macaron_encoder"


def _rmsnorm(x, g):
    x32 = x.astype(jnp.float32)
    y = x32 * lax.rsqrt(jnp.mean(x32 * x32, axis=-1, keepdims=True) + EPS)
    return (y * g.astype(jnp.float32)).astype(x.dtype)


def _swiglu(h, w_in, w_out):
    gate, up = jnp.split(h @ w_in, 2, axis=-1)
    return (jax.nn.silu(gate) * up) @ w_out


def _heads(t, n):
    b, l, _ = t.shape
    return t.reshape(b, l, n, -1).transpose(0, 2, 1, 3)


def _gla_direction(q, k, v, g, strict):
    B, H, L, dk = q.shape
    dv = v.shape[-1]
    C = GLA_CHUNK
    n = L // C

    def to_chunks(t):
        return t.reshape(B, H, n, C, t.shape[-1]).transpose(2, 0, 1, 3, 4)

    idx = jnp.arange(C)
    mask = (idx[:, None] > idx[None, :]) if strict else (idx[:, None] >= idx[None, :])

    def step(S, inp):
        qi, ki, vi, gi = inp
        b = jnp.cumsum(gi, axis=-2)
        b_last = b[..., -1:, :]
        o_inter = jnp.einsum('bhcd,bhde->bhce', qi * jnp.exp(b), S)
        rel = jnp.where(mask[:, :, None], b[..., :, None, :] - b[..., None, :, :], -jnp.inf)
        scores = jnp.einsum('bhid,bhjd,bhijd->bhij', qi, ki, jnp.exp(rel))
        o_intra = jnp.einsum('bhij,bhje->bhie', scores, vi)
        S_new = (jnp.exp(b_last[..., 0, :])[..., None] * S
                 + jnp.einsum('bhjd,bhje->bhde', ki * jnp.exp(b_last - b), vi))
        return S_new, o_inter + o_intra

    S0 = jnp.zeros((B, H, dk, dv), jnp.float32)
    _, o = lax.scan(step, S0, (to_chunks(q), to_chunks(k), to_chunks(v), to_chunks(g)))
    return o.transpose(1, 2, 0, 3, 4).reshape(B, H, L, dv)


def _axial_rope_tables(L):
    rows = L // GRID_W
    row_pos = jnp.repeat(jnp.arange(rows), GRID_W).astype(jnp.float32)
    col_pos = jnp.tile(jnp.arange(GRID_W), rows).astype(jnp.float32)
    half = ATT_HEAD_DIM // 2
    inv_freq = ROPE_THETA ** (-jnp.arange(0, half, 2, dtype=jnp.float32) / half)
    ang_r = row_pos[:, None] * inv_freq
    ang_c = col_pos[:, None] * inv_freq
    return jnp.cos(ang_r), jnp.sin(ang_r), jnp.cos(ang_c), jnp.sin(ang_c)


def _rot_half(x, cos, sin):
    x1, x2 = jnp.split(x, 2, axis=-1)
    c, s = cos[:, None, :], sin[:, None, :]
    return jnp.concatenate([x1 * c - x2 * s, x2 * c + x1 * s], axis=-1)


def _axial_rope(x, tables):
    cos_r, sin_r, cos_c, sin_c = tables
    x32 = x.astype(jnp.float32)
    xr, xc = jnp.split(x32, 2, axis=-1)
    out = jnp.concatenate([_rot_half(xr, cos_r, sin_r), _rot_half(xc, cos_c, sin_c)], axis=-1)
    return out.astype(x.dtype)


def _block_attention(q, k, v):
    B, Hk, G, L, hd = q.shape
    nb = L // ATT_BLOCK
    qb = q.reshape(B, Hk, G, nb, ATT_BLOCK, hd).transpose(3, 0, 1, 2, 4, 5)
    scale = hd ** -0.5

    def one(qi):
        s = jnp.einsum('bkgqd,bksd->bkgqs', qi, k).astype(jnp.float32) * scale
        p = jax.nn.softmax(s, axis=-1).astype(v.dtype)
        return jnp.einsum('bkgqs,bksd->bkgqd', p, v)

    o = lax.map(one, qb)
    return o.transpose(1, 2, 3, 0, 4, 5).reshape(B, Hk, G, L, hd)


def _token_mixers(u, w_in, up_f, bias_f, up_b, bias_b, gla_out_g, w_branch_a,
                  q_norm_g, k_norm_g, w_branch_b, w_out):
    B, L, _ = u.shape
    (gq, gk, gv, gr, za_f, za_b, aq, ak, av, ga, gb) = jnp.split(u @ w_in, IN_OFFSETS, axis=-1)

    log_a_f = jax.nn.log_sigmoid((za_f @ up_f + bias_f).astype(jnp.float32)) / GLA_TAU
    log_a_b = jax.nn.log_sigmoid((za_b @ up_b + bias_b).astype(jnp.float32)) / GLA_TAU
    q_h = _heads(gq, GLA_HEADS) * (GLA_DK ** -0.5)
    k_h = _heads(gk, GLA_HEADS)
    v_h = _heads(gv, GLA_HEADS)
    gf_h = _heads(log_a_f, GLA_HEADS)
    gb_h = _heads(log_a_b, GLA_HEADS)
    o_fwd = _gla_direction(q_h, k_h, v_h, gf_h, strict=False)
    flip = lambda t: t[..., ::-1, :]
    o_bwd = flip(_gla_direction(flip(q_h), flip(k_h), flip(v_h), flip(gb_h), strict=True))
    o = o_fwd + o_bwd
    o = _rmsnorm(o, gla_out_g[:, None, :]).astype(u.dtype)
    o = o.transpose(0, 2, 1, 3).reshape(B, L, GLA_VAL_W)
    y_a = (o * jax.nn.silu(gr)) @ w_branch_a

    tables = _axial_rope_tables(L)
    qa = aq.reshape(B, L, ATT_Q_HEADS, ATT_HEAD_DIM)
    ka = ak.reshape(B, L, ATT_KV_HEADS, ATT_HEAD_DIM)
    va = av.reshape(B, L, ATT_KV_HEADS, ATT_HEAD_DIM)
    qa = _axial_rope(_rmsnorm(qa, q_norm_g), tables)
    ka = _axial_rope(_rmsnorm(ka, k_norm_g), tables)
    group = ATT_Q_HEADS // ATT_KV_HEADS
    qa = qa.transpose(0, 2, 1, 3).reshape(B, ATT_KV_HEADS, group, L, ATT_HEAD_DIM)
    ka = ka.transpose(0, 2, 1, 3)
    va = va.transpose(0, 2, 1, 3)
    ob = _block_attention(qa, ka, va)
    ob = ob.reshape(B, ATT_Q_HEADS, L, ATT_HEAD_DIM).transpose(0, 2, 1, 3).reshape(B, L, ATT_Q_W)
    y_b = ob @ w_branch_b

    merged = jax.nn.sigmoid(ga) * y_a + jax.nn.sigmoid(gb) * y_b
    return merged @ w_out


def setup_inputs(seed: int = 0) -> dict:
    key = jax.random.key(seed)
    ks = iter(jax.random.split(key, 32))
    f32 = jnp.float32

    def w(shape, fan_in):
        return jax.random.normal(next(ks), shape, f32) * (fan_in ** -0.5)

    def gain(shape):
        return 1.0 + 0.05 * jax.random.normal(next(ks), shape, f32)

    def bias(shape):
        return 0.1 * jax.random.normal(next(ks), shape, f32)

    Dp = DEPTH
    return {
        "x": jax.random.normal(next(ks), (BATCH, SEQ, D_MODEL), f32),
        "ffn1_pre_g": gain((Dp, D_MODEL)),
        "ffn1_w_in": w((Dp, D_MODEL, 2 * D_FF), D_MODEL),
        "ffn1_w_out": w((Dp, D_FF, D_MODEL), D_FF),
        "ffn1_post_g": gain((Dp, D_MODEL)),
        "mix_pre_g": gain((Dp, D_MODEL)),
        "w_in": w((Dp, D_MODEL, N_IN), D_MODEL),
        "gla_decay_up_f": w((Dp, GLA_RANK, GLA_KEY_W), GLA_RANK),
        "gla_decay_bias_f": bias((Dp, GLA_KEY_W)),
        "gla_decay_up_b": w((Dp, GLA_RANK, GLA_KEY_W), GLA_RANK),
        "gla_decay_bias_b": bias((Dp, GLA_KEY_W)),
        "gla_out_g": gain((Dp, GLA_HEADS, GLA_DV)),
        "w_branch_a": w((Dp, GLA_VAL_W, D_MODEL), GLA_VAL_W),
        "att_q_norm_g": gain((Dp, ATT_HEAD_DIM)),
        "att_k_norm_g": gain((Dp, ATT_HEAD_DIM)),
        "w_branch_b": w((Dp, ATT_Q_W, D_MODEL), ATT_Q_W),
        "w_out": w((Dp, D_MODEL, D_MODEL), D_MODEL),
        "mix_post_g": gain((Dp, D_MODEL)),
        "ffn2_pre_g": gain((Dp, D_MODEL)),
        "ffn2_w_in": w((Dp, D_MODEL, 2 * D_FF), D_MODEL),
        "ffn2_w_out": w((Dp, D_FF, D_MODEL), D_FF),
        "ffn2_post_g": gain((Dp, D_MODEL)),
    }


def reference(x, ffn1_pre_g, ffn1_w_in, ffn1_w_out, ffn1_post_g, mix_pre_g, w_in,
              gla_decay_up_f, gla_decay_bias_f, gla_decay_up_b, gla_decay_bias_b,
              gla_out_g, w_branch_a, att_q_norm_g, att_k_norm_g, w_branch_b, w_out,
              mix_post_g, ffn2_pre_g, ffn2_w_in, ffn2_w_out, ffn2_post_g):
    h = x
    for l in range(DEPTH):
        f1 = _swiglu(_rmsnorm(h, ffn1_pre_g[l]), ffn1_w_in[l], ffn1_w_out[l])
        h = h + 0.5 * _rmsnorm(f1, ffn1_post_g[l])
        m = _token_mixers(_rmsnorm(h, mix_pre_g[l]), w_in[l],
                          gla_decay_up_f[l], gla_decay_bias_f[l],
                          gla_decay_up_b[l], gla_decay_bias_b[l],
                          gla_out_g[l], w_branch_a[l],
                          att_q_norm_g[l], att_k_norm_g[l], w_branch_b[l], w_out[l])
        h = h + _rmsnorm(m, mix_post_g[l])
        f2 = _swiglu(_rmsnorm(h, ffn2_pre_g[l]), ffn2_w_in[l], ffn2_w_out[l])
        h = h + 0.5 * _rmsnorm(f2, ffn2_post_g[l])
    return h
```

```python
from contextlib import ExitStack

import numpy as np
import concourse.bass as bass
import concourse.mybir as mybir
from concourse.bass_utils import run_bass_kernel_spmd

F32 = mybir.dt.float32
BF16 = mybir.dt.bfloat16
AF = mybir.ActivationFunctionType
ALU = mybir.AluOpType
AX = mybir.AxisListType

NCORES = 8
DEN_QUADS = False
D = 1024
L = 2048
SEQ_PER_CORE = 2
NTOK = L * SEQ_PER_CORE
DFF = 2816
NFC = DFF // 128
EPS = 1e-6
N_IN = 6688
OFF_GQ, OFF_GK, OFF_GV, OFF_GR, OFF_ZF, OFF_ZB = 0, 512, 1024, 2048, 3072, 3088
OFF_AQ, OFF_AK, OFF_AV, OFF_GA, OFF_GB = 3104, 4128, 4384, 4640, 5664

ENGS = ("pe", "act", "dve", "pool", "sp")


class Tile:
    __slots__ = ("name", "w", "r", "rdma")

    def __init__(self, name):
        self.name = name
        self.w = None
        self.r = {}
        self.rdma = {}


class Buf:
    def __init__(self, ap, t):
        self.ap = ap
        self.t = t

    def __getitem__(self, k):
        return Buf(self.ap[k], self.t)

    def bc(self, dt):
        return Buf(self.ap.bitcast(dt), self.t)

    def re(self, pat, **kw):
        return Buf(self.ap.rearrange(pat, **kw), self.t)

    def bcast(self, shape):
        return Buf(self.ap.to_broadcast(list(shape)), self.t)


class OpRec:
    __slots__ = ("eng", "idx", "sig", "waits", "fn", "dma_inc", "rank")

    def __init__(self, eng, idx, fn):
        self.eng = eng
        self.idx = idx
        self.sig = False
        self.waits = []
        self.fn = fn
        self.dma_inc = None
        self.rank = 0


class Prog:
    def __init__(self):
        self.q = {e: [] for e in ENGS}
        self.lastc = {e: None for e in ENGS}
        self.seen = {e: {} for e in ENGS}
        self.dma_cnt = {}
        self.dma_owner = {}
        self.nbar = 0
        self.semmap = {}
        self.free_phys = []
        self.nphys = 0

    def _need(self, eng, need, marker, same_eng_ok):
        if marker is None:
            return
        if marker[0] == "op":
            op = marker[1]
            if op.eng == eng and (same_eng_ok or eng == "pe"):
                return
            s, v = "e_" + op.eng, op.idx
            cur = need.get(s)
            if cur is None or cur[0] < v:
                need[s] = (v, op)
        else:
            s, v = marker[1], marker[2]
            cur = need.get(s)
            if cur is None or cur[0] < v:
                need[s] = (v, None)

    def _commit(self, eng, need):
        out = []
        seen = self.seen[eng]
        for s, (v, op) in need.items():
            if seen.get(s, -1) < v:
                seen[s] = v
                if op is not None:
                    op.sig = True
                    out.append((s, op))
                else:
                    out.append((s, v))
        return out

    def _deps(self, eng, reads, writes):
        need = {}
        for t in reads:
            self._need(eng, need, t.w, False)
        for t in writes:
            self._need(eng, need, t.w, True)
            for op in t.r.values():
                self._need(eng, need, ("op", op), True)
            for s, v in t.rdma.items():
                self._need(eng, need, ("dma", s, v), True)
        return self._commit(eng, need)

    def op(self, eng, fn, reads=(), writes=()):
        rt = [b.t for b in reads]
        wt = [b.t for b in writes]
        waits = self._deps(eng, rt, wt)
        rec = OpRec(eng, len(self.q[eng]), fn)
        rec.waits = waits
        self.q[eng].append(rec)
        self.lastc[eng] = rec
        for t in rt:
            t.r[eng] = rec
        for t in wt:
            t.w = ("op", rec)
            t.r = {}
            t.rdma = {}
        return rec

    def dma(self, eng, out, in_, sem, after=(), **kw):
        if sem not in self.semmap:
            if self.free_phys:
                self.semmap[sem] = self.free_phys.pop(0)
            else:
                self.semmap[sem] = f"d{self.nphys}"
                self.nphys += 1
        sem = self.semmap[sem]
        rt, wt = [in_.t], [out.t]
        need = {}
        self._need(eng, need, in_.t.w, False)
        for b_ in after:
            self._need(eng, need, b_.t.w, False)
        self._need(eng, need, out.t.w, True)
        for op in out.t.r.values():
            self._need(eng, need, ("op", op), True)
        for s_, v_ in out.t.rdma.items():
            self._need(eng, need, ("dma", s_, v_), True)
        cur = self.dma_cnt.get(sem, 0)
        if self.dma_owner.get(sem) is not out.t and cur > 0:
            self._need(eng, need, ("dma", sem, cur), True)
        waits = self._commit(eng, need)
        self.dma_owner[sem] = out.t
        o_ap, i_ap = out.ap, in_.ap
        rec = OpRec(eng, len(self.q[eng]), lambda e: e.dma_start(out=o_ap, in_=i_ap, **kw))
        rec.waits = waits
        self.dma_cnt[sem] = cur + 16
        rec.dma_inc = sem
        self.q[eng].append(rec)
        in_.t.rdma[sem] = cur + 16
        out.t.w = ("dma", sem, cur + 16)
        out.t.r = {}
        out.t.rdma = {}
        return rec

    def _all_done_waits(self, eng):
        need = {}
        for e in ("pe", "act", "dve", "pool"):
            if self.lastc[e] is not None:
                self._need(eng, need, ("op", self.lastc[e]), False)
        for s, v in self.dma_cnt.items():
            self._need(eng, need, ("dma", s, v), False)
        return self._commit(eng, need)

    def barrier(self):
        waits = self._all_done_waits("sp")
        self.free_phys = sorted(set(self.free_phys) | set(self.semmap.values()), key=lambda n: int(n[1:]))
        self.semmap = {}
        self.nbar += 1
        nb = self.nbar
        rec = OpRec("sp", len(self.q["sp"]), ("bar", nb))
        rec.waits = waits
        self.q["sp"].append(rec)
        for e in ("pe", "act", "dve", "pool"):
            r2 = OpRec(e, len(self.q[e]), None)
            r2.waits = [("bar", nb)]
            self.q[e].append(r2)
            for s, v in self.seen["sp"].items():
                if self.seen[e].get(s, -1) < v:
                    self.seen[e][s] = v

    def final_wait(self):
        rec = OpRec("sp", len(self.q["sp"]), None)
        rec.waits = self._all_done_waits("sp")
        self.q["sp"].append(rec)

    def emit(self, nc, es):
        semnames = set(["bar"])
        for e in ENGS:
            semnames.add("e_" + e)
            k = 0
            for rec in self.q[e]:
                if rec.sig:
                    k += 1
                    rec.rank = k
                for s, _ in rec.waits:
                    semnames.add(s)
                if rec.dma_inc:
                    semnames.add(rec.dma_inc)
        sems = {s: es.enter_context(nc.semaphore(s)) for s in sorted(semnames)}
        block = es.enter_context(nc.Block())
        engmap = {"pe": block.tensor, "act": block.scalar, "dve": block.vector,
                  "pool": block.gpsimd, "sp": block.sync}

        def mk(ename):
            def body(eng):
                for rec in self.q[ename]:
                    for s, v in rec.waits:
                        eng.wait_ge(sems[s], v.rank if isinstance(v, OpRec) else v)
                    if rec.fn is None:
                        continue
                    if isinstance(rec.fn, tuple):
                        eng.sem_inc(sems["bar"], 1)
                        continue
                    ins = rec.fn(eng)
                    if rec.dma_inc:
                        ins.then_inc(sems[rec.dma_inc], 16)
                    elif rec.sig:
                        ins.then_inc(sems["e_" + ename], 1)
            return body

        for ename in ENGS:
            engmap[ename](mk(ename))


class Builder:
    def __init__(self, debug=()):
        self.debug = set(debug)
        self.nc = bass.Bass("TRN2", target_bir_lowering=False)
        self.p = Prog()
        self.es = ExitStack()
        self.arena_off = 0
        self.arena_floor = 0

    def dram_in(self, name, shape, dt=F32):
        h = self.nc.dram_tensor(name, list(shape), dt, kind="ExternalInput")
        return Buf(h.ap(), Tile(name))

    def dram_out(self, name, shape, dt=F32):
        h = self.nc.dram_tensor(name, list(shape), dt, kind="ExternalOutput")
        return Buf(h.ap(), Tile(name))

    def dram_scr(self, name, shape, dt):
        kind = "ExternalOutput" if name in self.debug else "Internal"
        h = self.nc.dram_tensor(name, list(shape), dt, kind=kind)
        return Buf(h.ap(), Tile(name))

    def sb(self, name, cols, dt=F32):
        nby = cols * (4 if dt == F32 else 2)
        n4 = (nby + 31) // 32 * 8
        off = self.arena_off
        assert off + n4 <= self.arena_cols, f"SBUF arena overflow at {name}: {(off + n4) * 4} B"
        self.arena_off += n4
        ap = self.arena[:, off:off + n4]
        if dt != F32:
            ap = ap.bitcast(dt)[:, 0:cols]
        else:
            ap = ap[:, 0:cols]
        return Buf(ap, Tile(name))

    def phase_reset(self):
        self.p.barrier()
        self.arena_off = self.arena_floor

    def mm(self, out, lhsT, rhs, start, stop, extra_reads=()):
        o, a, b = out.ap, lhsT.ap, rhs.ap
        self.p.op("pe", lambda e: e.matmul(o, lhsT=a, rhs=b, start=start, stop=stop),
                  reads=[lhsT, rhs, *extra_reads], writes=[out])

    def tr(self, out, in_, ident):
        o, a, i = out.ap, in_.ap, ident.ap
        self.p.op("pe", lambda e: e.transpose(o, a, i), reads=[in_, ident], writes=[out])

    def act(self, out, in_, func, scale=1.0, bias=0.0, accum=None, eng="act"):
        o, a = out.ap, in_.ap
        reads = [in_]
        writes = [out]
        kw = {}
        if isinstance(scale, Buf):
            reads.append(scale)
            kw["scale"] = scale.ap
        else:
            kw["scale"] = float(scale)
        if isinstance(bias, Buf):
            reads.append(bias)
            kw["bias"] = bias.ap
        elif bias != 0.0:
            kw["bias"] = float(bias)
        if accum is not None:
            writes.append(accum)
            kw["accum_out"] = accum.ap
        self.p.op(eng, lambda e: e.activation(out=o, in_=a, func=func, **kw), reads=reads, writes=writes)

    def tt(self, out, in0, in1, op, eng="dve"):
        o, a, b = out.ap, in0.ap, in1.ap
        self.p.op(eng, lambda e: e.tensor_tensor(out=o, in0=a, in1=b, op=op), reads=[in0, in1], writes=[out])

    def ts(self, out, in0, s1, op0, s2=None, op1=None, eng="dve"):
        o, a = out.ap, in0.ap
        reads = [in0]
        if isinstance(s1, Buf):
            reads.append(s1)
            s1v = s1.ap
        else:
            s1v = float(s1)
        if isinstance(s2, Buf):
            reads.append(s2)
            s2v = s2.ap
        elif s2 is None:
            s2v = None
        else:
            s2v = float(s2)
        if op1 is None:
            self.p.op(eng, lambda e: e.tensor_scalar(out=o, in0=a, scalar1=s1v, scalar2=None, op0=op0),
                      reads=reads, writes=[out])
        else:
            self.p.op(eng, lambda e: e.tensor_scalar(out=o, in0=a, scalar1=s1v, scalar2=s2v, op0=op0, op1=op1),
                      reads=reads, writes=[out])

    def stt(self, out, in0, scalar, in1, op0, op1, eng="dve"):
        o, a, b = out.ap, in0.ap, in1.ap
        reads = [in0, in1]
        if isinstance(scalar, Buf):
            reads.append(scalar)
            sv = scalar.ap
        else:
            sv = float(scalar)
        self.p.op(eng, lambda e: e.scalar_tensor_tensor(out=o, in0=a, scalar=sv, in1=b, op0=op0, op1=op1),
                  reads=reads, writes=[out])

    def copy(self, out, in_, eng="dve"):
        o, a = out.ap, in_.ap
        self.p.op(eng, lambda e: e.tensor_copy(out=o, in_=a), reads=[in_], writes=[out])

    def recip(self, out, in_):
        o, a = out.ap, in_.ap
        self.p.op("dve", lambda e: e.reciprocal(out=o, in_=a), reads=[in_], writes=[out])

    def reduce(self, out, in_, op=ALU.add, eng="dve"):
        o, a = out.ap, in_.ap
        self.p.op(eng, lambda e: e.tensor_reduce(out=o, in_=a, axis=AX.X, op=op), reads=[in_], writes=[out])

    def memset(self, out, val, eng="pool"):
        o = out.ap
        self.p.op(eng, lambda e: e.memset(o, val), reads=[], writes=[out])

    def dma(self, out, in_, sem, eng="sp", **kw):
        self.p.dma(eng, out, in_, sem, **kw)

    def setup(self):
        nc, es = self.nc, self.es
        self.arena_cols = 52 * 1024 - 512
        self.arena = es.enter_context(nc.sbuf_tensor("arena", [128, self.arena_cols], F32))
        self.banks = []
        for i in range(8):
            ps = es.enter_context(nc.psum_tensor(f"ps{i}", [128, 512], F32))
            self.banks.append(Buf(ps[:, :], Tile(f"ps{i}")))
        self.identf = self.sb("identf", 128, F32)
        self.ident = self.sb("ident", 128, BF16)
        self.ones_bf = self.sb("ones_bf", 128, BF16)
        self.ones_f = self.sb("ones_f", 128, F32)
        self.memset(self.identf, 0.0)
        ia = self.identf.ap
        self.p.op("pool", lambda e: e.affine_select(out=ia, in_=ia, pattern=[[-1, 128]], compare_op=ALU.not_equal,
                                                   fill=1.0, base=0, channel_multiplier=1),
                  reads=[self.identf], writes=[self.identf])
        self.copy(self.ident, self.identf)
        self.memset(self.ones_f, 1.0)
        self.copy(self.ones_bf, self.ones_f)
        self.arena_floor = self.arena_off


    def norm_scale(self, xi, xsb, stb, junk=None):
        self.act(xsb, xi, AF.Square, accum=stb[:, 0:1])
        self.act(stb[:, 1:2], stb[:, 0:1], AF.Sqrt, scale=1.0 / D, bias=self.eps_t)
        self.recip(stb[:, 2:3], stb[:, 1:2])
        self.act(xsb, xi, AF.Copy, scale=stb[:, 2:3])

    def transpose_gain(self, xsb, bank, gT, dstv):
        bv = bank.bc(BF16).re("p (c t) -> p c t", c=8)
        for c in range(8):
            self.tr(bv[:, c, :], xsb[:, c * 128:(c + 1) * 128], self.ident)
        self.tt(dstv, bv, gT.re("p (c o) -> p c o", o=1).bcast([128, 8, 128]), ALU.mult)

    def post_norm_residual(self, banks2, s2, junk, pg, o, xr, half, lnexp=False):
        for hh in range(2):
            self.act(junk[:, 0:512], banks2[hh], AF.Square, accum=s2[:, hh:hh + 1])
        self.tt(s2[:, 2:3], s2[:, 0:1], s2[:, 1:2], ALU.add)
        if lnexp:
            assert not half
            self.act(s2[:, 3:4], s2[:, 2:3], AF.Ln, scale=1.0 / D, bias=self.eps_t)
            self.act(s2[:, 4:5], s2[:, 3:4], AF.Exp, scale=-0.5)
        elif half:
            self.act(s2[:, 3:4], s2[:, 2:3], AF.Sqrt, scale=4.0 / D, bias=self.eps4_t)
        else:
            self.act(s2[:, 3:4], s2[:, 2:3], AF.Sqrt, scale=1.0 / D, bias=self.eps_t)
        if not lnexp:
            self.recip(s2[:, 4:5], s2[:, 3:4])
        for hh in range(2):
            self.stt(o[:, hh * 512:(hh + 1) * 512], banks2[hh], s2[:, 4:5], pg[:, hh * 512:(hh + 1) * 512],
                     ALU.mult, ALU.mult)
        self.tt(o, o, xr, ALU.add)

    def ffn_phase(self, src, dst, pre_g, wkeys, post_g, tag, nblocks=NTOK // 512, after_weights=None):
        p = self.p
        k_in, k_out = wkeys
        pre = k_in in self.WB
        w_in = self.WB[k_in] if pre else self.I[k_in]
        w_out = self.WB[k_out] if pre else self.I[k_out]
        weng = "sp" if pre else "pool"
        w_in_v = w_in.re("(c p) f -> p c f", p=128)
        w_out_v = w_out.re("(c p) f -> p c f", p=128)
        gT = self.sb(f"gT{tag}", 8, F32)
        self.dma(gT, pre_g.re("(c p) -> p c", p=128), f"sm{tag}", allow_slow_non_contiguous=True)
        pg = self.sb(f"pg{tag}", D, F32)
        self.dma(pg, Buf(post_g.ap.partition_broadcast(128), post_g.t), f"sm{tag}")
        xin = [self.sb(f"xin{i}{tag}", D, F32) for i in range(2)]
        xs = [self.sb(f"xs{i}{tag}", D, BF16) for i in range(4)]
        junk = self.sb(f"junk{tag}", 512, BF16)
        xT = self.sb(f"xT{tag}", 8 * 512, BF16)
        xTv = xT.re("p (c t) -> p c t", c=8)
        actT = self.sb(f"actT{tag}", NFC * 512, BF16)
        actTv = actT.re("p (c t) -> p c t", c=NFC)
        sg = [self.sb(f"sg{i}{tag}", 512, F32) for i in range(2)]
        st = [self.sb(f"st{i}{tag}", 8, F32) for i in range(2)]
        st2 = [self.sb(f"st2{i}{tag}", 8, F32) for i in range(2)]
        xres = [self.sb(f"xres{i}{tag}", D, F32) for i in range(2)]
        ot = [self.sb(f"ot{i}{tag}", D, F32) for i in range(2)]
        B = self.banks

        def front_elem(b, tt):
            k = b * 4 + tt
            xi = xin[k % 2]
            self.dma(xi, src[k * 128:(k + 1) * 128, :], f"xin{k % 2}{tag}")
            self.norm_scale(xi, xs[k % 4], st[k % 2], junk)

        def front_pe(b, tt):
            k = b * 4 + tt
            self.transpose_gain(xs[k % 4], B[k % 2], gT, xTv[:, :, tt * 128:(tt + 1) * 128])

        for tt in range(4):
            front_elem(0, tt)
        gb_ = [0, 3, 9, 15, 22]
        W1g, jgrp = [], {}
        for g in range(4):
            j0, j1 = gb_[g], gb_[g + 1]
            gw = (j1 - j0) * 128
            Wt = self.sb(f"W1{tag}g{g}", 8 * 2 * gw, BF16).re("p (c f) -> p c f", c=8)
            self.dma(Wt[:, :, 0:gw], w_in_v[:, :, j0 * 128:j1 * 128], f"W1{tag}g{g}", eng=weng)
            self.dma(Wt[:, :, gw:2 * gw], w_in_v[:, :, DFF + j0 * 128:DFF + j1 * 128], f"W1{tag}g{g}", eng=weng)
            W1g.append(Wt)
            for j in range(j0, j1):
                jgrp[j] = (g, (j - j0) * 128, gw)
        W2 = self.sb(f"W2{tag}", NFC * D, BF16)
        W2v = W2.re("p (c f) -> p c f", c=NFC)
        for c in range(0, NFC, 11):
            self.dma(W2v[:, c:c + 11, :], w_out_v[:, c:c + 11, :], f"W2{tag}", eng=weng)
        if after_weights is not None:
            after_weights(after=[W2])
        for tt in range(4):
            front_pe(0, tt)
        for b in range(nblocks):
            for j in range(NFC):
                pg_, pu_ = B[(j % 2) * 2], B[(j % 2) * 2 + 1]
                g_, lo_, gw_ = jgrp[j]
                for c in range(8):
                    self.mm(pg_, W1g[g_][:, c, lo_:lo_ + 128], xTv[:, c, :], c == 0, c == 7)
                for c in range(8):
                    self.mm(pu_, W1g[g_][:, c, gw_ + lo_:gw_ + lo_ + 128], xTv[:, c, :], c == 0, c == 7)
                s = sg[j % 2]
                self.act(s, pg_, AF.Silu)
                self.tt(actTv[:, j, :], s, pu_, ALU.mult)
                if b + 1 < nblocks and j in (2, 7, 12, 17):
                    front_elem(b + 1, (j - 2) // 5)
            if b + 1 < nblocks:
                for tt in range(4):
                    front_pe(b + 1, tt)
            for tt in range(4):
                k = b * 4 + tt
                banks2 = (B[4 + (tt % 2) * 2], B[5 + (tt % 2) * 2])
                for hh in range(2):
                    for j in range(NFC):
                        self.mm(banks2[hh], actTv[:, j, tt * 128:(tt + 1) * 128], W2v[:, j, hh * 512:(hh + 1) * 512],
                                j == 0, j == NFC - 1)
                s2 = st2[k % 2]
                xr = xres[k % 2]
                self.dma(xr, src[k * 128:(k + 1) * 128, :], f"xres{k % 2}{tag}")
                o = ot[k % 2]
                self.post_norm_residual(banks2, s2, junk, pg, o, xr, half=True)
                self.dma(dst[k * 128:(k + 1) * 128, :], o, f"st{k % 4}{tag}")


    def walloc(self, name, ncols):
        return self.sb(name, 8 * ncols, BF16).re("p (c f) -> p c f", c=8)

    def wissue(self, Wv, srckey, col0, ncols, sem):
        if srckey in self.WB:
            sv = self.WB[srckey].re("(c p) f -> p c f", p=128)
            for c0 in range(0, 8, 4):
                self.dma(Wv[:, c0:c0 + 4, :], sv[:, c0:c0 + 4, col0:col0 + ncols], sem, eng="sp")
        else:
            sv = self.I[srckey].re("(c p) f -> p c f", p=128)
            for c0 in range(0, 8, 2):
                self.dma(Wv[:, c0:c0 + 2, :], sv[:, c0:c0 + 2, col0:col0 + ncols], sem, eng="pool")
        return Wv

    def wload(self, name, srckey, col0, ncols, sem):
        return self.wissue(self.walloc(name, ncols), srckey, col0, ncols, sem)

    def convert_weights(self, after=()):
        for key, rows, cols in (("w_in", D, N_IN), ("w_branch_a", D, D), ("w_branch_b", D, D), ("w_out", D, D),
                                ("ffn2_w_in", D, 2 * DFF), ("ffn2_w_out", DFF, D)):
            dst = self.dram_scr("wb_" + key, [rows, cols], BF16)
            dv = dst.re("(c p) f -> p c f", p=128)
            sv = self.I[key].re("(c p) f -> p c f", p=128)
            nchunk = rows // 128
            step = 2
            for c0 in range(0, nchunk, step):
                self.dma(dv[:, c0:c0 + step, :], sv[:, c0:c0 + step, :], "wb_" + key, eng="pool", after=after)
            self.WB[key] = dst

    def aff(self, buf, val, pattern, cm, cmp):
        self.memset(buf, val)
        a = buf.ap
        self.p.op("pool", lambda e: e.affine_select(out=a, in_=a, pattern=pattern, compare_op=cmp, fill=0.0,
                                                   base=0, channel_multiplier=cm), reads=[buf], writes=[buf])

    def pipeline(self, n_items, stages):
        ns = len(stages)
        for step in range(n_items + ns - 1):
            for si in range(ns - 1, -1, -1):
                i = step - si
                if 0 <= i < n_items:
                    stages[si](i)

    def gla_pass_a(self, H1, S, nseq=SEQ_PER_CORE, preB=None):
        I, B = self.I, self.banks
        Wg = self.wload("Wg", "w_in", 0, 3104, "Wg")
        if preB is not None:
            self.wissue(preB["Wa"], "w_branch_a", 0, D, "Wa")
            self.wissue(preB["Wga"], "w_in", OFF_GA, D, "Wga")
            self.wissue(preB["WgbB"], "w_in", OFF_GB, D, "WgbB")
        gT = self.sb("gTm", 8, F32)
        self.dma(gT, I["mix_pre_g"].re("(c p) -> p c", p=128), "smA", allow_slow_non_contiguous=True)
        up = {"f": self.sb("upf", 512, F32), "b": self.sb("upb", 512, F32)}
        for d in "fb":
            self.memset(up[d][0:64, :], 0.0)
            self.dma(up[d][0:16, :], I["gla_decay_up_" + d], "smA")
            self.dma(up[d][32:33, :], I["gla_decay_bias_" + d].re("(o f) -> o f", o=1), "smA")
        tri = {"f": self.sb("triF", 128, F32), "b": self.sb("triB", 128, F32)}
        self.aff(tri["f"], -1.0 / 16, [[1, 128]], -1, ALU.is_ge)
        self.aff(tri["b"], -1.0 / 16, [[-1, 128]], 1, ALU.is_ge)
        xin = [self.sb(f"xinA{i}", D, F32) for i in range(2)]
        xs = [self.sb(f"xsA{i}", D, BF16) for i in range(8)]
        st = [self.sb(f"stA{i}", 8, F32) for i in range(2)]
        uT2 = [self.sb(f"uTA{i}", 8 * 512, BF16).re("p (c t) -> p c t", c=8) for i in range(2)]
        qraw2 = [self.sb(f"qraw{i}", 4 * 512, F32).re("p (h t) -> p h t", h=4) for i in range(2)]
        kraw2 = [self.sb(f"kraw{i}", 4 * 512, F32).re("p (h t) -> p h t", h=4) for i in range(2)]
        sgr2 = [self.sb(f"sgrA{i}", 8 * 512, BF16).re("p (c t) -> p c t", c=8) for i in range(2)]
        z2 = [{"f": self.sb(f"zf{i}", 512, F32), "b": self.sb(f"zb{i}", 512, F32)} for i in range(2)]
        for zz in z2:
            for d in "fb":
                self.memset(zz[d][0:64, :], 0.0)
                self.memset(zz[d][32:33, :], 1.0)
        vtm2 = [[self.sb(f"vtm{j}_{i}", D, BF16) for i in range(4)] for j in range(2)]
        e1 = [self.sb(f"e1_{i}", 512, F32) for i in range(2)]
        sp = [self.sb(f"sp_{i}", 512, F32) for i in range(2)]
        eb = [self.sb(f"eb_{i}", 512, F32) for i in range(2)]
        enb = [self.sb(f"enb_{i}", 512, F32) for i in range(2)]
        eblb = [self.sb(f"eblb{i}", 4, F32) for i in range(4)]
        qt = [self.sb(f"qt{i}", 512, BF16) for i in range(4)]
        kt = [self.sb(f"kt{i}", 512, BF16) for i in range(4)]
        ktm = [self.sb(f"ktm{i}", 512, BF16) for i in range(4)]
        Sb = self.sb("SbA", 1024, F32)
        Sbb = [self.sb(f"SbbA{i}", 1024, BF16) for i in range(2)]
        UTv = S["UT"].re("(c p) t -> p c t", p=128)
        qscale = 128.0 ** -0.5
        ZB, BTB, TRB, KVB = (B[0], B[1]), (B[2], B[3]), (B[4], B[5]), (B[6], B[7])
        pairs = [(seq, (3, 2)) for seq in range(nseq)]
        pairs = [p_ for seq in range(nseq) for p_ in ((seq, (3, 2)), (seq, (1, 0)))]

        def front_elem(pi, t8):
            seq, blks = pairs[pi]
            blk, tt = blks[t8 // 4], t8 % 4
            r0 = seq * L + blk * 512
            k = pi * 8 + t8
            self.dma(xin[k % 2], H1[r0 + tt * 128:r0 + (tt + 1) * 128, :], f"xinA{k % 2}")
            self.norm_scale(xin[k % 2], xs[k % 8], st[k % 2], None)

        for t8 in range(8):
            front_elem(0, t8)
        for pi, (seq, blks) in enumerate(pairs):
            if blks[0] == 3:
                self.memset(Sb, 0.0)
            for bp, blk in enumerate(blks):
                r0 = seq * L + blk * 512
                uTv, qraw, kraw, z, vtm, sgr = uT2[bp], qraw2[bp], kraw2[bp], z2[bp], vtm2[bp], sgr2[bp]
                for tt in range(4):
                    k = pi * 8 + bp * 4 + tt
                    self.transpose_gain(xs[k % 8], B[6 + k % 2], gT, uTv[:, :, tt * 128:(tt + 1) * 128])
                self.dma(UTv[:, :, r0:r0 + 512], uTv, f"UTst{bp}", eng="pool")
                nb = 0
                for h in range(4):
                    for (off, dstb) in ((OFF_GQ, qraw), (OFF_GK, kraw)):
                        bank = B[nb % 2]
                        nb += 1
                        for c in range(8):
                            self.mm(bank, Wg[:, c, off + h * 128:off + (h + 1) * 128], uTv[:, c, :], c == 0, c == 7)
                        if off == OFF_GQ:
                            self.act(dstb[:, h, :], bank, AF.Copy)
                        else:
                            self.copy(dstb[:, h, :], bank)
                for d, off in (("f", OFF_ZF), ("b", OFF_ZB)):
                    bank = B[nb % 2]
                    nb += 1
                    for c in range(8):
                        self.mm(bank[0:16, :], Wg[:, c, off:off + 16], uTv[:, c, :], c == 0, c == 7)
                    self.copy(z[d][0:16, :], bank[0:16, :])
                for fc in range(8):
                    bank = B[nb % 2]
                    nb += 1
                    for c in range(8):
                        self.mm(bank, Wg[:, c, OFF_GR + fc * 128:OFF_GR + (fc + 1) * 128], uTv[:, c, :], c == 0, c == 7)
                    self.act(sgr[:, fc, :], bank, AF.Silu)
                for ck in range(4):
                    gc = seq * 16 + blk * 4 + ck
                    for hh in range(2):
                        bank = B[2 + hh]
                        for c in range(8):
                            self.mm(bank, uTv[:, c, ck * 128:(ck + 1) * 128],
                                    Wg[:, c, OFF_GV + hh * 512:OFF_GV + (hh + 1) * 512], c == 0, c == 7)
                        if hh == 0:
                            self.act(vtm[ck][:, 0:512], bank, AF.Copy)
                        else:
                            self.copy(vtm[ck][:, 512:1024], bank)
                    self.dma(S["V"][gc], vtm[ck], f"Vst{bp}{ck}", eng="pool")
                    self.dma(S["SGR"][gc].re("p (c t) -> p c t", c=8), sgr[:, :, ck * 128:(ck + 1) * 128], f"SGRst{bp}", eng="pool")
            items = [(bp, ck, d) for bp in range(2) for ck in range(3, -1, -1) for d in "bf"]

            def s0(i):
                bp, ck, d = items[i]
                self.mm(ZB[i % 2], z2[bp][d][0:33, ck * 128:(ck + 1) * 128], up[d][0:33, :], True, True)

            def s1(i):
                self.act(e1[i % 2], ZB[i % 2], AF.Exp, scale=-1.0)
                self.act(sp[i % 2], e1[i % 2], AF.Ln, bias=1.0)

            def s2(i):
                bp, ck, d = items[i]
                for h in range(4):
                    self.mm(BTB[i % 2][:, h * 128:(h + 1) * 128], sp[i % 2][:, h * 128:(h + 1) * 128], tri[d], True, True)

            def s3(i):
                bp, ck, d = items[i]
                gc = seq * 16 + blks[bp] * 4 + ck
                e_, en_ = eb[i % 2], enb[i % 2]
                self.act(e_, BTB[i % 2], AF.Exp)
                self.act(en_, BTB[i % 2], AF.Exp, scale=-1.0)
                q_, k_ = qt[i % 4], kt[i % 4]
                self.stt(q_.re("p (h t) -> p h t", h=4), qraw2[bp][:, :, ck * 128:(ck + 1) * 128], qscale,
                         e_.re("p (h t) -> p h t", h=4), ALU.mult, ALU.mult)
                self.tt(k_.re("p (h t) -> p h t", h=4), kraw2[bp][:, :, ck * 128:(ck + 1) * 128],
                        en_.re("p (h t) -> p h t", h=4), ALU.mult)
                ev = e_.re("p (h t) -> p h t", h=4)
                if d == "f":
                    self.copy(self.ebl[:, gc * 4:(gc + 1) * 4], ev[:, :, 127])
                else:
                    self.copy(eblb[i % 4], ev[:, :, 0])
                self.dma(S["Q" + d][gc], q_, f"Qst{i % 4}", eng="pool")
                self.dma(S["K" + d][gc], k_, f"Kst{i % 4}", eng="pool")

            def s4(i):
                trb = TRB[i % 2].bc(BF16)
                for h in range(4):
                    self.tr(trb[:, h * 128:(h + 1) * 128], kt[i % 4][:, h * 128:(h + 1) * 128], self.ident)

            def s5(i):
                bp, ck, d = items[i]
                gc = seq * 16 + blks[bp] * 4 + ck
                km_ = ktm[i % 4]
                self.copy(km_, TRB[i % 2].bc(BF16)[:, 0:512])
                if d == "f":
                    self.dma(S["KFT"][gc], km_, f"KFTst{i % 4}", eng="pool")
                else:
                    sbb = Sbb[(i // 2) % 2]
                    self.act(sbb, Sb, AF.Copy)
                    self.dma(S["SB"][gc], sbb, f"SBst{(i // 2) % 2}", eng="pool")
                    for h in range(4):
                        self.mm(KVB[h // 2][:, (h % 2) * 256:(h % 2 + 1) * 256], km_[:, h * 128:(h + 1) * 128],
                                vtm2[bp][ck][:, h * 256:(h + 1) * 256], True, True)
                    for h in range(4):
                        sc = eblb[i % 4][:, h:h + 1]
                        self.ts(Sb[:, h * 256:(h + 1) * 256], Sb[:, h * 256:(h + 1) * 256], sc, ALU.mult)
                        self.stt(Sb[:, h * 256:(h + 1) * 256], KVB[h // 2][:, (h % 2) * 256:(h % 2 + 1) * 256], sc,
                                 Sb[:, h * 256:(h + 1) * 256], ALU.mult, ALU.add)
                if pi + 1 < len(pairs) and i % 2 == 1:
                    front_elem(pi + 1, i // 2)

            self.pipeline(len(items), [s0, s1, s2, s3, s4, s5])

    def gla_pass_b(self, S, nseq=SEQ_PER_CORE, preB=None):
        I, B = self.I, self.banks
        if preB is not None:
            Wa, Wga, Wgb = preB["Wa"], preB["Wga"], preB["WgbB"]
        else:
            Wa = self.wload("Wa", "w_branch_a", 0, D, "Wa")
            Wga = self.wload("Wga", "w_in", OFF_GA, D, "Wga")
            Wgb = self.wload("WgbB", "w_in", OFF_GB, D, "WgbB")
        SGBb = self.sb("SGBb", 8 * 512, BF16).re("p (c t) -> p c t", c=8)
        SGBv = S["SGB"].re("(c p) t -> p c t", p=128)
        gog = self.sb("gog", 8, F32)
        self.dma(gog, I["gla_out_g"].re("h (e p) -> p (h e)", p=128), "smB", allow_slow_non_contiguous=True)
        maskF = self.sb("maskF", 128, F32)
        maskB = self.sb("maskB", 128, F32)
        self.aff(maskF, 1.0, [[1, 128]], -1, ALU.is_ge)
        self.aff(maskB, 1.0, [[-1, 128]], 1, ALU.is_gt)
        NLD = 3
        names = ("Qf", "Qb", "Kf", "Kb", "KFT")
        ld = {n: [self.sb(f"ld{n}{i}", 512, BF16) for i in range(NLD)] for n in names}
        for n in ("V", "SGR", "SB"):
            ld[n] = [self.sb(f"ld{n}{i}", 1024, BF16) for i in range(NLD)]
        sT = {d: [self.sb(f"sT{d}{i}", 512, BF16) for i in range(2)] for d in "fb"}
        Sf = self.sb("SfB", 1024, F32)
        Sfb = self.sb("SfbB", 1024, BF16)
        sq = [self.sb(f"sqB{i}", 1024, BF16) for i in range(2)]
        oTs = [self.sb(f"oTsB{i}", 1024, F32) for i in range(2)]
        lnv = self.sb("lnvB", 512, F32)
        rstd = self.sb("rstdB", 512, F32)
        t1 = [self.sb(f"t1B{i}", 1024, F32) for i in range(5)]
        mT = [self.sb(f"mTB{i}", 8 * 512, BF16).re("p (c t) -> p c t", c=8) for i in range(2)]
        uT = [self.sb(f"uTB{i}", 8 * 512, BF16).re("p (c t) -> p c t", c=8) for i in range(3)]
        sga = [self.sb(f"sga{i}", 512, F32) for i in range(2)]
        MAb = self.sb("MAb", 8 * 512, BF16).re("p (c t) -> p c t", c=8)
        UTv = S["UT"].re("(c p) t -> p c t", p=128)
        MAv = S["MA"].re("(c p) t -> p c t", p=128)
        eps_t = self.eps_t
        SC = (B[0], B[1])
        OB = (B[2], B[3])
        KVB = (B[4], B[5])
        NB = B[6]
        EP = (B[7], B[6])
        nch = nseq * 16

        def T(i):
            return {n: ld[n][i % NLD] for n in ld}

        def p0(i):
            t = T(i)
            if i % 4 == 0:
                blk_r0 = (i // 4) * 512
                self.dma(uT[(i // 4) % 3], UTv[:, :, blk_r0:blk_r0 + 512], f"uTB{(i // 4) % 3}")
            for n in ("Kf", "Qf", "Kb", "Qb", "V", "SB", "KFT", "SGR"):
                self.dma(t[n], S[n][i], f"ldB{n}{i % NLD}")

        def p1(i):
            t = T(i)
            for di, d in enumerate("fb"):
                for h in range(4):
                    self.mm(SC[di][:, h * 128:(h + 1) * 128], t["K" + d][:, h * 128:(h + 1) * 128],
                            t["Q" + d][:, h * 128:(h + 1) * 128], True, True)

        def p2(i):
            for di, d in enumerate("fb"):
                m = (maskF if d == "f" else maskB).re("p (o t) -> p o t", o=1).bcast([128, 4, 128])
                self.tt(sT[d][i % 2].re("p (h t) -> p h t", h=4), SC[di].re("p (h t) -> p h t", h=4), m, ALU.mult)
            self.tt(t1[i % 5].re("p (c t) -> p c t", c=8), T(i)["SGR"].re("p (c t) -> p c t", c=8),
                    gog.re("p (c o) -> p c o", o=1).bcast([128, 8, 128]), ALU.mult)

        def p3(i):
            t = T(i)
            if i % 16 == 0:
                self.memset(Sf, 0.0)
                self.memset(Sfb, 0.0)
            for fc in range(8):
                h = fc // 2
                ob = OB[fc // 4][:, (fc % 4) * 128:(fc % 4 + 1) * 128]
                vcol = t["V"][:, fc * 128:(fc + 1) * 128]
                self.mm(ob, vcol, sT["f"][i % 2][:, h * 128:(h + 1) * 128], True, False)
                self.mm(ob, vcol, sT["b"][i % 2][:, h * 128:(h + 1) * 128], False, False)
                self.mm(ob, Sfb[:, fc * 128:(fc + 1) * 128], t["Qf"][:, h * 128:(h + 1) * 128], False, False)
                self.mm(ob, t["SB"][:, fc * 128:(fc + 1) * 128], t["Qb"][:, h * 128:(h + 1) * 128], False, True)
            for h in range(4):
                self.mm(KVB[h // 2][:, (h % 2) * 256:(h % 2 + 1) * 256], t["KFT"][:, h * 128:(h + 1) * 128],
                        t["V"][:, h * 256:(h + 1) * 256], True, True)

        def p4(i):
            for hb in range(2):
                self.act(sq[i % 2][:, hb * 512:(hb + 1) * 512], OB[hb], AF.Square)
                self.act(oTs[i % 2][:, hb * 512:(hb + 1) * 512], OB[hb], AF.Copy)
            for h in range(4):
                sc = self.ebl[:, i * 4 + h:i * 4 + h + 1]
                self.ts(Sf[:, h * 256:(h + 1) * 256], Sf[:, h * 256:(h + 1) * 256], sc, ALU.mult)
                self.stt(Sf[:, h * 256:(h + 1) * 256], KVB[h // 2][:, (h % 2) * 256:(h % 2 + 1) * 256], sc,
                         Sf[:, h * 256:(h + 1) * 256], ALU.mult, ALU.add)
            self.act(Sfb, Sf, AF.Copy)

        def p5(i):
            for h in range(4):
                nbk = NB[:, h * 128:(h + 1) * 128]
                self.mm(nbk, self.ones_bf, sq[i % 2][:, (2 * h) * 128:(2 * h + 1) * 128], True, False)
                self.mm(nbk, self.ones_bf, sq[i % 2][:, (2 * h + 1) * 128:(2 * h + 2) * 128], False, True)

        def p6(i):
            ck = i % 4
            blk = i // 4
            self.act(lnv, NB, AF.Ln, scale=1.0 / 256, bias=eps_t)
            self.act(rstd, lnv, AF.Exp, scale=-0.5)
            tv = t1[i % 5].re("p (h e t) -> p h e t", h=4, e=2)
            self.tt(tv, tv, rstd.re("p (h o t) -> p h o t", h=4, o=1).bcast([128, 4, 2, 128]), ALU.mult)
            m_ = mT[blk % 2]
            self.tt(m_[:, :, ck * 128:(ck + 1) * 128], oTs[i % 2].re("p (c t) -> p c t", c=8),
                    t1[i % 5].re("p (c t) -> p c t", c=8), ALU.mult)
            if ck == 3:
                r0 = blk * 512
                u_ = uT[blk % 3]
                ne = 0
                for o in range(8):
                    bg = EP[ne % 2]
                    ne += 1
                    for c in range(8):
                        self.mm(bg, Wga[:, c, o * 128:(o + 1) * 128], u_[:, c, :], c == 0, c == 7)
                    self.act(sga[o % 2], bg, AF.Sigmoid)
                    by = EP[ne % 2]
                    ne += 1
                    for c in range(8):
                        self.mm(by, Wa[:, c, o * 128:(o + 1) * 128], m_[:, c, :], c == 0, c == 7)
                    self.tt(MAb[:, o, :], by, sga[o % 2], ALU.mult)
                self.dma(MAv[:, :, r0:r0 + 512], MAb, "MAst", eng="pool")
                for o in range(8):
                    bg = EP[ne % 2]
                    ne += 1
                    for c in range(8):
                        self.mm(bg, Wgb[:, c, o * 128:(o + 1) * 128], u_[:, c, :], c == 0, c == 7)
                    self.act(SGBb[:, o, :], bg, AF.Sigmoid)
                self.dma(SGBv[:, :, r0:r0 + 512], SGBb, "SGBst", eng="pool")

        self.pipeline(nch, [p0, p1, p2, p3, p4, p5, p6])

    def rope_chain(self, src, W, gbc, Ct, St, tile, dst, T):
        nh = W // 128
        ta, ss, lnv, rs, qn, tb = T
        self.act(ta[:, 0:W], src, AF.Square)
        self.reduce(ss[:, 0:nh], ta[:, 0:W].re("p (h d) -> p h d", h=nh))
        self.act(lnv[:, 0:nh], ss[:, 0:nh], AF.Ln, scale=1.0 / 128, bias=self.eps_t)
        self.act(rs[:, 0:nh], lnv[:, 0:nh], AF.Exp, scale=-0.5)
        qv = qn[:, 0:W].re("p (h d) -> p h d", h=nh)
        self.tt(qv, src.re("p (h d) -> p h d", h=nh),
                rs[:, 0:nh].re("p (h o) -> p h o", o=1).bcast([128, nh, 128]), ALU.mult)
        self.tt(qv, qv, gbc.re("p (o d) -> p o d", o=1).bcast([128, nh, 128]), ALU.mult)
        self.tt(ta[:, 0:W].re("p (h d) -> p h d", h=nh), qv,
                Ct[:, tile, :].re("p (o d) -> p o d", o=1).bcast([128, nh, 128]), ALU.mult)
        q5 = qn[:, 0:W].re("p (h a j e) -> p h a j e", h=nh, a=2, j=2)
        t5 = tb[:, 0:W].re("p (h a j e) -> p h a j e", h=nh, a=2, j=2)
        S5 = St[:, tile, :].re("p (o a j e) -> p o a j e", o=1, a=2, j=2)
        for j in range(2):
            self.tt(t5[:, :, :, j, :], q5[:, :, :, 1 - j, :], S5[:, :, :, j, :].bcast([128, nh, 2, 32]), ALU.mult)
        self.tt(dst, ta[:, 0:W], tb[:, 0:W], ALU.add)

    def attn_phase(self, H1, H2, S, nseq=SEQ_PER_CORE):
        I, B = self.I, self.banks
        Wq = self.wload("Wq", "w_in", OFF_AQ, D, "Wq")
        Wkv = self.wload("Wkv", "w_in", OFF_AK, 512, "Wkv")
        Wb = self.wload("Wb", "w_branch_b", 0, D, "Wb")
        Wo = self.wload("Wo", "w_out", 0, D, "Wo")
        pg = self.sb("pgM", D, F32)
        self.dma(pg, Buf(I["mix_post_g"].ap.partition_broadcast(128), I["mix_post_g"].t), "smC")
        Ct = self.sb("ropeC", 16 * 128, F32).re("p (n d) -> p n d", n=16)
        St = self.sb("ropeS", 16 * 128, F32).re("p (n d) -> p n d", n=16)
        rv = I["rope"].re("(n p) d -> p n d", p=128)
        self.dma(Ct, rv[:, :, 0:128], "smC")
        self.dma(St, rv[:, :, 128:256], "smC")
        gq = self.sb("g_q", 128, F32)
        gk = self.sb("g_k", 128, F32)
        self.dma(gq, Buf(I["att_q_norm_g"].ap.partition_broadcast(128), I["att_q_norm_g"].t), "smC")
        self.dma(gk, Buf(I["att_k_norm_g"].ap.partition_broadcast(128), I["att_k_norm_g"].t), "smC")
        KT = self.sb("KT", 2 * L, BF16).re("p (h t) -> p h t", h=2)
        Vt = self.sb("Vt", 16 * 256, BF16).re("p (n d) -> p n d", n=16)
        uT = [self.sb(f"uTC{i}", 8 * 512, BF16).re("p (c t) -> p c t", c=8) for i in range(2)]
        Ts = []
        for i in range(3):
            tb_ = self.sb(f"tbC{i}", 512, F32)
            Ts.append((self.sb(f"taC{i}", 512, F32), self.sb(f"ssC{i}", 4, F32), self.sb(f"lnvC{i}", 4, F32),
                       self.sb(f"rsC{i}", 4, F32), self.sb(f"qnC{i}", 512, F32), tb_, tb_))
        qr = [self.sb(f"qrC{i}", D, BF16) for i in range(2)]
        kr = [self.sb(f"krC{i}", 256, BF16) for i in range(3)]
        qT = [self.sb(f"qTC{i}", 8 * 512, BF16).re("p (h t) -> p h t", h=8) for i in range(2)]
        obT = [self.sb(f"obT{i}", 8 * 512, BF16).re("p (h t) -> p h t", h=8) for i in range(2)]
        PT = [self.sb(f"PT{i}", 512, BF16) for i in range(4)]
        rden = self.sb("rdenC", 512, F32)
        lden = self.sb("ldenC", 512, F32)
        if DEN_QUADS:
            ots = [self.sb(f"otsC{i}", 512, F32) for i in range(2)]
            PA = [self.sb(f"PAC{i}", 512, BF16) for i in range(2)]
            PQ = [self.sb(f"PQC{i}", 512, BF16) for i in range(3)]
        else:
            ots = [self.sb(f"otsC{i}", 512, F32) for i in range(2)]
        t2 = self.sb("t2C0", 512, F32)
        MAl = self.sb("MAl", 8 * 512, BF16).re("p (c t) -> p c t", c=8)
        SGl = self.sb("SGl", 8 * 512, BF16).re("p (c t) -> p c t", c=8)
        junk = self.sb("junkC", 512, BF16)
        st2 = [self.sb(f"st2C{i}", 8, F32) for i in range(2)]
        xres = self.sb("xresC0", D, F32)
        ot = self.sb("otC0", D, F32)
        fs = ot
        UTv = S["UT"].re("(c p) t -> p c t", p=128)
        MAv = S["MA"].re("(c p) t -> p c t", p=128)
        SGv = S["SGB"].re("(c p) t -> p c t", p=128)
        sm_scale = 128.0 ** -0.5
        SIDE = [B[0], B[1], B[2]]
        OT, DEN = B[3], B[4]
        STB = [B[5], B[6], B[7]]
        cnt = {"side": 0, "chain": 0, "u": 0}

        def side_bank():
            cnt["side"] += 1
            return SIDE[cnt["side"] % 3]

        def chain_T():
            cnt["chain"] += 1
            return Ts[cnt["chain"] % 3][:6]

        def rope_stages(src_fn, W, gbc, tile_fn, dst_fn):
            nh = W // 128
            stt_ = {}

            def s1():
                T_ = Ts[cnt["chain"] % 3]
                cnt["chain"] += 1
                stt_["T"] = T_
                ta, ss, lnv, rs, qn, tb, qs = T_
                src = src_fn()
                self.act(ta[:, 0:W], src, AF.Square)
                self.act(qs[:, 0:W], src, AF.Copy)

            def s2():
                ta, ss, lnv, rs, qn, tb, qs = stt_["T"]
                self.reduce(ss[:, 0:nh], ta[:, 0:W].re("p (h d) -> p h d", h=nh))

            def s3():
                ta, ss, lnv, rs, qn, tb, qs = stt_["T"]
                self.act(lnv[:, 0:nh], ss[:, 0:nh], AF.Ln, scale=1.0 / 128, bias=self.eps_t)
                self.act(rs[:, 0:nh], lnv[:, 0:nh], AF.Exp, scale=-0.5)

            def s4():
                ta, ss, lnv, rs, qn, tb, qs = stt_["T"]
                tile = tile_fn()
                qv = qn[:, 0:W].re("p (h d) -> p h d", h=nh)
                self.tt(qv, qs[:, 0:W].re("p (h d) -> p h d", h=nh),
                        rs[:, 0:nh].re("p (h o) -> p h o", o=1).bcast([128, nh, 128]), ALU.mult)
                self.tt(qv, qv, gbc.re("p (o d) -> p o d", o=1).bcast([128, nh, 128]), ALU.mult)
                self.tt(ta[:, 0:W].re("p (h d) -> p h d", h=nh), qv,
                        Ct[:, tile, :].re("p (o d) -> p o d", o=1).bcast([128, nh, 128]), ALU.mult)

            def s5():
                ta, ss, lnv, rs, qn, tb, qs = stt_["T"]
                tile = tile_fn()
                q5 = qn[:, 0:W].re("p (h a j e) -> p h a j e", h=nh, a=2, j=2)
                t5 = tb[:, 0:W].re("p (h a j e) -> p h a j e", h=nh, a=2, j=2)
                S5 = St[:, tile, :].re("p (o a j e) -> p o a j e", o=1, a=2, j=2)
                for j in range(2):
                    self.tt(t5[:, :, :, j, :], q5[:, :, :, 1 - j, :], S5[:, :, :, j, :].bcast([128, nh, 2, 32]), ALU.mult)
                self.tt(dst_fn(), ta[:, 0:W], tb[:, 0:W], ALU.add)

            return [s1, s2, s3, s4, s5]

        def q_items(seq, blk, par):
            items = []
            st = {}

            def qp(tt, hh):
                def s0():
                    u = uT[1]
                    if tt == 0 and hh == 0:
                        self.dma(u, UTv[:, :, seq * L + blk * 512:seq * L + (blk + 1) * 512], "uTC1")
                    bank = side_bank()
                    st[("b", tt, hh)] = bank
                    for c in range(8):
                        self.mm(bank, u[:, c, tt * 128:(tt + 1) * 128], Wq[:, c, hh * 512:(hh + 1) * 512], c == 0, c == 7)
                return [s0] + rope_stages(lambda: st[("b", tt, hh)], 512, gq, lambda: blk * 4 + tt,
                                          lambda: qr[tt % 2][:, hh * 512:(hh + 1) * 512])

            def qt(tt):
                def s0():
                    trb = side_bank().bc(BF16)
                    st[("t", tt)] = trb
                    for h in range(8):
                        self.tr(trb[:, h * 128:(h + 1) * 128], qr[tt % 2][:, h * 128:(h + 1) * 128], self.ident)

                def s1():
                    self.copy(qT[par][:, :, tt * 128:(tt + 1) * 128], st[("t", tt)].re("p (h t) -> p h t", h=8))
                return [s0, s1]

            for tt in range(4):
                if tt >= 1:
                    items.append((qt(tt - 1), [8]))
                items.append((qp(tt, 0), []))
                items.append((qp(tt, 1), []))
            items.append((qt(3), [8]))
            return items

        def epi_items(seq, blk, par):
            items = []
            r0 = seq * L + blk * 512
            st = {}

            def yb(o):
                def s0():
                    if o == 0:
                        self.dma(MAl, MAv[:, :, r0:r0 + 512], "MAld")
                        self.dma(SGl, SGv[:, :, r0:r0 + 512], "SGld")
                    by = side_bank()
                    st[("y", o)] = by
                    for h in range(8):
                        self.mm(by, Wb[:, h, o * 128:(o + 1) * 128], obT[par][:, h, :], h == 0, h == 7)

                def s1():
                    self.tt(t2, st[("y", o)], SGl[:, o, :], ALU.mult)
                    self.tt(MAl[:, o, :], t2, MAl[:, o, :], ALU.add)
                return [s0, s1]

            def op(tt):
                row = r0 + tt * 128
                s2_ = st2[tt % 2]

                def s0():
                    banks2 = (side_bank(), side_bank())
                    st[("o", tt)] = banks2
                    for hh in range(2):
                        for o in range(8):
                            self.mm(banks2[hh], MAl[:, o, tt * 128:(tt + 1) * 128], Wo[:, o, hh * 512:(hh + 1) * 512], o == 0, o == 7)
                    self.dma(xres, H1[row:row + 128, :], "xresC0")

                def s1():
                    banks2 = st[("o", tt)]
                    for hh in range(2):
                        self.act(junk[:, 0:512], banks2[hh], AF.Square, accum=s2_[:, hh:hh + 1])
                        self.act(fs[:, hh * 512:(hh + 1) * 512], banks2[hh], AF.Copy)

                def s2():
                    self.tt(s2_[:, 2:3], s2_[:, 0:1], s2_[:, 1:2], ALU.add)

                def s3():
                    self.act(s2_[:, 3:4], s2_[:, 2:3], AF.Ln, scale=1.0 / D, bias=self.eps_t)
                    self.act(s2_[:, 4:5], s2_[:, 3:4], AF.Exp, scale=-0.5)

                def s4():
                    for hh in range(2):
                        self.stt(ot[:, hh * 512:(hh + 1) * 512], fs[:, hh * 512:(hh + 1) * 512], s2_[:, 4:5],
                                 pg[:, hh * 512:(hh + 1) * 512], ALU.mult, ALU.mult)
                    self.tt(ot, ot, xres, ALU.add)
                    self.dma(H2[row:row + 128, :], ot, "stC0", eng="pool")
                return [s0, s1, s2, s3, s4]

            for o in range(8):
                items.append((yb(o), []))
            for tt in range(4):
                items.append((op(tt), [6]))
            return items

        def kv_items(seq):
            items = []
            st = {}

            def kvp(t):
                blk, tt = divmod(t, 4)

                def s0():
                    u = uT[0]
                    if tt == 0:
                        self.dma(u, UTv[:, :, seq * L + blk * 512:seq * L + (blk + 1) * 512], "uTC0")
                    bank = side_bank()
                    st[("b", t)] = bank
                    for c in range(8):
                        self.mm(bank, u[:, c, tt * 128:(tt + 1) * 128], Wkv[:, c, :], c == 0, c == 7)

                def sv():
                    self.act(Vt[:, t, :], st[("b", t)][:, 256:512], AF.Copy)

                rs_ = rope_stages(lambda: st[("b", t)][:, 0:256], 256, gk, lambda: t, lambda: kr[t % 3])

                def s1():
                    sv()
                    rs_[0]()

                def s6():
                    trb = side_bank().bc(BF16)
                    st[("t", t)] = trb
                    for hk in range(2):
                        self.tr(trb[:, hk * 128:(hk + 1) * 128], kr[t % 3][:, hk * 128:(hk + 1) * 128], self.ident)

                def s7():
                    self.copy(KT[:, :, t * 128:(t + 1) * 128], st[("t", t)][:, 0:256].re("p (h t) -> p h t", h=2))
                return [s0, s1] + rs_[1:] + [s6, s7]

            for t in range(16):
                items.append((kvp(t), []))
            return items

        GAP = 2

        def plan_side(items, nslots, spacing):
            plan = {}
            start_prev, ends = -spacing, []
            for stages, deps in items:
                start = start_prev + spacing
                if deps and ends:
                    start = max(start, max(ends) + deps[0])
                for k, f in enumerate(stages):
                    plan.setdefault(start + GAP * k, []).append(f)
                ends.append(start + GAP * (len(stages) - 1))
                start_prev = start
            return plan

        def run_streams_standalone(streams):
            plan = {}
            for lst in streams:
                for sl, fs_ in plan_side(lst, 0, 4).items():
                    plan.setdefault(sl, []).extend(fs_)
            for sl in sorted(plan):
                for f in plan[sl]:
                    f()

        def head_loop(par, side):
            q_ = qT[par]
            ob = obT[par]
            its = [(h, kt) for h in range(8) for kt in range(16)]
            n = len(its)
            plan = {}
            for lst in side:
                for sl, fs_ in plan_side(lst, n, 4).items():
                    plan.setdefault(sl, []).extend(fs_)

            def issue_st(i):
                h, kt = its[i]
                self.mm(STB[i % 3], KT[:, h // 4, kt * 128:(kt + 1) * 128], q_[:, h, :], True, True)

            issue_st(0)
            issue_st(1)
            pend = []
            DLAG = 3

            def run_due(i):
                while pend and pend[0][0] <= i:
                    pend.pop(0)[1]()

            for i in range(n):
                h, kt = its[i]
                hk = h // 4
                pt = PT[i % 4]
                self.act(pt, STB[i % 3], AF.Exp, scale=sm_scale)
                if i + 2 < n:
                    issue_st(i + 2)
                self.mm(OT, Vt[:, kt, hk * 128:(hk + 1) * 128], pt, kt == 0, kt == 15)
                if not DEN_QUADS:
                    self.mm(DEN, self.ones_bf, pt, kt == 0, kt == 15)
                run_due(i)
                if DEN_QUADS:
                    if kt % 4 == 1:
                        self.tt(PA[0], PT[(i - 1) % 4], pt, ALU.add)
                    elif kt % 4 == 3:
                        self.tt(PA[1], PT[(i - 1) % 4], pt, ALU.add)
                        pq = PQ[(i // 4) % 3]
                        self.tt(pq, PA[0], PA[1], ALU.add)

                        def den_mm(g_=kt // 4, pq_=pq):
                            self.mm(DEN, self.ones_bf, pq_, g_ == 0, g_ == 3)
                        pend.append((i + DLAG, den_mm))
                if kt == 15:
                    ots_ = ots[h % 2]
                    self.copy(ots_, OT)
                    self.act(lden, DEN, AF.Ln)

                    def fin(h_=h, ots__=ots_):
                        self.act(rden, lden, AF.Exp, scale=-1.0)
                        self.tt(ob[:, h_, :], ots__, rden, ALU.mult)
                    pend.append((i + (DLAG if DEN_QUADS else 1), fin))
                for f in plan.get(i, ()):
                    f()
            run_due(n + DLAG)
            for sl in sorted(k for k in plan if k >= n):
                for f in plan[sl]:
                    f()

        for seq in range(nseq):
            kv_, q_ = kv_items(seq), q_items(seq, 0, 0)
            merged = []
            while kv_ or q_:
                if kv_:
                    merged.append(kv_.pop(0))
                if q_:
                    merged.append(q_.pop(0))
            run_streams_standalone([merged])
            for blk in range(4):
                par = blk % 2
                side = []
                if blk > 0:
                    side.append(epi_items(seq, blk - 1, 1 - par))
                if blk < 3:
                    side.append(q_items(seq, blk + 1, 1 - par))
                head_loop(par, side)
            run_streams_standalone([epi_items(seq, 3, 1)])

    def build(self, phases=("ffn1", "glaA", "glaB", "attn", "ffn2"), nblocks=None, nseq=SEQ_PER_CORE):
        self.nseq = nseq
        nc = self.nc
        I = {}
        I["x"] = self.dram_in("x", [NTOK, D])
        for n, shp in (("ffn1_pre_g", [D]), ("ffn1_w_in", [D, 2 * DFF]), ("ffn1_w_out", [DFF, D]),
                       ("ffn1_post_g", [D]), ("mix_pre_g", [D]), ("w_in", [D, N_IN]),
                       ("gla_decay_up_f", [16, 512]), ("gla_decay_bias_f", [512]),
                       ("gla_decay_up_b", [16, 512]), ("gla_decay_bias_b", [512]),
                       ("gla_out_g", [4, 256]), ("w_branch_a", [D, D]), ("att_q_norm_g", [128]),
                       ("att_k_norm_g", [128]), ("w_branch_b", [D, D]), ("w_out", [D, D]),
                       ("mix_post_g", [D]), ("ffn2_pre_g", [D]), ("ffn2_w_in", [D, 2 * DFF]),
                       ("ffn2_w_out", [DFF, D]), ("ffn2_post_g", [D])):
            I[n] = self.dram_in(n, shp)
        I["rope"] = self.dram_in("rope", [L, 256])
        out = self.dram_out("out", [NTOK, D])
        H1 = self.dram_scr("H1", [NTOK, D], F32)
        H2 = self.dram_scr("H2", [NTOK, D], F32)
        self.I = I
        self.setup()
        self.eps_t = self.sb("eps_t", 1, F32)
        self.memset(self.eps_t, EPS)
        self.eps4_t = self.sb("eps4_t", 1, F32)
        self.memset(self.eps4_t, 4 * EPS)
        self.arena_floor = self.arena_off
        nb = nblocks or NTOK // 512
        self.ebl = self.sb("ebl", 128, F32)
        self.arena_floor = self.arena_off
        S = {}
        for n in ("Qf", "Qb", "Kf", "Kb", "KFT"):
            S[n] = self.dram_scr("scr_" + n, [32, 128, 512], BF16)
        for n in ("V", "SGR", "SB"):
            S[n] = self.dram_scr("scr_" + n, [32, 128, 1024], BF16)
        S["UT"] = self.dram_scr("scr_UT", [D, NTOK], BF16)
        S["MA"] = self.dram_scr("scr_MA", [D, NTOK], BF16)
        S["SGB"] = self.dram_scr("scr_SGB", [D, NTOK], BF16)
        self.S = S
        self.WB = {}
        last = phases[-1]
        if "ffn1" in phases:
            self.ffn_phase(I["x"], out if last == "ffn1" else H1, I["ffn1_pre_g"], ("ffn1_w_in", "ffn1_w_out"),
                           I["ffn1_post_g"], "a", nblocks=nb,
                           after_weights=self.convert_weights if len(phases) > 1 else None)
            self.phase_reset()
        src_mix = H1 if "ffn1" in phases else I["x"]
        nseq = self.nseq
        if "glaA" in phases:
            self.gla_pass_a(src_mix, S, nseq)
            self.phase_reset()
        if "glaB" in phases:
            self.gla_pass_b(S, nseq)
            self.phase_reset()
        if "attn" in phases:
            self.attn_phase(src_mix, out if last == "attn" else H2, S, nseq)
            self.phase_reset()
        if "ffn2" in phases:
            self.ffn_phase(H2, out, I["ffn2_pre_g"], ("ffn2_w_in", "ffn2_w_out"), I["ffn2_post_g"], "b", nblocks=nb)
            self.phase_reset()
        self.p.final_wait()
        self.p.emit(nc, self.es)
        self.es.close()
        return nc


_ROPE = None


def rope_tables():
    global _ROPE
    if _ROPE is None:
        half = 64
        inv = 10000.0 ** (-np.arange(0, half, 2, dtype=np.float32) / half)
        t = np.arange(L)
        row = (t // 64).astype(np.float32)[:, None] * inv[None, :]
        col = (t % 64).astype(np.float32)[:, None] * inv[None, :]
        cr, sr, cc, sc = np.cos(row), np.sin(row), np.cos(col), np.sin(col)
        C = np.concatenate([cr, cr, cc, cc], axis=1)
        S = np.concatenate([-sr, sr, -sc, sc], axis=1)
        _ROPE = np.ascontiguousarray(np.concatenate([C, S], axis=1).astype(np.float32))
    return _ROPE


def make_in_maps(inputs):
    x = np.ascontiguousarray(np.asarray(inputs["x"], dtype=np.float32))
    maps = []
    shared = {}
    for k, v in inputs.items():
        if k == "x":
            continue
        a = np.asarray(v, dtype=np.float32)
        shared[k] = np.ascontiguousarray(a.reshape(a.shape[1:]))
    shared["rope"] = rope_tables()
    for c in range(NCORES):
        m = dict(shared)
        m["x"] = x[c * SEQ_PER_CORE:(c + 1) * SEQ_PER_CORE].reshape(NTOK, D)
        maps.append(m)
    return maps


_NC = None


def kernel(**inputs):
    global _NC
    if _NC is None:
        _NC = Builder().build()
    maps = make_in_maps(inputs)
    res = run_bass_kernel_spmd(_NC, maps, core_ids=list(range(NCORES)))
    outs = [np.asarray(r["out"]).reshape(SEQ_PER_CORE, L, D) for r in res.results]
    return np.concatenate(outs, axis=0).astype(np.float32)
```

```python
from contextlib import ExitStack

import numpy as np
import concourse.bass as bass
import concourse.mybir as mybir
from concourse.bass_utils import run_bass_kernel_spmd

F32 = mybir.dt.float32
BF16 = mybir.dt.bfloat16
AF = mybir.ActivationFunctionType
ALU = mybir.AluOpType
AX = mybir.AxisListType

NCORES = 8
DEN_QUADS = False
D = 1024
L = 2048
SEQ_PER_CORE = 2
NTOK = L * SEQ_PER_CORE
DFF = 2816
NFC = DFF // 128
EPS = 1e-6
N_IN = 6688
OFF_GQ, OFF_GK, OFF_GV, OFF_GR, OFF_ZF, OFF_ZB = 0, 512, 1024, 2048, 3072, 3088
OFF_AQ, OFF_AK, OFF_AV, OFF_GA, OFF_GB = 3104, 4128, 4384, 4640, 5664

ENGS = ("pe", "act", "dve", "pool", "sp")


class Tile:
    __slots__ = ("name", "w", "r", "rdma")

    def __init__(self, name):
        self.name = name
        self.w = None
        self.r = {}
        self.rdma = {}


class Buf:
    def __init__(self, ap, t):
        self.ap = ap
        self.t = t

    def __getitem__(self, k):
        return Buf(self.ap[k], self.t)

    def bc(self, dt):
        return Buf(self.ap.bitcast(dt), self.t)

    def re(self, pat, **kw):
        return Buf(self.ap.rearrange(pat, **kw), self.t)

    def bcast(self, shape):
        return Buf(self.ap.to_broadcast(list(shape)), self.t)


class OpRec:
    __slots__ = ("eng", "idx", "sig", "waits", "fn", "dma_inc", "rank")

    def __init__(self, eng, idx, fn):
        self.eng = eng
        self.idx = idx
        self.sig = False
        self.waits = []
        self.fn = fn
        self.dma_inc = None
        self.rank = 0


class Prog:
    def __init__(self):
        self.q = {e: [] for e in ENGS}
        self.lastc = {e: None for e in ENGS}
        self.seen = {e: {} for e in ENGS}
        self.dma_cnt = {}
        self.dma_owner = {}
        self.nbar = 0
        self.semmap = {}
        self.free_phys = []
        self.nphys = 0

    def _need(self, eng, need, marker, same_eng_ok):
        if marker is None:
            return
        if marker[0] == "op":
            op = marker[1]
            if op.eng == eng and (same_eng_ok or eng == "pe"):
                return
            s, v = "e_" + op.eng, op.idx
            cur = need.get(s)
            if cur is None or cur[0] < v:
                need[s] = (v, op)
        else:
            s, v = marker[1], marker[2]
            cur = need.get(s)
            if cur is None or cur[0] < v:
                need[s] = (v, None)

    def _commit(self, eng, need):
        out = []
        seen = self.seen[eng]
        for s, (v, op) in need.items():
            if seen.get(s, -1) < v:
                seen[s] = v
                if op is not None:
                    op.sig = True
                    out.append((s, op))
                else:
                    out.append((s, v))
        return out

    def _deps(self, eng, reads, writes):
        need = {}
        for t in reads:
            self._need(eng, need, t.w, False)
        for t in writes:
            self._need(eng, need, t.w, True)
            for op in t.r.values():
                self._need(eng, need, ("op", op), True)
            for s, v in t.rdma.items():
                self._need(eng, need, ("dma", s, v), True)
        return self._commit(eng, need)

    def op(self, eng, fn, reads=(), writes=()):
        rt = [b.t for b in reads]
        wt = [b.t for b in writes]
        waits = self._deps(eng, rt, wt)
        rec = OpRec(eng, len(self.q[eng]), fn)
        rec.waits = waits
        self.q[eng].append(rec)
        self.lastc[eng] = rec
        for t in rt:
            t.r[eng] = rec
        for t in wt:
            t.w = ("op", rec)
            t.r = {}
            t.rdma = {}
        return rec

    def dma(self, eng, out, in_, sem, after=(), **kw):
        if sem not in self.semmap:
            if self.free_phys:
                self.semmap[sem] = self.free_phys.pop(0)
            else:
                self.semmap[sem] = f"d{self.nphys}"
                self.nphys += 1
        sem = self.semmap[sem]
        rt, wt = [in_.t], [out.t]
        need = {}
        self._need(eng, need, in_.t.w, False)
        for b_ in after:
            self._need(eng, need, b_.t.w, False)
        self._need(eng, need, out.t.w, True)
        for op in out.t.r.values():
            self._need(eng, need, ("op", op), True)
        for s_, v_ in out.t.rdma.items():
            self._need(eng, need, ("dma", s_, v_), True)
        cur = self.dma_cnt.get(sem, 0)
        if self.dma_owner.get(sem) is not out.t and cur > 0:
            self._need(eng, need, ("dma", sem, cur), True)
        waits = self._commit(eng, need)
        self.dma_owner[sem] = out.t
        o_ap, i_ap = out.ap, in_.ap
        rec = OpRec(eng, len(self.q[eng]), lambda e: e.dma_start(out=o_ap, in_=i_ap, **kw))
        rec.waits = waits
        self.dma_cnt[sem] = cur + 16
        rec.dma_inc = sem
        self.q[eng].append(rec)
        in_.t.rdma[sem] = cur + 16
        out.t.w = ("dma", sem, cur + 16)
        out.t.r = {}
        out.t.rdma = {}
        return rec

    def _all_done_waits(self, eng):
        need = {}
        for e in ("pe", "act", "dve", "pool"):
            if self.lastc[e] is not None:
                self._need(eng, need, ("op", self.lastc[e]), False)
        for s, v in self.dma_cnt.items():
            self._need(eng, need, ("dma", s, v), False)
        return self._commit(eng, need)

    def barrier(self):
        waits = self._all_done_waits("sp")
        self.free_phys = sorted(set(self.free_phys) | set(self.semmap.values()), key=lambda n: int(n[1:]))
        self.semmap = {}
        self.nbar += 1
        nb = self.nbar
        rec = OpRec("sp", len(self.q["sp"]), ("bar", nb))
        rec.waits = waits
        self.q["sp"].append(rec)
        for e in ("pe", "act", "dve", "pool"):
            r2 = OpRec(e, len(self.q[e]), None)
            r2.waits = [("bar", nb)]
            self.q[e].append(r2)
            for s, v in self.seen["sp"].items():
                if self.seen[e].get(s, -1) < v:
                    self.seen[e][s] = v

    def final_wait(self):
        rec = OpRec("sp", len(self.q["sp"]), None)
        rec.waits = self._all_done_waits("sp")
        self.q["sp"].append(rec)

    def emit(self, nc, es):
        semnames = set(["bar"])
        for e in ENGS:
            semnames.add("e_" + e)
            k = 0
            for rec in self.q[e]:
                if rec.sig:
                    k += 1
                    rec.rank = k
                for s, _ in rec.waits:
                    semnames.add(s)
                if rec.dma_inc:
                    semnames.add(rec.dma_inc)
        sems = {s: es.enter_context(nc.semaphore(s)) for s in sorted(semnames)}
        block = es.enter_context(nc.Block())
        engmap = {"pe": block.tensor, "act": block.scalar, "dve": block.vector,
                  "pool": block.gpsimd, "sp": block.sync}

        def mk(ename):
            def body(eng):
                for rec in self.q[ename]:
                    for s, v in rec.waits:
                        eng.wait_ge(sems[s], v.rank if isinstance(v, OpRec) else v)
                    if rec.fn is None:
                        continue
                    if isinstance(rec.fn, tuple):
                        eng.sem_inc(sems["bar"], 1)
                        continue
                    ins = rec.fn(eng)
                    if rec.dma_inc:
                        ins.then_inc(sems[rec.dma_inc], 16)
                    elif rec.sig:
                        ins.then_inc(sems["e_" + ename], 1)
            return body

        for ename in ENGS:
            engmap[ename](mk(ename))


class Builder:
    def __init__(self, debug=()):
        self.debug = set(debug)
        self.nc = bass.Bass("TRN2", target_bir_lowering=False)
        self.p = Prog()
        self.es = ExitStack()
        self.arena_off = 0
        self.arena_floor = 0

    def dram_in(self, name, shape, dt=F32):
        h = self.nc.dram_tensor(name, list(shape), dt, kind="ExternalInput")
        return Buf(h.ap(), Tile(name))

    def dram_out(self, name, shape, dt=F32):
        h = self.nc.dram_tensor(name, list(shape), dt, kind="ExternalOutput")
        return Buf(h.ap(), Tile(name))

    def dram_scr(self, name, shape, dt):
        kind = "ExternalOutput" if name in self.debug else "Internal"
        h = self.nc.dram_tensor(name, list(shape), dt, kind=kind)
        return Buf(h.ap(), Tile(name))

    def sb(self, name, cols, dt=F32):
        nby = cols * (4 if dt == F32 else 2)
        n4 = (nby + 31) // 32 * 8
        off = self.arena_off
        assert off + n4 <= self.arena_cols, f"SBUF arena overflow at {name}: {(off + n4) * 4} B"
        self.arena_off += n4
        ap = self.arena[:, off:off + n4]
        if dt != F32:
            ap = ap.bitcast(dt)[:, 0:cols]
        else:
            ap = ap[:, 0:cols]
        return Buf(ap, Tile(name))

    def phase_reset(self):
        self.p.barrier()
        self.arena_off = self.arena_floor

    def mm(self, out, lhsT, rhs, start, stop, extra_reads=()):
        o, a, b = out.ap, lhsT.ap, rhs.ap
        self.p.op("pe", lambda e: e.matmul(o, lhsT=a, rhs=b, start=start, stop=stop),
                  reads=[lhsT, rhs, *extra_reads], writes=[out])

    def tr(self, out, in_, ident):
        o, a, i = out.ap, in_.ap, ident.ap
        self.p.op("pe", lambda e: e.transpose(o, a, i), reads=[in_, ident], writes=[out])

    def act(self, out, in_, func, scale=1.0, bias=0.0, accum=None, eng="act"):
        o, a = out.ap, in_.ap
        reads = [in_]
        writes = [out]
        kw = {}
        if isinstance(scale, Buf):
            reads.append(scale)
            kw["scale"] = scale.ap
        else:
            kw["scale"] = float(scale)
        if isinstance(bias, Buf):
            reads.append(bias)
            kw["bias"] = bias.ap
        elif bias != 0.0:
            kw["bias"] = float(bias)
        if accum is not None:
            writes.append(accum)
            kw["accum_out"] = accum.ap
        self.p.op(eng, lambda e: e.activation(out=o, in_=a, func=func, **kw), reads=reads, writes=writes)

    def tt(self, out, in0, in1, op, eng="dve"):
        o, a, b = out.ap, in0.ap, in1.ap
        self.p.op(eng, lambda e: e.tensor_tensor(out=o, in0=a, in1=b, op=op), reads=[in0, in1], writes=[out])

    def ts(self, out, in0, s1, op0, s2=None, op1=None, eng="dve"):
        o, a = out.ap, in0.ap
        reads = [in0]
        if isinstance(s1, Buf):
            reads.append(s1)
            s1v = s1.ap
        else:
            s1v = float(s1)
        if isinstance(s2, Buf):
            reads.append(s2)
            s2v = s2.ap
        elif s2 is None:
            s2v = None
        else:
            s2v = float(s2)
        if op1 is None:
            self.p.op(eng, lambda e: e.tensor_scalar(out=o, in0=a, scalar1=s1v, scalar2=None, op0=op0),
                      reads=reads, writes=[out])
        else:
            self.p.op(eng, lambda e: e.tensor_scalar(out=o, in0=a, scalar1=s1v, scalar2=s2v, op0=op0, op1=op1),
                      reads=reads, writes=[out])

    def stt(self, out, in0, scalar, in1, op0, op1, eng="dve"):
        o, a, b = out.ap, in0.ap, in1.ap
        reads = [in0, in1]
        if isinstance(scalar, Buf):
            reads.append(scalar)
            sv = scalar.ap
        else:
            sv = float(scalar)
        self.p.op(eng, lambda e: e.scalar_tensor_tensor(out=o, in0=a, scalar=sv, in1=b, op0=op0, op1=op1),
                  reads=reads, writes=[out])

    def copy(self, out, in_, eng="dve"):
        o, a = out.ap, in_.ap
        self.p.op(eng, lambda e: e.tensor_copy(out=o, in_=a), reads=[in_], writes=[out])

    def recip(self, out, in_):
        o, a = out.ap, in_.ap
        self.p.op("dve", lambda e: e.reciprocal(out=o, in_=a), reads=[in_], writes=[out])

    def reduce(self, out, in_, op=ALU.add, eng="dve"):
        o, a = out.ap, in_.ap
        self.p.op(eng, lambda e: e.tensor_reduce(out=o, in_=a, axis=AX.X, op=op), reads=[in_], writes=[out])

    def memset(self, out, val, eng="pool"):
        o = out.ap
        self.p.op(eng, lambda e: e.memset(o, val), reads=[], writes=[out])

    def dma(self, out, in_, sem, eng="sp", **kw):
        self.p.dma(eng, out, in_, sem, **kw)

    def setup(self):
        nc, es = self.nc, self.es
        self.arena_cols = 52 * 1024 - 512
        self.arena = es.enter_context(nc.sbuf_tensor("arena", [128, self.arena_cols], F32))
        self.banks = []
        for i in range(8):
            ps = es.enter_context(nc.psum_tensor(f"ps{i}", [128, 512], F32))
            self.banks.append(Buf(ps[:, :], Tile(f"ps{i}")))
        self.identf = self.sb("identf", 128, F32)
        self.ident = self.sb("ident", 128, BF16)
        self.ones_bf = self.sb("ones_bf", 128, BF16)
        self.ones_f = self.sb("ones_f", 128, F32)
        self.memset(self.identf, 0.0)
        ia = self.identf.ap
        self.p.op("pool", lambda e: e.affine_select(out=ia, in_=ia, pattern=[[-1, 128]], compare_op=ALU.not_equal,
                                                   fill=1.0, base=0, channel_multiplier=1),
                  reads=[self.identf], writes=[self.identf])
        self.copy(self.ident, self.identf)
        self.memset(self.ones_f, 1.0)
        self.copy(self.ones_bf, self.ones_f)
        self.arena_floor = self.arena_off


    def norm_scale(self, xi, xsb, stb, junk=None):
        self.act(xsb, xi, AF.Square, accum=stb[:, 0:1])
        self.act(stb[:, 1:2], stb[:, 0:1], AF.Sqrt, scale=1.0 / D, bias=self.eps_t)
        self.recip(stb[:, 2:3], stb[:, 1:2])
        self.act(xsb, xi, AF.Copy, scale=stb[:, 2:3])

    def transpose_gain(self, xsb, bank, gT, dstv):
        bv = bank.bc(BF16).re("p (c t) -> p c t", c=8)
        for c in range(8):
            self.tr(bv[:, c, :], xsb[:, c * 128:(c + 1) * 128], self.ident)
        self.tt(dstv, bv, gT.re("p (c o) -> p c o", o=1).bcast([128, 8, 128]), ALU.mult)

    def post_norm_residual(self, banks2, s2, junk, pg, o, xr, half, lnexp=False):
        for hh in range(2):
            self.act(junk[:, 0:512], banks2[hh], AF.Square, accum=s2[:, hh:hh + 1])
        self.tt(s2[:, 2:3], s2[:, 0:1], s2[:, 1:2], ALU.add)
        if lnexp:
            assert not half
            self.act(s2[:, 3:4], s2[:, 2:3], AF.Ln, scale=1.0 / D, bias=self.eps_t)
            self.act(s2[:, 4:5], s2[:, 3:4], AF.Exp, scale=-0.5)
        elif half:
            self.act(s2[:, 3:4], s2[:, 2:3], AF.Sqrt, scale=4.0 / D, bias=self.eps4_t)
        else:
            self.act(s2[:, 3:4], s2[:, 2:3], AF.Sqrt, scale=1.0 / D, bias=self.eps_t)
        if not lnexp:
            self.recip(s2[:, 4:5], s2[:, 3:4])
        for hh in range(2):
            self.stt(o[:, hh * 512:(hh + 1) * 512], banks2[hh], s2[:, 4:5], pg[:, hh * 512:(hh + 1) * 512],
                     ALU.mult, ALU.mult)
        self.tt(o, o, xr, ALU.add)

    def ffn_phase(self, src, dst, pre_g, wkeys, post_g, tag, nblocks=NTOK // 512, after_weights=None):
        p = self.p
        k_in, k_out = wkeys
        pre = k_in in self.WB
        w_in = self.WB[k_in] if pre else self.I[k_in]
        w_out = self.WB[k_out] if pre else self.I[k_out]
        weng = "sp" if pre else "pool"
        w_in_v = w_in.re("(c p) f -> p c f", p=128)
        w_out_v = w_out.re("(c p) f -> p c f", p=128)
        gT = self.sb(f"gT{tag}", 8, F32)
        self.dma(gT, pre_g.re("(c p) -> p c", p=128), f"sm{tag}", allow_slow_non_contiguous=True)
        pg = self.sb(f"pg{tag}", D, F32)
        self.dma(pg, Buf(post_g.ap.partition_broadcast(128), post_g.t), f"sm{tag}")
        xin = [self.sb(f"xin{i}{tag}", D, F32) for i in range(2)]
        xs = [self.sb(f"xs{i}{tag}", D, BF16) for i in range(4)]
        junk = self.sb(f"junk{tag}", 512, BF16)
        xT = self.sb(f"xT{tag}", 8 * 512, BF16)
        xTv = xT.re("p (c t) -> p c t", c=8)
        actT = self.sb(f"actT{tag}", NFC * 512, BF16)
        actTv = actT.re("p (c t) -> p c t", c=NFC)
        sg = [self.sb(f"sg{i}{tag}", 512, F32) for i in range(2)]
        st = [self.sb(f"st{i}{tag}", 8, F32) for i in range(2)]
        st2 = [self.sb(f"st2{i}{tag}", 8, F32) for i in range(2)]
        xres = [self.sb(f"xres{i}{tag}", D, F32) for i in range(2)]
        ot = [self.sb(f"ot{i}{tag}", D, F32) for i in range(2)]
        B = self.banks

        def front_elem(b, tt):
            k = b * 4 + tt
            xi = xin[k % 2]
            self.dma(xi, src[k * 128:(k + 1) * 128, :], f"xin{k % 2}{tag}")
            self.norm_scale(xi, xs[k % 4], st[k % 2], junk)

        def front_pe(b, tt):
            k = b * 4 + tt
            self.transpose_gain(xs[k % 4], B[k % 2], gT, xTv[:, :, tt * 128:(tt + 1) * 128])

        for tt in range(4):
            front_elem(0, tt)
        gb_ = [0, 6, 12, 17, 22]
        W1g, jgrp = [], {}
        for g in range(4):
            j0, j1 = gb_[g], gb_[g + 1]
            gw = (j1 - j0) * 128
            Wt = self.sb(f"W1{tag}g{g}", 8 * 2 * gw, BF16).re("p (c f) -> p c f", c=8)
            self.dma(Wt[:, :, 0:gw], w_in_v[:, :, j0 * 128:j1 * 128], f"W1{tag}g{g}", eng=weng)
            self.dma(Wt[:, :, gw:2 * gw], w_in_v[:, :, DFF + j0 * 128:DFF + j1 * 128], f"W1{tag}g{g}", eng=weng)
            W1g.append(Wt)
            for j in range(j0, j1):
                jgrp[j] = (g, (j - j0) * 128, gw)
        W2 = self.sb(f"W2{tag}", NFC * D, BF16)
        W2v = W2.re("p (c f) -> p c f", c=NFC)
        for c in range(0, NFC, 11):
            self.dma(W2v[:, c:c + 11, :], w_out_v[:, c:c + 11, :], f"W2{tag}", eng=weng)
        if after_weights is not None:
            after_weights(after=[W2])
        for tt in range(4):
            front_pe(0, tt)
        for b in range(nblocks):
            for j in range(NFC):
                pg_, pu_ = B[(j % 2) * 2], B[(j % 2) * 2 + 1]
                g_, lo_, gw_ = jgrp[j]
                for c in range(8):
                    self.mm(pg_, W1g[g_][:, c, lo_:lo_ + 128], xTv[:, c, :], c == 0, c == 7)
                for c in range(8):
                    self.mm(pu_, W1g[g_][:, c, gw_ + lo_:gw_ + lo_ + 128], xTv[:, c, :], c == 0, c == 7)
                s = sg[j % 2]
                self.act(s, pg_, AF.Silu)
                self.tt(actTv[:, j, :], s, pu_, ALU.mult)
                if b + 1 < nblocks and j in (2, 7, 12, 17):
                    front_elem(b + 1, (j - 2) // 5)
            if b + 1 < nblocks:
                for tt in range(4):
                    front_pe(b + 1, tt)
            for tt in range(4):
                k = b * 4 + tt
                banks2 = (B[4 + (tt % 2) * 2], B[5 + (tt % 2) * 2])
                for hh in range(2):
                    for j in range(NFC):
                        self.mm(banks2[hh], actTv[:, j, tt * 128:(tt + 1) * 128], W2v[:, j, hh * 512:(hh + 1) * 512],
                                j == 0, j == NFC - 1)
                s2 = st2[k % 2]
                xr = xres[k % 2]
                self.dma(xr, src[k * 128:(k + 1) * 128, :], f"xres{k % 2}{tag}")
                o = ot[k % 2]
                self.post_norm_residual(banks2, s2, junk, pg, o, xr, half=True)
                self.dma(dst[k * 128:(k + 1) * 128, :], o, f"st{k % 4}{tag}")


    def wload(self, name, srckey, col0, ncols, sem):
        W = self.sb(name, 8 * ncols, BF16)
        Wv = W.re("p (c f) -> p c f", c=8)
        if srckey in self.WB:
            sv = self.WB[srckey].re("(c p) f -> p c f", p=128)
            for c0 in range(0, 8, 4):
                self.dma(Wv[:, c0:c0 + 4, :], sv[:, c0:c0 + 4, col0:col0 + ncols], sem, eng="sp")
        else:
            sv = self.I[srckey].re("(c p) f -> p c f", p=128)
            for c0 in range(0, 8, 2):
                self.dma(Wv[:, c0:c0 + 2, :], sv[:, c0:c0 + 2, col0:col0 + ncols], sem, eng="pool")
        return Wv

    def convert_weights(self, after=()):
        for key, rows, cols in (("w_in", D, N_IN), ("w_branch_a", D, D), ("w_branch_b", D, D), ("w_out", D, D),
                                ("ffn2_w_in", D, 2 * DFF), ("ffn2_w_out", DFF, D)):
            dst = self.dram_scr("wb_" + key, [rows, cols], BF16)
            dv = dst.re("(c p) f -> p c f", p=128)
            sv = self.I[key].re("(c p) f -> p c f", p=128)
            nchunk = rows // 128
            step = 2
            for c0 in range(0, nchunk, step):
                self.dma(dv[:, c0:c0 + step, :], sv[:, c0:c0 + step, :], "wb_" + key, eng="pool", after=after)
            self.WB[key] = dst

    def aff(self, buf, val, pattern, cm, cmp):
        self.memset(buf, val)
        a = buf.ap
        self.p.op("pool", lambda e: e.affine_select(out=a, in_=a, pattern=pattern, compare_op=cmp, fill=0.0,
                                                   base=0, channel_multiplier=cm), reads=[buf], writes=[buf])

    def pipeline(self, n_items, stages, order=None):
        ns = len(stages)
        for step in range(n_items + ns - 1):
            for si in (order if order is not None else range(ns - 1, -1, -1)):
                i = step - si
                if 0 <= i < n_items:
                    stages[si](i)

    def gla_pass_a(self, H1, S, nseq=SEQ_PER_CORE):
        I, B = self.I, self.banks
        Wg = self.wload("Wg", "w_in", 0, 3104, "Wg")
        gT = self.sb("gTm", 8, F32)
        self.dma(gT, I["mix_pre_g"].re("(c p) -> p c", p=128), "smA", allow_slow_non_contiguous=True)
        up = {"f": self.sb("upf", 512, F32), "b": self.sb("upb", 512, F32)}
        for d in "fb":
            self.memset(up[d][0:64, :], 0.0)
            self.dma(up[d][0:16, :], I["gla_decay_up_" + d], "smA")
            self.dma(up[d][32:33, :], I["gla_decay_bias_" + d].re("(o f) -> o f", o=1), "smA")
        tri = {"f": self.sb("triF", 128, F32), "b": self.sb("triB", 128, F32)}
        self.aff(tri["f"], -1.0 / 16, [[1, 128]], -1, ALU.is_ge)
        self.aff(tri["b"], -1.0 / 16, [[-1, 128]], 1, ALU.is_ge)
        xin = [self.sb(f"xinA{i}", D, F32) for i in range(2)]
        xs = [self.sb(f"xsA{i}", D, BF16) for i in range(4)]
        st = [self.sb(f"stA{i}", 8, F32) for i in range(2)]
        uT = self.sb("uTA", 8 * 512, BF16)
        uTv = uT.re("p (c t) -> p c t", c=8)
        qraw = self.sb("qraw", 4 * 512, F32).re("p (h t) -> p h t", h=4)
        kraw = self.sb("kraw", 4 * 512, F32).re("p (h t) -> p h t", h=4)
        sgr = self.sb("sgrA", 8 * 512, BF16).re("p (c t) -> p c t", c=8)
        z = {"f": self.sb("zf", 512, F32), "b": self.sb("zb", 512, F32)}
        for d in "fb":
            self.memset(z[d][0:64, :], 0.0)
            self.memset(z[d][32:33, :], 1.0)
        vtm = [self.sb(f"vtm{i}", D, BF16) for i in range(4)]
        e1 = [self.sb(f"e1_{i}", 512, F32) for i in range(2)]
        sp = [self.sb(f"sp_{i}", 512, F32) for i in range(2)]
        eb = [self.sb(f"eb_{i}", 512, F32) for i in range(2)]
        enb = [self.sb(f"enb_{i}", 512, F32) for i in range(2)]
        eblb = [self.sb(f"eblb{i}", 4, F32) for i in range(4)]
        qt = [self.sb(f"qt{i}", 512, BF16) for i in range(4)]
        kt = [self.sb(f"kt{i}", 512, BF16) for i in range(4)]
        ktm = [self.sb(f"ktm{i}", 512, BF16) for i in range(4)]
        Sb = self.sb("SbA", 1024, F32)
        Sbb = [self.sb(f"SbbA{i}", 1024, BF16) for i in range(2)]
        UTv = S["UT"].re("(c p) t -> p c t", p=128)
        qscale = 128.0 ** -0.5
        ZB, BTB, TRB, KVB = (B[0], B[1]), (B[2], B[3]), (B[4], B[5]), (B[6], B[7])
        blocks = [(seq, blk) for seq in range(nseq) for blk in range(3, -1, -1)]

        def front_elem(bi, tt):
            seq, blk = blocks[bi]
            r0 = seq * L + blk * 512
            k = bi * 4 + tt
            self.dma(xin[k % 2], H1[r0 + tt * 128:r0 + (tt + 1) * 128, :], f"xinA{k % 2}")
            self.norm_scale(xin[k % 2], xs[k % 4], st[k % 2], None)

        for tt in range(4):
            front_elem(0, tt)
        for bi, (seq, blk) in enumerate(blocks):
            r0 = seq * L + blk * 512
            if blk == 3:
                self.memset(Sb, 0.0)
            for tt in range(4):
                k = bi * 4 + tt
                self.transpose_gain(xs[k % 4], B[6 + k % 2], gT, uTv[:, :, tt * 128:(tt + 1) * 128])
            self.dma(UTv[:, :, r0:r0 + 512], uTv, "UTst", eng="pool")
            nb = 0
            for h in range(4):
                for (off, dstb) in ((OFF_GQ, qraw), (OFF_GK, kraw)):
                    bank = B[nb % 2]
                    nb += 1
                    for c in range(8):
                        self.mm(bank, Wg[:, c, off + h * 128:off + (h + 1) * 128], uTv[:, c, :], c == 0, c == 7)
                    if off == OFF_GQ:
                        self.act(dstb[:, h, :], bank, AF.Copy)
                    else:
                        self.copy(dstb[:, h, :], bank)
            for d, off in (("f", OFF_ZF), ("b", OFF_ZB)):
                bank = B[nb % 2]
                nb += 1
                for c in range(8):
                    self.mm(bank[0:16, :], Wg[:, c, off:off + 16], uTv[:, c, :], c == 0, c == 7)
                self.copy(z[d][0:16, :], bank[0:16, :])
            for ck in range(4):
                gc = seq * 16 + blk * 4 + ck
                for hh in range(2):
                    bank = B[2 + hh]
                    for c in range(8):
                        self.mm(bank, uTv[:, c, ck * 128:(ck + 1) * 128],
                                Wg[:, c, OFF_GV + hh * 512:OFF_GV + (hh + 1) * 512], c == 0, c == 7)
                    if hh == 0:
                        self.act(vtm[ck][:, 0:512], bank, AF.Copy)
                    else:
                        self.copy(vtm[ck][:, 512:1024], bank)
                self.dma(S["V"][gc], vtm[ck], f"Vst{ck}", eng="pool")
            for fc in range(8):
                bank = B[4 + fc % 2]
                for c in range(8):
                    self.mm(bank, Wg[:, c, OFF_GR + fc * 128:OFF_GR + (fc + 1) * 128], uTv[:, c, :], c == 0, c == 7)
                self.act(sgr[:, fc, :], bank, AF.Silu)
            for ck in range(4):
                gc = seq * 16 + blk * 4 + ck
                self.dma(S["SGR"][gc].re("p (c t) -> p c t", c=8), sgr[:, :, ck * 128:(ck + 1) * 128], "SGRst", eng="pool")
            items = [(ck, d) for ck in range(3, -1, -1) for d in "bf"]

            def s0(i):
                ck, d = items[i]
                self.mm(ZB[i % 2], z[d][0:33, ck * 128:(ck + 1) * 128], up[d][0:33, :], True, True)

            def s1(i):
                self.act(e1[i % 2], ZB[i % 2], AF.Exp, scale=-1.0)
                self.act(sp[i % 2], e1[i % 2], AF.Ln, bias=1.0)

            def s2(i):
                ck, d = items[i]
                for h in range(4):
                    self.mm(BTB[i % 2][:, h * 128:(h + 1) * 128], sp[i % 2][:, h * 128:(h + 1) * 128], tri[d], True, True)

            def s3(i):
                ck, d = items[i]
                gc = seq * 16 + blk * 4 + ck
                e_, en_ = eb[i % 2], enb[i % 2]
                self.act(e_, BTB[i % 2], AF.Exp)
                self.act(en_, BTB[i % 2], AF.Exp, scale=-1.0)
                q_, k_ = qt[i % 4], kt[i % 4]
                self.stt(q_.re("p (h t) -> p h t", h=4), qraw[:, :, ck * 128:(ck + 1) * 128], qscale,
                         e_.re("p (h t) -> p h t", h=4), ALU.mult, ALU.mult)
                self.tt(k_.re("p (h t) -> p h t", h=4), kraw[:, :, ck * 128:(ck + 1) * 128],
                        en_.re("p (h t) -> p h t", h=4), ALU.mult)
                ev = e_.re("p (h t) -> p h t", h=4)
                if d == "f":
                    self.copy(self.ebl[:, gc * 4:(gc + 1) * 4], ev[:, :, 127])
                else:
                    self.copy(eblb[i % 4], ev[:, :, 0])
                self.dma(S["Q" + d][gc], q_, f"Qst{i % 4}", eng="pool")
                self.dma(S["K" + d][gc], k_, f"Kst{i % 4}", eng="pool")

            def s4(i):
                trb = TRB[i % 2].bc(BF16)
                for h in range(4):
                    self.tr(trb[:, h * 128:(h + 1) * 128], kt[i % 4][:, h * 128:(h + 1) * 128], self.ident)

            def s5(i):
                ck, d = items[i]
                gc = seq * 16 + blk * 4 + ck
                km_ = ktm[i % 4]
                self.copy(km_, TRB[i % 2].bc(BF16)[:, 0:512])
                if d == "f":
                    self.dma(S["KFT"][gc], km_, f"KFTst{i % 4}", eng="pool")
                else:
                    sbb = Sbb[(i // 2) % 2]
                    self.act(sbb, Sb, AF.Copy)
                    self.dma(S["SB"][gc], sbb, f"SBst{(i // 2) % 2}", eng="pool")
                    for h in range(4):
                        self.mm(KVB[h // 2][:, (h % 2) * 256:(h % 2 + 1) * 256], km_[:, h * 128:(h + 1) * 128],
                                vtm[ck][:, h * 256:(h + 1) * 256], True, True)
                    for h in range(4):
                        sc = eblb[i % 4][:, h:h + 1]
                        self.ts(Sb[:, h * 256:(h + 1) * 256], Sb[:, h * 256:(h + 1) * 256], sc, ALU.mult)
                        self.stt(Sb[:, h * 256:(h + 1) * 256], KVB[h // 2][:, (h % 2) * 256:(h % 2 + 1) * 256], sc,
                                 Sb[:, h * 256:(h + 1) * 256], ALU.mult, ALU.add)
                if bi + 1 < len(blocks) and i % 2 == 1:
                    front_elem(bi + 1, i // 2)

            self.pipeline(len(items), [s0, s1, s2, s3, s4, s5])

    def gla_pass_b(self, S, nseq=SEQ_PER_CORE):
        I, B = self.I, self.banks
        Wa = self.wload("Wa", "w_branch_a", 0, D, "Wa")
        Wga = self.wload("Wga", "w_in", OFF_GA, D, "Wga")
        Wgb = self.wload("WgbB", "w_in", OFF_GB, D, "WgbB")
        SGBb = self.sb("SGBb", 8 * 512, BF16).re("p (c t) -> p c t", c=8)
        SGBv = S["SGB"].re("(c p) t -> p c t", p=128)
        gog = self.sb("gog", 8, F32)
        self.dma(gog, I["gla_out_g"].re("h (e p) -> p (h e)", p=128), "smB", allow_slow_non_contiguous=True)
        maskF = self.sb("maskF", 128, F32)
        maskB = self.sb("maskB", 128, F32)
        self.aff(maskF, 1.0, [[1, 128]], -1, ALU.is_ge)
        self.aff(maskB, 1.0, [[-1, 128]], 1, ALU.is_gt)
        NLD = 3
        names = ("Qf", "Qb", "Kf", "Kb", "KFT")
        ld = {n: [self.sb(f"ld{n}{i}", 512, BF16) for i in range(NLD)] for n in names}
        for n in ("V", "SGR", "SB"):
            ld[n] = [self.sb(f"ld{n}{i}", 1024, BF16) for i in range(NLD)]
        sT = {d: [self.sb(f"sT{d}{i}", 512, BF16) for i in range(2)] for d in "fb"}
        Sf = self.sb("SfB", 1024, F32)
        Sfb = self.sb("SfbB", 1024, BF16)
        sq = [self.sb(f"sqB{i}", 1024, BF16) for i in range(2)]
        oTs = [self.sb(f"oTsB{i}", 1024, F32) for i in range(3)]
        lnv = self.sb("lnvB", 512, F32)
        rstd = self.sb("rstdB", 512, F32)
        t1 = [self.sb(f"t1B{i}", 1024, F32) for i in range(5)]
        mT = [self.sb(f"mTB{i}", 8 * 512, BF16).re("p (c t) -> p c t", c=8) for i in range(2)]
        uT = [self.sb(f"uTB{i}", 8 * 512, BF16).re("p (c t) -> p c t", c=8) for i in range(3)]
        sga = [self.sb(f"sga{i}", 512, F32) for i in range(2)]
        MAb = [self.sb(f"MAb{i}", 8 * 512, BF16).re("p (c t) -> p c t", c=8) for i in range(2)]
        UTv = S["UT"].re("(c p) t -> p c t", p=128)
        MAv = S["MA"].re("(c p) t -> p c t", p=128)
        eps_t = self.eps_t
        SC = (B[0], B[1])
        OB = (B[2], B[3])
        KVB = (B[4], B[5])
        NB = B[6]
        EP = (B[7], B[6])
        nch = nseq * 16

        def T(i):
            return {n: ld[n][i % NLD] for n in ld}

        def p0(i):
            t = T(i)
            if i % 4 == 0:
                blk_r0 = (i // 4) * 512
                self.dma(uT[(i // 4) % 3], UTv[:, :, blk_r0:blk_r0 + 512], f"uTB{(i // 4) % 3}")
            for n in ("Kf", "Qf", "Kb", "Qb", "V", "SB", "KFT", "SGR"):
                self.dma(t[n], S[n][i], f"ldB{n}{i % NLD}")

        def p1(i):
            t = T(i)
            for di, d in enumerate("fb"):
                for h in range(4):
                    self.mm(SC[di][:, h * 128:(h + 1) * 128], t["K" + d][:, h * 128:(h + 1) * 128],
                            t["Q" + d][:, h * 128:(h + 1) * 128], True, True)

        def p2(i):
            for di, d in enumerate("fb"):
                m = (maskF if d == "f" else maskB).re("p (o t) -> p o t", o=1).bcast([128, 4, 128])
                self.tt(sT[d][i % 2].re("p (h t) -> p h t", h=4), SC[di].re("p (h t) -> p h t", h=4), m, ALU.mult)
            self.tt(t1[i % 5].re("p (c t) -> p c t", c=8), T(i)["SGR"].re("p (c t) -> p c t", c=8),
                    gog.re("p (c o) -> p c o", o=1).bcast([128, 8, 128]), ALU.mult)

        def p3(i):
            t = T(i)
            if i % 16 == 0:
                self.memset(Sf, 0.0)
                self.memset(Sfb, 0.0)
            for fc in range(8):
                h = fc // 2
                ob = OB[fc // 4][:, (fc % 4) * 128:(fc % 4 + 1) * 128]
                vcol = t["V"][:, fc * 128:(fc + 1) * 128]
                self.mm(ob, vcol, sT["f"][i % 2][:, h * 128:(h + 1) * 128], True, False)
                self.mm(ob, vcol, sT["b"][i % 2][:, h * 128:(h + 1) * 128], False, False)
                self.mm(ob, Sfb[:, fc * 128:(fc + 1) * 128], t["Qf"][:, h * 128:(h + 1) * 128], False, False)
                self.mm(ob, t["SB"][:, fc * 128:(fc + 1) * 128], t["Qb"][:, h * 128:(h + 1) * 128], False, True)
            for h in range(4):
                self.mm(KVB[h // 2][:, (h % 2) * 256:(h % 2 + 1) * 256], t["KFT"][:, h * 128:(h + 1) * 128],
                        t["V"][:, h * 256:(h + 1) * 256], True, True)

        def p4(i):
            for hb in range(2):
                self.act(sq[i % 2][:, hb * 512:(hb + 1) * 512], OB[hb], AF.Square)
                self.act(oTs[i % 3][:, hb * 512:(hb + 1) * 512], OB[hb], AF.Copy)
            for h in range(4):
                sc = self.ebl[:, i * 4 + h:i * 4 + h + 1]
                self.ts(Sf[:, h * 256:(h + 1) * 256], Sf[:, h * 256:(h + 1) * 256], sc, ALU.mult)
                self.stt(Sf[:, h * 256:(h + 1) * 256], KVB[h // 2][:, (h % 2) * 256:(h % 2 + 1) * 256], sc,
                         Sf[:, h * 256:(h + 1) * 256], ALU.mult, ALU.add)
            self.copy(Sfb, Sf)

        def p5(i):
            for h in range(4):
                nbk = NB[:, h * 128:(h + 1) * 128]
                self.mm(nbk, self.ones_bf, sq[i % 2][:, (2 * h) * 128:(2 * h + 1) * 128], True, False)
                self.mm(nbk, self.ones_bf, sq[i % 2][:, (2 * h + 1) * 128:(2 * h + 2) * 128], False, True)

        def p6(i):
            ck = i % 4
            blk = i // 4
            self.act(lnv, NB, AF.Ln, scale=1.0 / 256, bias=eps_t)
            self.act(rstd, lnv, AF.Exp, scale=-0.5)
            tv = t1[i % 5].re("p (h e t) -> p h e t", h=4, e=2)
            self.tt(tv, tv, rstd.re("p (h o t) -> p h o t", h=4, o=1).bcast([128, 4, 2, 128]), ALU.mult)
            m_ = mT[blk % 2]
            self.tt(m_[:, :, ck * 128:(ck + 1) * 128], oTs[i % 3].re("p (c t) -> p c t", c=8),
                    t1[i % 5].re("p (c t) -> p c t", c=8), ALU.mult)
            if ck == 3:
                push_block(blk)
            drain(6)

        epi_q = []
        epc = {"n": 0}

        def nextbank():
            epc["n"] += 1
            return EP[epc["n"] % 2]

        sg_e0 = self.sb("sgeB0", 512, F32)
        sg_e = [sg_e0, sg_e0]

        def sigm(dst, src):
            e_ = sg_e[epc["n"] % 2]
            self.act(e_, src, AF.Exp, scale=-1.0)
            self.act(e_, e_, AF.Ln, bias=1.0)
            self.act(dst, e_, AF.Exp, scale=-1.0)

        def push_block(blk):
            r0 = blk * 512
            u_, m_, MA_ = uT[blk % 3], mT[blk % 2], MAb[blk % 2]

            def ga(o):
                def f():
                    bg = nextbank()
                    for c in range(8):
                        self.mm(bg, Wga[:, c, o * 128:(o + 1) * 128], u_[:, c, :], c == 0, c == 7)
                    sigm(sga[o % 2], bg)
                return f

            def ya(o):
                def f():
                    by = nextbank()
                    for c in range(8):
                        self.mm(by, Wa[:, c, o * 128:(o + 1) * 128], m_[:, c, :], c == 0, c == 7)
                    self.tt(MA_[:, o, :], by, sga[o % 2], ALU.mult)
                    if o == 7:
                        self.dma(MAv[:, :, r0:r0 + 512], MA_, f"MAst{blk % 2}", eng="pool")
                return f

            def gb(o):
                def f():
                    bg = nextbank()
                    for c in range(8):
                        self.mm(bg, Wgb[:, c, o * 128:(o + 1) * 128], u_[:, c, :], c == 0, c == 7)
                    sigm(SGBb[:, o, :], bg)
                    if o == 7:
                        self.dma(SGBv[:, :, r0:r0 + 512], SGBb, "SGBst", eng="pool")
                return f

            for o in range(8):
                epi_q.extend([ga(o), ya(o), gb(o)])

        def drain(n):
            for _ in range(n):
                if epi_q:
                    epi_q.pop(0)()

        self.pipeline(nch, [p0, p1, p2, p3, p4, p5, p6], order=(4, 6, 5, 3, 2, 1, 0))
        drain(10 ** 6)

    def rope_chain(self, src, W, gbc, Ct, St, tile, dst, T):
        nh = W // 128
        ta, ss, lnv, rs, qn, tb = T
        self.act(ta[:, 0:W], src, AF.Square)
        self.reduce(ss[:, 0:nh], ta[:, 0:W].re("p (h d) -> p h d", h=nh))
        self.act(lnv[:, 0:nh], ss[:, 0:nh], AF.Ln, scale=1.0 / 128, bias=self.eps_t)
        self.act(rs[:, 0:nh], lnv[:, 0:nh], AF.Exp, scale=-0.5)
        qv = qn[:, 0:W].re("p (h d) -> p h d", h=nh)
        self.tt(qv, src.re("p (h d) -> p h d", h=nh),
                rs[:, 0:nh].re("p (h o) -> p h o", o=1).bcast([128, nh, 128]), ALU.mult)
        self.tt(qv, qv, gbc.re("p (o d) -> p o d", o=1).bcast([128, nh, 128]), ALU.mult)
        self.tt(ta[:, 0:W].re("p (h d) -> p h d", h=nh), qv,
                Ct[:, tile, :].re("p (o d) -> p o d", o=1).bcast([128, nh, 128]), ALU.mult)
        q5 = qn[:, 0:W].re("p (h a j e) -> p h a j e", h=nh, a=2, j=2)
        t5 = tb[:, 0:W].re("p (h a j e) -> p h a j e", h=nh, a=2, j=2)
        S5 = St[:, tile, :].re("p (o a j e) -> p o a j e", o=1, a=2, j=2)
        for j in range(2):
            self.tt(t5[:, :, :, j, :], q5[:, :, :, 1 - j, :], S5[:, :, :, j, :].bcast([128, nh, 2, 32]), ALU.mult)
        self.tt(dst, ta[:, 0:W], tb[:, 0:W], ALU.add)

    def attn_phase(self, H1, H2, S, nseq=SEQ_PER_CORE):
        I, B = self.I, self.banks
        Wq = self.wload("Wq", "w_in", OFF_AQ, D, "Wq")
        Wkv = self.wload("Wkv", "w_in", OFF_AK, 512, "Wkv")
        Wb = self.wload("Wb", "w_branch_b", 0, D, "Wb")
        Wo = self.wload("Wo", "w_out", 0, D, "Wo")
        pg = self.sb("pgM", D, F32)
        self.dma(pg, Buf(I["mix_post_g"].ap.partition_broadcast(128), I["mix_post_g"].t), "smC")
        Ct = self.sb("ropeC", 16 * 128, F32).re("p (n d) -> p n d", n=16)
        St = self.sb("ropeS", 16 * 128, F32).re("p (n d) -> p n d", n=16)
        rv = I["rope"].re("(n p) d -> p n d", p=128)
        self.dma(Ct, rv[:, :, 0:128], "smC")
        self.dma(St, rv[:, :, 128:256], "smC")
        gq = self.sb("g_q", 128, F32)
        gk = self.sb("g_k", 128, F32)
        self.dma(gq, Buf(I["att_q_norm_g"].ap.partition_broadcast(128), I["att_q_norm_g"].t), "smC")
        self.dma(gk, Buf(I["att_k_norm_g"].ap.partition_broadcast(128), I["att_k_norm_g"].t), "smC")
        KT = self.sb("KT", 2 * L, BF16).re("p (h t) -> p h t", h=2)
        Vt = self.sb("Vt", 16 * 256, BF16).re("p (n d) -> p n d", n=16)
        uT = [self.sb(f"uTC{i}", 8 * 512, BF16).re("p (c t) -> p c t", c=8) for i in range(2)]
        Ts = []
        for i in range(3):
            tb_ = self.sb(f"tbC{i}", 512, F32)
            Ts.append((self.sb(f"taC{i}", 512, F32), self.sb(f"ssC{i}", 4, F32), self.sb(f"lnvC{i}", 4, F32),
                       self.sb(f"rsC{i}", 4, F32), self.sb(f"qnC{i}", 512, F32), tb_, tb_))
        qr = [self.sb(f"qrC{i}", D, BF16) for i in range(2)]
        kr = [self.sb(f"krC{i}", 256, BF16) for i in range(3)]
        qT = [self.sb(f"qTC{i}", 8 * 512, BF16).re("p (h t) -> p h t", h=8) for i in range(2)]
        obT = [self.sb(f"obT{i}", 8 * 512, BF16).re("p (h t) -> p h t", h=8) for i in range(2)]
        PT = [self.sb(f"PT{i}", 512, BF16) for i in range(4)]
        rden = self.sb("rdenC", 512, F32)
        lden = self.sb("ldenC", 512, F32)
        if DEN_QUADS:
            ots = [self.sb(f"otsC{i}", 512, F32) for i in range(2)]
            PA = [self.sb(f"PAC{i}", 512, BF16) for i in range(2)]
            PQ = [self.sb(f"PQC{i}", 512, BF16) for i in range(3)]
        else:
            ots = [self.sb(f"otsC{i}", 512, F32) for i in range(2)]
        t2 = self.sb("t2C0", 512, F32)
        MAl = self.sb("MAl", 8 * 512, BF16).re("p (c t) -> p c t", c=8)
        SGl = self.sb("SGl", 8 * 512, BF16).re("p (c t) -> p c t", c=8)
        junk = self.sb("junkC", 512, BF16)
        st2 = [self.sb(f"st2C{i}", 8, F32) for i in range(2)]
        xres = self.sb("xresC0", D, F32)
        ot = self.sb("otC0", D, F32)
        fs = ot
        UTv = S["UT"].re("(c p) t -> p c t", p=128)
        MAv = S["MA"].re("(c p) t -> p c t", p=128)
        SGv = S["SGB"].re("(c p) t -> p c t", p=128)
        sm_scale = 128.0 ** -0.5
        SIDE = [B[0], B[1], B[2]]
        OT, DEN = B[3], B[4]
        STB = [B[5], B[6], B[7]]
        cnt = {"side": 0, "chain": 0, "u": 0}

        def side_bank():
            cnt["side"] += 1
            return SIDE[cnt["side"] % 3]

        def chain_T():
            cnt["chain"] += 1
            return Ts[cnt["chain"] % 3][:6]

        def rope_stages(src_fn, W, gbc, tile_fn, dst_fn):
            nh = W // 128
            stt_ = {}

            def s1():
                T_ = Ts[cnt["chain"] % 3]
                cnt["chain"] += 1
                stt_["T"] = T_
                ta, ss, lnv, rs, qn, tb, qs = T_
                src = src_fn()
                self.act(ta[:, 0:W], src, AF.Square)
                self.act(qs[:, 0:W], src, AF.Copy)

            def s2():
                ta, ss, lnv, rs, qn, tb, qs = stt_["T"]
                self.reduce(ss[:, 0:nh], ta[:, 0:W].re("p (h d) -> p h d", h=nh))

            def s3():
                ta, ss, lnv, rs, qn, tb, qs = stt_["T"]
                self.act(lnv[:, 0:nh], ss[:, 0:nh], AF.Ln, scale=1.0 / 128, bias=self.eps_t)
                self.act(rs[:, 0:nh], lnv[:, 0:nh], AF.Exp, scale=-0.5)

            def s4():
                ta, ss, lnv, rs, qn, tb, qs = stt_["T"]
                tile = tile_fn()
                qv = qn[:, 0:W].re("p (h d) -> p h d", h=nh)
                self.tt(qv, qs[:, 0:W].re("p (h d) -> p h d", h=nh),
                        rs[:, 0:nh].re("p (h o) -> p h o", o=1).bcast([128, nh, 128]), ALU.mult)
                self.tt(qv, qv, gbc.re("p (o d) -> p o d", o=1).bcast([128, nh, 128]), ALU.mult)
                self.tt(ta[:, 0:W].re("p (h d) -> p h d", h=nh), qv,
                        Ct[:, tile, :].re("p (o d) -> p o d", o=1).bcast([128, nh, 128]), ALU.mult)

            def s5():
                ta, ss, lnv, rs, qn, tb, qs = stt_["T"]
                tile = tile_fn()
                q5 = qn[:, 0:W].re("p (h a j e) -> p h a j e", h=nh, a=2, j=2)
                t5 = tb[:, 0:W].re("p (h a j e) -> p h a j e", h=nh, a=2, j=2)
                S5 = St[:, tile, :].re("p (o a j e) -> p o a j e", o=1, a=2, j=2)
                for j in range(2):
                    self.tt(t5[:, :, :, j, :], q5[:, :, :, 1 - j, :], S5[:, :, :, j, :].bcast([128, nh, 2, 32]), ALU.mult)
                self.tt(dst_fn(), ta[:, 0:W], tb[:, 0:W], ALU.add)

            return [s1, s2, s3, s4, s5]

        def q_items(seq, blk, par):
            items = []
            st = {}

            def qp(tt, hh):
                def s0():
                    u = uT[1]
                    if tt == 0 and hh == 0:
                        self.dma(u, UTv[:, :, seq * L + blk * 512:seq * L + (blk + 1) * 512], "uTC1")
                    bank = side_bank()
                    st[("b", tt, hh)] = bank
                    for c in range(8):
                        self.mm(bank, u[:, c, tt * 128:(tt + 1) * 128], Wq[:, c, hh * 512:(hh + 1) * 512], c == 0, c == 7)
                return [s0] + rope_stages(lambda: st[("b", tt, hh)], 512, gq, lambda: blk * 4 + tt,
                                          lambda: qr[tt % 2][:, hh * 512:(hh + 1) * 512])

            def qt(tt):
                def s0():
                    trb = side_bank().bc(BF16)
                    st[("t", tt)] = trb
                    for h in range(8):
                        self.tr(trb[:, h * 128:(h + 1) * 128], qr[tt % 2][:, h * 128:(h + 1) * 128], self.ident)

                def s1():
                    self.copy(qT[par][:, :, tt * 128:(tt + 1) * 128], st[("t", tt)].re("p (h t) -> p h t", h=8))
                return [s0, s1]

            for tt in range(4):
                if tt >= 1:
                    items.append((qt(tt - 1), [8]))
                items.append((qp(tt, 0), []))
                items.append((qp(tt, 1), []))
            items.append((qt(3), [8]))
            return items

        def epi_items(seq, blk, par):
            items = []
            r0 = seq * L + blk * 512
            st = {}

            def yb(o):
                def s0():
                    if o == 0:
                        self.dma(MAl, MAv[:, :, r0:r0 + 512], "MAld")
                        self.dma(SGl, SGv[:, :, r0:r0 + 512], "SGld")
                    by = side_bank()
                    st[("y", o)] = by
                    for h in range(8):
                        self.mm(by, Wb[:, h, o * 128:(o + 1) * 128], obT[par][:, h, :], h == 0, h == 7)

                def s1():
                    self.tt(t2, st[("y", o)], SGl[:, o, :], ALU.mult)
                    self.tt(MAl[:, o, :], t2, MAl[:, o, :], ALU.add)
                return [s0, s1]

            def op(tt):
                row = r0 + tt * 128
                s2_ = st2[tt % 2]

                def s0():
                    banks2 = (side_bank(), side_bank())
                    st[("o", tt)] = banks2
                    for hh in range(2):
                        for o in range(8):
                            self.mm(banks2[hh], MAl[:, o, tt * 128:(tt + 1) * 128], Wo[:, o, hh * 512:(hh + 1) * 512], o == 0, o == 7)
                    self.dma(xres, H1[row:row + 128, :], "xresC0")

                def s1():
                    banks2 = st[("o", tt)]
                    for hh in range(2):
                        self.act(junk[:, 0:512], banks2[hh], AF.Square, accum=s2_[:, hh:hh + 1])
                        self.act(fs[:, hh * 512:(hh + 1) * 512], banks2[hh], AF.Copy)

                def s2():
                    self.tt(s2_[:, 2:3], s2_[:, 0:1], s2_[:, 1:2], ALU.add)

                def s3():
                    self.act(s2_[:, 3:4], s2_[:, 2:3], AF.Ln, scale=1.0 / D, bias=self.eps_t)
                    self.act(s2_[:, 4:5], s2_[:, 3:4], AF.Exp, scale=-0.5)

                def s4():
                    for hh in range(2):
                        self.stt(ot[:, hh * 512:(hh + 1) * 512], fs[:, hh * 512:(hh + 1) * 512], s2_[:, 4:5],
                                 pg[:, hh * 512:(hh + 1) * 512], ALU.mult, ALU.mult)
                    self.tt(ot, ot, xres, ALU.add)
                    self.dma(H2[row:row + 128, :], ot, "stC0", eng="pool")
                return [s0, s1, s2, s3, s4]

            for o in range(8):
                items.append((yb(o), []))
            for tt in range(4):
                items.append((op(tt), [6]))
            return items

        def kv_items(seq):
            items = []
            st = {}

            def kvp(t):
                blk, tt = divmod(t, 4)

                def s0():
                    u = uT[0]
                    if tt == 0:
                        self.dma(u, UTv[:, :, seq * L + blk * 512:seq * L + (blk + 1) * 512], "uTC0")
                    bank = side_bank()
                    st[("b", t)] = bank
                    for c in range(8):
                        self.mm(bank, u[:, c, tt * 128:(tt + 1) * 128], Wkv[:, c, :], c == 0, c == 7)

                def sv():
                    self.act(Vt[:, t, :], st[("b", t)][:, 256:512], AF.Copy)

                rs_ = rope_stages(lambda: st[("b", t)][:, 0:256], 256, gk, lambda: t, lambda: kr[t % 3])

                def s1():
                    sv()
                    rs_[0]()

                def s6():
                    trb = side_bank().bc(BF16)
                    st[("t", t)] = trb
                    for hk in range(2):
                        self.tr(trb[:, hk * 128:(hk + 1) * 128], kr[t % 3][:, hk * 128:(hk + 1) * 128], self.ident)

                def s7():
                    self.copy(KT[:, :, t * 128:(t + 1) * 128], st[("t", t)][:, 0:256].re("p (h t) -> p h t", h=2))
                return [s0, s1] + rs_[1:] + [s6, s7]

            for t in range(16):
                items.append((kvp(t), []))
            return items

        GAP = 2

        def plan_side(items, nslots, spacing):
            plan = {}
            start_prev, ends = -spacing, []
            for stages, deps in items:
                start = start_prev + spacing
                if deps and ends:
                    start = max(start, max(ends) + deps[0])
                for k, f in enumerate(stages):
                    plan.setdefault(start + GAP * k, []).append(f)
                ends.append(start + GAP * (len(stages) - 1))
                start_prev = start
            return plan

        def run_streams_standalone(streams):
            plan = {}
            for lst in streams:
                for sl, fs_ in plan_side(lst, 0, 4).items():
                    plan.setdefault(sl, []).extend(fs_)
            for sl in sorted(plan):
                for f in plan[sl]:
                    f()

        def head_loop(par, side):
            q_ = qT[par]
            ob = obT[par]
            its = [(h, kt) for h in range(8) for kt in range(16)]
            n = len(its)
            plan = {}
            for lst in side:
                for sl, fs_ in plan_side(lst, n, 4).items():
                    plan.setdefault(sl, []).extend(fs_)

            def issue_st(i):
                h, kt = its[i]
                self.mm(STB[i % 3], KT[:, h // 4, kt * 128:(kt + 1) * 128], q_[:, h, :], True, True)

            issue_st(0)
            issue_st(1)
            pend = []
            DLAG = 3

            def run_due(i):
                while pend and pend[0][0] <= i:
                    pend.pop(0)[1]()

            for i in range(n):
                h, kt = its[i]
                hk = h // 4
                pt = PT[i % 4]
                self.act(pt, STB[i % 3], AF.Exp, scale=sm_scale)
                if i + 2 < n:
                    issue_st(i + 2)
                self.mm(OT, Vt[:, kt, hk * 128:(hk + 1) * 128], pt, kt == 0, kt == 15)
                if not DEN_QUADS:
                    self.mm(DEN, self.ones_bf, pt, kt == 0, kt == 15)
                run_due(i)
                if DEN_QUADS:
                    if kt % 4 == 1:
                        self.tt(PA[0], PT[(i - 1) % 4], pt, ALU.add)
                    elif kt % 4 == 3:
                        self.tt(PA[1], PT[(i - 1) % 4], pt, ALU.add)
                        pq = PQ[(i // 4) % 3]
                        self.tt(pq, PA[0], PA[1], ALU.add)

                        def den_mm(g_=kt // 4, pq_=pq):
                            self.mm(DEN, self.ones_bf, pq_, g_ == 0, g_ == 3)
                        pend.append((i + DLAG, den_mm))
                if kt == 15:
                    ots_ = ots[h % 2]
                    self.copy(ots_, OT)
                    self.act(lden, DEN, AF.Ln)

                    def fin(h_=h, ots__=ots_):
                        self.act(rden, lden, AF.Exp, scale=-1.0)
                        self.tt(ob[:, h_, :], ots__, rden, ALU.mult)
                    pend.append((i + (DLAG if DEN_QUADS else 1), fin))
                for f in plan.get(i, ()):
                    f()
            run_due(n + DLAG)
            for sl in sorted(k for k in plan if k >= n):
                for f in plan[sl]:
                    f()

        for seq in range(nseq):
            kv_, q_ = kv_items(seq), q_items(seq, 0, 0)
            merged = []
            while kv_ or q_:
                if kv_:
                    merged.append(kv_.pop(0))
                if q_:
                    merged.append(q_.pop(0))
            run_streams_standalone([merged])
            for blk in range(4):
                par = blk % 2
                side = []
                if blk > 0:
                    side.append(epi_items(seq, blk - 1, 1 - par))
                if blk < 3:
                    side.append(q_items(seq, blk + 1, 1 - par))
                head_loop(par, side)
            run_streams_standalone([epi_items(seq, 3, 1)])

    def build(self, phases=("ffn1", "glaA", "glaB", "attn", "ffn2"), nblocks=None, nseq=SEQ_PER_CORE):
        self.nseq = nseq
        nc = self.nc
        I = {}
        I["x"] = self.dram_in("x", [NTOK, D])
        for n, shp in (("ffn1_pre_g", [D]), ("ffn1_w_in", [D, 2 * DFF]), ("ffn1_w_out", [DFF, D]),
                       ("ffn1_post_g", [D]), ("mix_pre_g", [D]), ("w_in", [D, N_IN]),
                       ("gla_decay_up_f", [16, 512]), ("gla_decay_bias_f", [512]),
                       ("gla_decay_up_b", [16, 512]), ("gla_decay_bias_b", [512]),
                       ("gla_out_g", [4, 256]), ("w_branch_a", [D, D]), ("att_q_norm_g", [128]),
                       ("att_k_norm_g", [128]), ("w_branch_b", [D, D]), ("w_out", [D, D]),
                       ("mix_post_g", [D]), ("ffn2_pre_g", [D]), ("ffn2_w_in", [D, 2 * DFF]),
                       ("ffn2_w_out", [DFF, D]), ("ffn2_post_g", [D])):
            I[n] = self.dram_in(n, shp)
        I["rope"] = self.dram_in("rope", [L, 256])
        out = self.dram_out("out", [NTOK, D])
        H1 = self.dram_scr("H1", [NTOK, D], F32)
        H2 = self.dram_scr("H2", [NTOK, D], F32)
        self.I = I
        self.setup()
        self.eps_t = self.sb("eps_t", 1, F32)
        self.memset(self.eps_t, EPS)
        self.eps4_t = self.sb("eps4_t", 1, F32)
        self.memset(self.eps4_t, 4 * EPS)
        self.arena_floor = self.arena_off
        nb = nblocks or NTOK // 512
        self.ebl = self.sb("ebl", 128, F32)
        self.arena_floor = self.arena_off
        S = {}
        for n in ("Qf", "Qb", "Kf", "Kb", "KFT"):
            S[n] = self.dram_scr("scr_" + n, [32, 128, 512], BF16)
        for n in ("V", "SGR", "SB"):
            S[n] = self.dram_scr("scr_" + n, [32, 128, 1024], BF16)
        S["UT"] = self.dram_scr("scr_UT", [D, NTOK], BF16)
        S["MA"] = self.dram_scr("scr_MA", [D, NTOK], BF16)
        S["SGB"] = self.dram_scr("scr_SGB", [D, NTOK], BF16)
        self.S = S
        self.WB = {}
        last = phases[-1]
        if "ffn1" in phases:
            self.ffn_phase(I["x"], out if last == "ffn1" else H1, I["ffn1_pre_g"], ("ffn1_w_in", "ffn1_w_out"),
                           I["ffn1_post_g"], "a", nblocks=nb,
                           after_weights=self.convert_weights if len(phases) > 1 else None)
            self.phase_reset()
        src_mix = H1 if "ffn1" in phases else I["x"]
        nseq = self.nseq
        if "glaA" in phases:
            self.gla_pass_a(src_mix, S, nseq)
            self.phase_reset()
        if "glaB" in phases:
            self.gla_pass_b(S, nseq)
            self.phase_reset()
        if "attn" in phases:
            self.attn_phase(src_mix, out if last == "attn" else H2, S, nseq)
            self.phase_reset()
        if "ffn2" in phases:
            self.ffn_phase(H2, out, I["ffn2_pre_g"], ("ffn2_w_in", "ffn2_w_out"), I["ffn2_post_g"], "b", nblocks=nb)
            self.phase_reset()
        self.p.final_wait()
        self.p.emit(nc, self.es)
        self.es.close()
        return nc


_ROPE = None


def rope_tables():
    global _ROPE
    if _ROPE is None:
        half = 64
        inv = 10000.0 ** (-np.arange(0, half, 2, dtype=np.float32) / half)
        t = np.arange(L)
        row = (t // 64).astype(np.float32)[:, None] * inv[None, :]
        col = (t % 64).astype(np.float32)[:, None] * inv[None, :]
        cr, sr, cc, sc = np.cos(row), np.sin(row), np.cos(col), np.sin(col)
        C = np.concatenate([cr, cr, cc, cc], axis=1)
        S = np.concatenate([-sr, sr, -sc, sc], axis=1)
        _ROPE = np.ascontiguousarray(np.concatenate([C, S], axis=1).astype(np.float32))
    return _ROPE


def make_in_maps(inputs):
    x = np.ascontiguousarray(np.asarray(inputs["x"], dtype=np.float32))
    maps = []
    shared = {}
    for k, v in inputs.items():
        if k == "x":
            continue
        a = np.asarray(v, dtype=np.float32)
        shared[k] = np.ascontiguousarray(a.reshape(a.shape[1:]))
    shared["rope"] = rope_tables()
    for c in range(NCORES):
        m = dict(shared)
        m["x"] = x[c * SEQ_PER_CORE:(c + 1) * SEQ_PER_CORE].reshape(NTOK, D)
        maps.append(m)
    return maps


_NC = None


def kernel(**inputs):
    global _NC
    if _NC is None:
        _NC = Builder().build()
    maps = make_in_maps(inputs)
    res = run_bass_kernel_spmd(_NC, maps, core_ids=list(range(NCORES)))
    outs = [np.asarray(r["out"]).reshape(SEQ_PER_CORE, L, D) for r in res.results]
    return np.concatenate(outs, axis=0).astype(np.float32)
```

```python
from contextlib import ExitStack

import numpy as np
import concourse.bass as bass
import concourse.mybir as mybir
from concourse.bass_utils import run_bass_kernel_spmd

F32 = mybir.dt.float32
BF16 = mybir.dt.bfloat16
AF = mybir.ActivationFunctionType
ALU = mybir.AluOpType
AX = mybir.AxisListType

NCORES = 8
DEN_QUADS = False
D = 1024
L = 2048
SEQ_PER_CORE = 2
NTOK = L * SEQ_PER_CORE
DFF = 2816
NFC = DFF // 128
EPS = 1e-6
N_IN = 6688
OFF_GQ, OFF_GK, OFF_GV, OFF_GR, OFF_ZF, OFF_ZB = 0, 512, 1024, 2048, 3072, 3088
OFF_AQ, OFF_AK, OFF_AV, OFF_GA, OFF_GB = 3104, 4128, 4384, 4640, 5664

ENGS = ("pe", "act", "dve", "pool", "sp")


class Tile:
    __slots__ = ("name", "w", "r", "rdma")

    def __init__(self, name):
        self.name = name
        self.w = None
        self.r = {}
        self.rdma = {}


class Buf:
    def __init__(self, ap, t):
        self.ap = ap
        self.t = t

    def __getitem__(self, k):
        return Buf(self.ap[k], self.t)

    def bc(self, dt):
        return Buf(self.ap.bitcast(dt), self.t)

    def re(self, pat, **kw):
        return Buf(self.ap.rearrange(pat, **kw), self.t)

    def bcast(self, shape):
        return Buf(self.ap.to_broadcast(list(shape)), self.t)


class OpRec:
    __slots__ = ("eng", "idx", "sig", "waits", "fn", "dma_inc", "rank")

    def __init__(self, eng, idx, fn):
        self.eng = eng
        self.idx = idx
        self.sig = False
        self.waits = []
        self.fn = fn
        self.dma_inc = None
        self.rank = 0


class Prog:
    def __init__(self):
        self.q = {e: [] for e in ENGS}
        self.lastc = {e: None for e in ENGS}
        self.seen = {e: {} for e in ENGS}
        self.dma_cnt = {}
        self.dma_owner = {}
        self.nbar = 0
        self.semmap = {}
        self.free_phys = []
        self.nphys = 0

    def _need(self, eng, need, marker, same_eng_ok):
        if marker is None:
            return
        if marker[0] == "op":
            op = marker[1]
            if op.eng == eng and (same_eng_ok or eng == "pe"):
                return
            s, v = "e_" + op.eng, op.idx
            cur = need.get(s)
            if cur is None or cur[0] < v:
                need[s] = (v, op)
        else:
            s, v = marker[1], marker[2]
            cur = need.get(s)
            if cur is None or cur[0] < v:
                need[s] = (v, None)

    def _commit(self, eng, need):
        out = []
        seen = self.seen[eng]
        for s, (v, op) in need.items():
            if seen.get(s, -1) < v:
                seen[s] = v
                if op is not None:
                    op.sig = True
                    out.append((s, op))
                else:
                    out.append((s, v))
        return out

    def _deps(self, eng, reads, writes):
        need = {}
        for t in reads:
            self._need(eng, need, t.w, False)
        for t in writes:
            self._need(eng, need, t.w, True)
            for op in t.r.values():
                self._need(eng, need, ("op", op), True)
            for s, v in t.rdma.items():
                self._need(eng, need, ("dma", s, v), True)
        return self._commit(eng, need)

    def op(self, eng, fn, reads=(), writes=()):
        rt = [b.t for b in reads]
        wt = [b.t for b in writes]
        waits = self._deps(eng, rt, wt)
        rec = OpRec(eng, len(self.q[eng]), fn)
        rec.waits = waits
        self.q[eng].append(rec)
        self.lastc[eng] = rec
        for t in rt:
            t.r[eng] = rec
        for t in wt:
            t.w = ("op", rec)
            t.r = {}
            t.rdma = {}
        return rec

    def dma(self, eng, out, in_, sem, after=(), **kw):
        if sem not in self.semmap:
            if self.free_phys:
                self.semmap[sem] = self.free_phys.pop(0)
            else:
                self.semmap[sem] = f"d{self.nphys}"
                self.nphys += 1
        sem = self.semmap[sem]
        rt, wt = [in_.t], [out.t]
        need = {}
        self._need(eng, need, in_.t.w, False)
        for b_ in after:
            self._need(eng, need, b_.t.w, False)
        self._need(eng, need, out.t.w, True)
        for op in out.t.r.values():
            self._need(eng, need, ("op", op), True)
        for s_, v_ in out.t.rdma.items():
            self._need(eng, need, ("dma", s_, v_), True)
        cur = self.dma_cnt.get(sem, 0)
        if self.dma_owner.get(sem) is not out.t and cur > 0:
            self._need(eng, need, ("dma", sem, cur), True)
        waits = self._commit(eng, need)
        self.dma_owner[sem] = out.t
        o_ap, i_ap = out.ap, in_.ap
        rec = OpRec(eng, len(self.q[eng]), lambda e: e.dma_start(out=o_ap, in_=i_ap, **kw))
        rec.waits = waits
        self.dma_cnt[sem] = cur + 16
        rec.dma_inc = sem
        self.q[eng].append(rec)
        in_.t.rdma[sem] = cur + 16
        out.t.w = ("dma", sem, cur + 16)
        out.t.r = {}
        out.t.rdma = {}
        return rec

    def _all_done_waits(self, eng):
        need = {}
        for e in ("pe", "act", "dve", "pool"):
            if self.lastc[e] is not None:
                self._need(eng, need, ("op", self.lastc[e]), False)
        for s, v in self.dma_cnt.items():
            self._need(eng, need, ("dma", s, v), False)
        return self._commit(eng, need)

    def barrier(self):
        waits = self._all_done_waits("sp")
        self.free_phys = sorted(set(self.free_phys) | set(self.semmap.values()), key=lambda n: int(n[1:]))
        self.semmap = {}
        self.nbar += 1
        nb = self.nbar
        rec = OpRec("sp", len(self.q["sp"]), ("bar", nb))
        rec.waits = waits
        self.q["sp"].append(rec)
        for e in ("pe", "act", "dve", "pool"):
            r2 = OpRec(e, len(self.q[e]), None)
            r2.waits = [("bar", nb)]
            self.q[e].append(r2)
            for s, v in self.seen["sp"].items():
                if self.seen[e].get(s, -1) < v:
                    self.seen[e][s] = v

    def final_wait(self):
        rec = OpRec("sp", len(self.q["sp"]), None)
        rec.waits = self._all_done_waits("sp")
        self.q["sp"].append(rec)

    def emit(self, nc, es):
        semnames = set(["bar"])
        for e in ENGS:
            semnames.add("e_" + e)
            k = 0
            for rec in self.q[e]:
                if rec.sig:
                    k += 1
                    rec.rank = k
                for s, _ in rec.waits:
                    semnames.add(s)
                if rec.dma_inc:
                    semnames.add(rec.dma_inc)
        sems = {s: es.enter_context(nc.semaphore(s)) for s in sorted(semnames)}
        block = es.enter_context(nc.Block())
        engmap = {"pe": block.tensor, "act": block.scalar, "dve": block.vector,
                  "pool": block.gpsimd, "sp": block.sync}

        def mk(ename):
            def body(eng):
                for rec in self.q[ename]:
                    for s, v in rec.waits:
                        eng.wait_ge(sems[s], v.rank if isinstance(v, OpRec) else v)
                    if rec.fn is None:
                        continue
                    if isinstance(rec.fn, tuple):
                        eng.sem_inc(sems["bar"], 1)
                        continue
                    ins = rec.fn(eng)
                    if rec.dma_inc:
                        ins.then_inc(sems[rec.dma_inc], 16)
                    elif rec.sig:
                        ins.then_inc(sems["e_" + ename], 1)
            return body

        for ename in ENGS:
            engmap[ename](mk(ename))


class Builder:
    def __init__(self, debug=()):
        self.debug = set(debug)
        self.nc = bass.Bass("TRN2", target_bir_lowering=False)
        self.p = Prog()
        self.es = ExitStack()
        self.arena_off = 0
        self.arena_floor = 0

    def dram_in(self, name, shape, dt=F32):
        h = self.nc.dram_tensor(name, list(shape), dt, kind="ExternalInput")
        return Buf(h.ap(), Tile(name))

    def dram_out(self, name, shape, dt=F32):
        h = self.nc.dram_tensor(name, list(shape), dt, kind="ExternalOutput")
        return Buf(h.ap(), Tile(name))

    def dram_scr(self, name, shape, dt):
        kind = "ExternalOutput" if name in self.debug else "Internal"
        h = self.nc.dram_tensor(name, list(shape), dt, kind=kind)
        return Buf(h.ap(), Tile(name))

    def sb(self, name, cols, dt=F32):
        nby = cols * (4 if dt == F32 else 2)
        n4 = (nby + 31) // 32 * 8
        off = self.arena_off
        assert off + n4 <= self.arena_cols, f"SBUF arena overflow at {name}: {(off + n4) * 4} B"
        self.arena_off += n4
        ap = self.arena[:, off:off + n4]
        if dt != F32:
            ap = ap.bitcast(dt)[:, 0:cols]
        else:
            ap = ap[:, 0:cols]
        return Buf(ap, Tile(name))

    def phase_reset(self):
        self.p.barrier()
        self.arena_off = self.arena_floor

    def mm(self, out, lhsT, rhs, start, stop, extra_reads=()):
        o, a, b = out.ap, lhsT.ap, rhs.ap
        self.p.op("pe", lambda e: e.matmul(o, lhsT=a, rhs=b, start=start, stop=stop),
                  reads=[lhsT, rhs, *extra_reads], writes=[out])

    def tr(self, out, in_, ident):
        o, a, i = out.ap, in_.ap, ident.ap
        self.p.op("pe", lambda e: e.transpose(o, a, i), reads=[in_, ident], writes=[out])

    def act(self, out, in_, func, scale=1.0, bias=0.0, accum=None, eng="act"):
        o, a = out.ap, in_.ap
        reads = [in_]
        writes = [out]
        kw = {}
        if isinstance(scale, Buf):
            reads.append(scale)
            kw["scale"] = scale.ap
        else:
            kw["scale"] = float(scale)
        if isinstance(bias, Buf):
            reads.append(bias)
            kw["bias"] = bias.ap
        elif bias != 0.0:
            kw["bias"] = float(bias)
        if accum is not None:
            writes.append(accum)
            kw["accum_out"] = accum.ap
        self.p.op(eng, lambda e: e.activation(out=o, in_=a, func=func, **kw), reads=reads, writes=writes)

    def tt(self, out, in0, in1, op, eng="dve"):
        o, a, b = out.ap, in0.ap, in1.ap
        self.p.op(eng, lambda e: e.tensor_tensor(out=o, in0=a, in1=b, op=op), reads=[in0, in1], writes=[out])

    def ts(self, out, in0, s1, op0, s2=None, op1=None, eng="dve"):
        o, a = out.ap, in0.ap
        reads = [in0]
        if isinstance(s1, Buf):
            reads.append(s1)
            s1v = s1.ap
        else:
            s1v = float(s1)
        if isinstance(s2, Buf):
            reads.append(s2)
            s2v = s2.ap
        elif s2 is None:
            s2v = None
        else:
            s2v = float(s2)
        if op1 is None:
            self.p.op(eng, lambda e: e.tensor_scalar(out=o, in0=a, scalar1=s1v, scalar2=None, op0=op0),
                      reads=reads, writes=[out])
        else:
            self.p.op(eng, lambda e: e.tensor_scalar(out=o, in0=a, scalar1=s1v, scalar2=s2v, op0=op0, op1=op1),
                      reads=reads, writes=[out])

    def stt(self, out, in0, scalar, in1, op0, op1, eng="dve"):
        o, a, b = out.ap, in0.ap, in1.ap
        reads = [in0, in1]
        if isinstance(scalar, Buf):
            reads.append(scalar)
            sv = scalar.ap
        else:
            sv = float(scalar)
        self.p.op(eng, lambda e: e.scalar_tensor_tensor(out=o, in0=a, scalar=sv, in1=b, op0=op0, op1=op1),
                  reads=reads, writes=[out])

    def copy(self, out, in_, eng="dve"):
        o, a = out.ap, in_.ap
        self.p.op(eng, lambda e: e.tensor_copy(out=o, in_=a), reads=[in_], writes=[out])

    def recip(self, out, in_):
        o, a = out.ap, in_.ap
        self.p.op("dve", lambda e: e.reciprocal(out=o, in_=a), reads=[in_], writes=[out])

    def reduce(self, out, in_, op=ALU.add, eng="dve"):
        o, a = out.ap, in_.ap
        self.p.op(eng, lambda e: e.tensor_reduce(out=o, in_=a, axis=AX.X, op=op), reads=[in_], writes=[out])

    def memset(self, out, val, eng="pool"):
        o = out.ap
        self.p.op(eng, lambda e: e.memset(o, val), reads=[], writes=[out])

    def dma(self, out, in_, sem, eng="sp", **kw):
        self.p.dma(eng, out, in_, sem, **kw)

    def setup(self):
        nc, es = self.nc, self.es
        self.arena_cols = 52 * 1024 - 512
        self.arena = es.enter_context(nc.sbuf_tensor("arena", [128, self.arena_cols], F32))
        self.banks = []
        for i in range(8):
            ps = es.enter_context(nc.psum_tensor(f"ps{i}", [128, 512], F32))
            self.banks.append(Buf(ps[:, :], Tile(f"ps{i}")))
        self.identf = self.sb("identf", 128, F32)
        self.ident = self.sb("ident", 128, BF16)
        self.ones_bf = self.sb("ones_bf", 128, BF16)
        self.ones_f = self.sb("ones_f", 128, F32)
        self.memset(self.identf, 0.0)
        ia = self.identf.ap
        self.p.op("pool", lambda e: e.affine_select(out=ia, in_=ia, pattern=[[-1, 128]], compare_op=ALU.not_equal,
                                                   fill=1.0, base=0, channel_multiplier=1),
                  reads=[self.identf], writes=[self.identf])
        self.copy(self.ident, self.identf)
        self.memset(self.ones_f, 1.0)
        self.copy(self.ones_bf, self.ones_f)
        self.arena_floor = self.arena_off


    def norm_scale(self, xi, xsb, stb, junk=None, lnexp=False):
        self.act(xsb, xi, AF.Square, accum=stb[:, 0:1])
        if lnexp:
            self.act(stb[:, 1:2], stb[:, 0:1], AF.Ln, scale=1.0 / D, bias=self.eps_t)
            self.act(stb[:, 2:3], stb[:, 1:2], AF.Exp, scale=-0.5)
        else:
            self.act(stb[:, 1:2], stb[:, 0:1], AF.Sqrt, scale=1.0 / D, bias=self.eps_t)
            self.recip(stb[:, 2:3], stb[:, 1:2])
        self.act(xsb, xi, AF.Copy, scale=stb[:, 2:3])

    def transpose_gain(self, xsb, bank, gT, dstv):
        bv = bank.bc(BF16).re("p (c t) -> p c t", c=8)
        for c in range(8):
            self.tr(bv[:, c, :], xsb[:, c * 128:(c + 1) * 128], self.ident)
        self.tt(dstv, bv, gT.re("p (c o) -> p c o", o=1).bcast([128, 8, 128]), ALU.mult)

    def post_norm_residual(self, banks2, s2, junk, pg, o, xr, half, lnexp=False):
        for hh in range(2):
            self.act(junk[:, 0:512], banks2[hh], AF.Square, accum=s2[:, hh:hh + 1])
        self.tt(s2[:, 2:3], s2[:, 0:1], s2[:, 1:2], ALU.add)
        if lnexp:
            assert not half
            self.act(s2[:, 3:4], s2[:, 2:3], AF.Ln, scale=1.0 / D, bias=self.eps_t)
            self.act(s2[:, 4:5], s2[:, 3:4], AF.Exp, scale=-0.5)
        elif half:
            self.act(s2[:, 3:4], s2[:, 2:3], AF.Sqrt, scale=4.0 / D, bias=self.eps4_t)
        else:
            self.act(s2[:, 3:4], s2[:, 2:3], AF.Sqrt, scale=1.0 / D, bias=self.eps_t)
        if not lnexp:
            self.recip(s2[:, 4:5], s2[:, 3:4])
        for hh in range(2):
            self.stt(o[:, hh * 512:(hh + 1) * 512], banks2[hh], s2[:, 4:5], pg[:, hh * 512:(hh + 1) * 512],
                     ALU.mult, ALU.mult)
        self.tt(o, o, xr, ALU.add)

    def ffn_phase(self, src, dst, pre_g, wkeys, post_g, tag, nblocks=NTOK // 512, after_weights=None):
        p = self.p
        k_in, k_out = wkeys
        pre = k_in in self.WB
        w_in = self.WB[k_in] if pre else self.I[k_in]
        w_out = self.WB[k_out] if pre else self.I[k_out]
        weng = "sp" if pre else "pool"
        w_in_v = w_in.re("(c p) f -> p c f", p=128)
        w_out_v = w_out.re("(c p) f -> p c f", p=128)
        gT = self.sb(f"gT{tag}", 8, F32)
        self.dma(gT, pre_g.re("(c p) -> p c", p=128), f"sm{tag}", allow_slow_non_contiguous=True)
        pg = self.sb(f"pg{tag}", D, F32)
        self.dma(pg, Buf(post_g.ap.partition_broadcast(128), post_g.t), f"sm{tag}")
        xin = [self.sb(f"xin{i}{tag}", D, F32) for i in range(2)]
        xs = [self.sb(f"xs{i}{tag}", D, BF16) for i in range(4)]
        junk = self.sb(f"junk{tag}", 512, BF16)
        xT = self.sb(f"xT{tag}", 8 * 512, BF16)
        xTv = xT.re("p (c t) -> p c t", c=8)
        actT = self.sb(f"actT{tag}", NFC * 512, BF16)
        actTv = actT.re("p (c t) -> p c t", c=NFC)
        sg = [self.sb(f"sg{i}{tag}", 512, F32) for i in range(2)]
        st = [self.sb(f"st{i}{tag}", 8, F32) for i in range(2)]
        st2 = [self.sb(f"st2{i}{tag}", 8, F32) for i in range(2)]
        xres = [self.sb(f"xres{i}{tag}", D, F32) for i in range(2)]
        ot = [self.sb(f"ot{i}{tag}", D, F32) for i in range(2)]
        B = self.banks

        def front_elem(b, tt):
            k = b * 4 + tt
            xi = xin[k % 2]
            self.dma(xi, src[k * 128:(k + 1) * 128, :], f"xin{k % 2}{tag}")
            self.norm_scale(xi, xs[k % 4], st[k % 2], junk)

        def front_pe(b, tt):
            k = b * 4 + tt
            self.transpose_gain(xs[k % 4], B[k % 2], gT, xTv[:, :, tt * 128:(tt + 1) * 128])

        for tt in range(4):
            front_elem(0, tt)
        gb_ = [0, 6, 12, 17, 22]
        W1g, jgrp = [], {}
        for g in range(4):
            j0, j1 = gb_[g], gb_[g + 1]
            gw = (j1 - j0) * 128
            Wt = self.sb(f"W1{tag}g{g}", 8 * 2 * gw, BF16).re("p (c f) -> p c f", c=8)
            self.dma(Wt[:, :, 0:gw], w_in_v[:, :, j0 * 128:j1 * 128], f"W1{tag}g{g}", eng=weng)
            self.dma(Wt[:, :, gw:2 * gw], w_in_v[:, :, DFF + j0 * 128:DFF + j1 * 128], f"W1{tag}g{g}", eng=weng)
            W1g.append(Wt)
            for j in range(j0, j1):
                jgrp[j] = (g, (j - j0) * 128, gw)
        W2 = self.sb(f"W2{tag}", NFC * D, BF16)
        W2v = W2.re("p (c f) -> p c f", c=NFC)
        for c in range(0, NFC, 11):
            self.dma(W2v[:, c:c + 11, :], w_out_v[:, c:c + 11, :], f"W2{tag}", eng=weng)
        if after_weights is not None:
            after_weights(after=[W2])
        for tt in range(4):
            front_pe(0, tt)
        for b in range(nblocks):
            for j in range(NFC):
                pg_, pu_ = B[(j % 2) * 2], B[(j % 2) * 2 + 1]
                g_, lo_, gw_ = jgrp[j]
                for c in range(8):
                    self.mm(pg_, W1g[g_][:, c, lo_:lo_ + 128], xTv[:, c, :], c == 0, c == 7)
                for c in range(8):
                    self.mm(pu_, W1g[g_][:, c, gw_ + lo_:gw_ + lo_ + 128], xTv[:, c, :], c == 0, c == 7)
                s = sg[j % 2]
                self.act(s, pg_, AF.Silu)
                self.tt(actTv[:, j, :], s, pu_, ALU.mult)
                if b + 1 < nblocks and j in (2, 7, 12, 17):
                    front_elem(b + 1, (j - 2) // 5)
            if b + 1 < nblocks:
                for tt in range(4):
                    front_pe(b + 1, tt)
            for tt in range(4):
                k = b * 4 + tt
                banks2 = (B[4 + (tt % 2) * 2], B[5 + (tt % 2) * 2])
                for hh in range(2):
                    for j in range(NFC):
                        self.mm(banks2[hh], actTv[:, j, tt * 128:(tt + 1) * 128], W2v[:, j, hh * 512:(hh + 1) * 512],
                                j == 0, j == NFC - 1)
                s2 = st2[k % 2]
                xr = xres[k % 2]
                self.dma(xr, src[k * 128:(k + 1) * 128, :], f"xres{k % 2}{tag}")
                o = ot[k % 2]
                self.post_norm_residual(banks2, s2, junk, pg, o, xr, half=True)
                self.dma(dst[k * 128:(k + 1) * 128, :], o, f"st{k % 4}{tag}")


    def wload(self, name, srckey, col0, ncols, sem):
        W = self.sb(name, 8 * ncols, BF16)
        Wv = W.re("p (c f) -> p c f", c=8)
        if srckey in self.WB:
            sv = self.WB[srckey].re("(c p) f -> p c f", p=128)
            for c0 in range(0, 8, 4):
                self.dma(Wv[:, c0:c0 + 4, :], sv[:, c0:c0 + 4, col0:col0 + ncols], sem, eng="sp")
        else:
            sv = self.I[srckey].re("(c p) f -> p c f", p=128)
            for c0 in range(0, 8, 2):
                self.dma(Wv[:, c0:c0 + 2, :], sv[:, c0:c0 + 2, col0:col0 + ncols], sem, eng="pool")
        return Wv

    def convert_weights(self, after=()):
        for key, rows, cols in (("w_in", D, N_IN), ("w_branch_a", D, D), ("w_branch_b", D, D), ("w_out", D, D),
                                ("ffn2_w_in", D, 2 * DFF), ("ffn2_w_out", DFF, D)):
            dst = self.dram_scr("wb_" + key, [rows, cols], BF16)
            dv = dst.re("(c p) f -> p c f", p=128)
            sv = self.I[key].re("(c p) f -> p c f", p=128)
            nchunk = rows // 128
            step = 2
            for c0 in range(0, nchunk, step):
                self.dma(dv[:, c0:c0 + step, :], sv[:, c0:c0 + step, :], "wb_" + key, eng="pool", after=after)
            self.WB[key] = dst

    def aff(self, buf, val, pattern, cm, cmp):
        self.memset(buf, val)
        a = buf.ap
        self.p.op("pool", lambda e: e.affine_select(out=a, in_=a, pattern=pattern, compare_op=cmp, fill=0.0,
                                                   base=0, channel_multiplier=cm), reads=[buf], writes=[buf])

    def pipeline(self, n_items, stages, order=None):
        ns = len(stages)
        for step in range(n_items + ns - 1):
            for si in (order if order is not None else range(ns - 1, -1, -1)):
                i = step - si
                if 0 <= i < n_items:
                    stages[si](i)

    def gla_pass_a(self, H1, S, nseq=SEQ_PER_CORE):
        I, B = self.I, self.banks
        Wg = self.wload("Wg", "w_in", 0, 3104, "Wg")
        gT = self.sb("gTm", 8, F32)
        self.dma(gT, I["mix_pre_g"].re("(c p) -> p c", p=128), "smA", allow_slow_non_contiguous=True)
        up = {"f": self.sb("upf", 512, F32), "b": self.sb("upb", 512, F32)}
        for d in "fb":
            self.memset(up[d][0:64, :], 0.0)
            self.dma(up[d][0:16, :], I["gla_decay_up_" + d], "smA")
            self.dma(up[d][32:33, :], I["gla_decay_bias_" + d].re("(o f) -> o f", o=1), "smA")
        tri = {"f": self.sb("triF", 128, F32), "b": self.sb("triB", 128, F32)}
        self.aff(tri["f"], -1.0 / 16, [[1, 128]], -1, ALU.is_ge)
        self.aff(tri["b"], -1.0 / 16, [[-1, 128]], 1, ALU.is_ge)
        xin = [self.sb(f"xinA{i}", D, F32) for i in range(2)]
        xs = [self.sb(f"xsA{i}", D, BF16) for i in range(4)]
        st = [self.sb(f"stA{i}", 8, F32) for i in range(2)]
        uT = self.sb("uTA", 8 * 512, BF16)
        uTv = uT.re("p (c t) -> p c t", c=8)
        qraw = self.sb("qraw", 4 * 512, F32).re("p (h t) -> p h t", h=4)
        kraw = self.sb("kraw", 4 * 512, F32).re("p (h t) -> p h t", h=4)
        sgr = self.sb("sgrA", 8 * 512, BF16).re("p (c t) -> p c t", c=8)
        z = {"f": self.sb("zf", 512, F32), "b": self.sb("zb", 512, F32)}
        for d in "fb":
            self.memset(z[d][0:64, :], 0.0)
            self.memset(z[d][32:33, :], 1.0)
        vtm = [self.sb(f"vtm{i}", D, BF16) for i in range(4)]
        e1 = [self.sb(f"e1_{i}", 512, F32) for i in range(2)]
        sp = [self.sb(f"sp_{i}", 512, F32) for i in range(2)]
        eb = [self.sb(f"eb_{i}", 512, F32) for i in range(2)]
        enb = [self.sb(f"enb_{i}", 512, F32) for i in range(2)]
        eblb = [self.sb(f"eblb{i}", 4, F32) for i in range(4)]
        qt = [self.sb(f"qt{i}", 512, BF16) for i in range(4)]
        kt = [self.sb(f"kt{i}", 512, BF16) for i in range(4)]
        ktm = [self.sb(f"ktm{i}", 512, BF16) for i in range(4)]
        Sb = self.sb("SbA", 1024, F32)
        Sbb = [self.sb(f"SbbA{i}", 1024, BF16) for i in range(2)]
        UTv = S["UT"].re("(c p) t -> p c t", p=128)
        qscale = 128.0 ** -0.5
        ZB, BTB, TRB, KVB = (B[0], B[1]), (B[2], B[3]), (B[4], B[5]), (B[6], B[7])
        blocks = [(seq, blk) for seq in range(nseq) for blk in range(3, -1, -1)]

        def front_elem(bi, tt):
            seq, blk = blocks[bi]
            r0 = seq * L + blk * 512
            k = bi * 4 + tt
            self.dma(xin[k % 2], H1[r0 + tt * 128:r0 + (tt + 1) * 128, :], f"xinA{k % 2}")
            self.norm_scale(xin[k % 2], xs[k % 4], st[k % 2], None, lnexp=True)

        qraw2 = [qraw, self.sb("qrawB2", 4 * 512, F32).re("p (h t) -> p h t", h=4)]
        kraw2 = [kraw, self.sb("krawB2", 4 * 512, F32).re("p (h t) -> p h t", h=4)]

        def pre_block(bj, part):
            seq_, blk_ = blocks[bj]
            r0_ = seq_ * L + blk_ * 512
            if part == 0:
                for tt in range(4):
                    k = bj * 4 + tt
                    self.transpose_gain(xs[k % 4], B[k % 2], gT, uTv[:, :, tt * 128:(tt + 1) * 128])
                self.dma(UTv[:, :, r0_:r0_ + 512], uTv, "UTst", eng="pool")
            elif part in (1, 2):
                nb_ = 0
                for h in ((0, 1) if part == 1 else (2, 3)):
                    for (off, dstb) in ((OFF_GQ, qraw2[bj % 2]), (OFF_GK, kraw2[bj % 2])):
                        bank = B[nb_ % 2]
                        nb_ += 1
                        for c in range(8):
                            self.mm(bank, Wg[:, c, off + h * 128:off + (h + 1) * 128], uTv[:, c, :], c == 0, c == 7)
                        if off == OFF_GQ:
                            self.act(dstb[:, h, :], bank, AF.Copy)
                        else:
                            self.copy(dstb[:, h, :], bank)
            else:
                for nb_, (d, off) in enumerate((("f", OFF_ZF), ("b", OFF_ZB))):
                    bank = B[nb_ % 2]
                    for c in range(8):
                        self.mm(bank[0:16, :], Wg[:, c, off:off + 16], uTv[:, c, :], c == 0, c == 7)
                    self.copy(z[d][0:16, :], bank[0:16, :])

        for tt in range(4):
            front_elem(0, tt)
        for part in range(4):
            pre_block(0, part)
        for bi, (seq, blk) in enumerate(blocks):
            r0 = seq * L + blk * 512
            has_next = bi + 1 < len(blocks)
            if blk == 3:
                self.memset(Sb, 0.0)
            nb = 0
            for fc in range(8):
                bank = B[4 + fc % 2]
                for c in range(8):
                    self.mm(bank, Wg[:, c, OFF_GR + fc * 128:OFF_GR + (fc + 1) * 128], uTv[:, c, :], c == 0, c == 7)
                self.act(sgr[:, fc, :], bank, AF.Silu)
            for ck in range(4):
                gc = seq * 16 + blk * 4 + ck
                for hh in range(2):
                    bank = B[2 + hh]
                    for c in range(8):
                        self.mm(bank, uTv[:, c, ck * 128:(ck + 1) * 128],
                                Wg[:, c, OFF_GV + hh * 512:OFF_GV + (hh + 1) * 512], c == 0, c == 7)
                    if hh == 0:
                        self.act(vtm[ck][:, 0:512], bank, AF.Copy)
                    else:
                        self.copy(vtm[ck][:, 512:1024], bank)
                self.dma(S["V"][gc], vtm[ck], f"Vst{ck}", eng="pool")
            for ck in range(4):
                gc = seq * 16 + blk * 4 + ck
                self.dma(S["SGR"][gc].re("p (c t) -> p c t", c=8), sgr[:, :, ck * 128:(ck + 1) * 128], "SGRst", eng="pool")
            items = [(ck, d) for ck in range(3, -1, -1) for d in "bf"]

            def s0(i):
                ck, d = items[i]
                self.mm(ZB[i % 2], z[d][0:33, ck * 128:(ck + 1) * 128], up[d][0:33, :], True, True)
                if has_next and i < 4:
                    front_elem(bi + 1, i)

            def s1(i):
                self.act(e1[i % 2], ZB[i % 2], AF.Exp, scale=-1.0)
                self.act(sp[i % 2], e1[i % 2], AF.Ln, bias=1.0)
                if has_next and i == 7:
                    pre_block(bi + 1, 0)

            def s2(i):
                ck, d = items[i]
                for h in range(4):
                    self.mm(BTB[i % 2][:, h * 128:(h + 1) * 128], sp[i % 2][:, h * 128:(h + 1) * 128], tri[d], True, True)
                if has_next and i == 7:
                    pre_block(bi + 1, 1)

            def s3(i):
                ck, d = items[i]
                gc = seq * 16 + blk * 4 + ck
                e_, en_ = eb[i % 2], enb[i % 2]
                self.act(e_, BTB[i % 2], AF.Exp)
                self.act(en_, BTB[i % 2], AF.Exp, scale=-1.0)
                q_, k_ = qt[i % 4], kt[i % 4]
                self.stt(q_.re("p (h t) -> p h t", h=4), qraw2[bi % 2][:, :, ck * 128:(ck + 1) * 128], qscale,
                         e_.re("p (h t) -> p h t", h=4), ALU.mult, ALU.mult)
                self.tt(k_.re("p (h t) -> p h t", h=4), kraw2[bi % 2][:, :, ck * 128:(ck + 1) * 128],
                        en_.re("p (h t) -> p h t", h=4), ALU.mult)
                ev = e_.re("p (h t) -> p h t", h=4)
                if d == "f":
                    self.copy(self.ebl[:, gc * 4:(gc + 1) * 4], ev[:, :, 127])
                else:
                    self.copy(eblb[i % 4], ev[:, :, 0])
                self.dma(S["Q" + d][gc], q_, f"Qst{i % 4}", eng="pool")
                self.dma(S["K" + d][gc], k_, f"Kst{i % 4}", eng="pool")
                if has_next and i == 7:
                    pre_block(bi + 1, 2)

            def s4(i):
                trb = TRB[i % 2].bc(BF16)
                for h in range(4):
                    self.tr(trb[:, h * 128:(h + 1) * 128], kt[i % 4][:, h * 128:(h + 1) * 128], self.ident)

            def s5(i):
                ck, d = items[i]
                gc = seq * 16 + blk * 4 + ck
                km_ = ktm[i % 4]
                self.copy(km_, TRB[i % 2].bc(BF16)[:, 0:512])
                if d == "f":
                    self.dma(S["KFT"][gc], km_, f"KFTst{i % 4}", eng="pool")
                else:
                    sbb = Sbb[(i // 2) % 2]
                    self.act(sbb, Sb, AF.Copy)
                    self.dma(S["SB"][gc], sbb, f"SBst{(i // 2) % 2}", eng="pool")
                    for h in range(4):
                        self.mm(KVB[h // 2][:, (h % 2) * 256:(h % 2 + 1) * 256], km_[:, h * 128:(h + 1) * 128],
                                vtm[ck][:, h * 256:(h + 1) * 256], True, True)
                    for h in range(4):
                        sc = eblb[i % 4][:, h:h + 1]
                        self.ts(Sb[:, h * 256:(h + 1) * 256], Sb[:, h * 256:(h + 1) * 256], sc, ALU.mult)
                        self.stt(Sb[:, h * 256:(h + 1) * 256], KVB[h // 2][:, (h % 2) * 256:(h % 2 + 1) * 256], sc,
                                 Sb[:, h * 256:(h + 1) * 256], ALU.mult, ALU.add)
                if has_next and i == 7:
                    pre_block(bi + 1, 3)

            self.pipeline(len(items), [s0, s1, s2, s3, s4, s5])

    def gla_pass_b(self, S, nseq=SEQ_PER_CORE):
        I, B = self.I, self.banks
        Wa = self.wload("Wa", "w_branch_a", 0, D, "Wa")
        Wga = self.wload("Wga", "w_in", OFF_GA, D, "Wga")
        Wgb = self.wload("WgbB", "w_in", OFF_GB, D, "WgbB")
        SGBb = self.sb("SGBb", 8 * 512, BF16).re("p (c t) -> p c t", c=8)
        SGBv = S["SGB"].re("(c p) t -> p c t", p=128)
        gog = self.sb("gog", 8, F32)
        self.dma(gog, I["gla_out_g"].re("h (e p) -> p (h e)", p=128), "smB", allow_slow_non_contiguous=True)
        maskF = self.sb("maskF", 128, F32)
        maskB = self.sb("maskB", 128, F32)
        self.aff(maskF, 1.0, [[1, 128]], -1, ALU.is_ge)
        self.aff(maskB, 1.0, [[-1, 128]], 1, ALU.is_gt)
        NLD = 3
        names = ("Qf", "Qb", "Kf", "Kb", "KFT")
        ld = {n: [self.sb(f"ld{n}{i}", 512, BF16) for i in range(NLD)] for n in names}
        for n in ("V", "SGR", "SB"):
            ld[n] = [self.sb(f"ld{n}{i}", 1024, BF16) for i in range(NLD)]
        sT = {d: [self.sb(f"sT{d}{i}", 512, BF16) for i in range(2)] for d in "fb"}
        Sf = self.sb("SfB", 1024, F32)
        Sfb = self.sb("SfbB", 1024, BF16)
        sq = [self.sb(f"sqB{i}", 1024, BF16) for i in range(2)]
        oTs = [self.sb(f"oTsB{i}", 1024, F32) for i in range(3)]
        lnv = self.sb("lnvB", 512, F32)
        rstd = self.sb("rstdB", 512, F32)
        t1 = [self.sb(f"t1B{i}", 1024, F32) for i in range(5)]
        mT = [self.sb(f"mTB{i}", 8 * 512, BF16).re("p (c t) -> p c t", c=8) for i in range(2)]
        uT = [self.sb(f"uTB{i}", 8 * 512, BF16).re("p (c t) -> p c t", c=8) for i in range(3)]
        sga = [self.sb(f"sga{i}", 512, F32) for i in range(2)]
        MAb = [self.sb(f"MAb{i}", 8 * 512, BF16).re("p (c t) -> p c t", c=8) for i in range(2)]
        UTv = S["UT"].re("(c p) t -> p c t", p=128)
        MAv = S["MA"].re("(c p) t -> p c t", p=128)
        eps_t = self.eps_t
        SC = (B[0], B[1])
        OB = (B[2], B[3])
        KVB = (B[4], B[5])
        NB = B[6]
        EP = (B[7], B[6])
        nch = nseq * 16

        def T(i):
            return {n: ld[n][i % NLD] for n in ld}

        def p0(i):
            t = T(i)
            if i % 4 == 0:
                blk_r0 = (i // 4) * 512
                self.dma(uT[(i // 4) % 3], UTv[:, :, blk_r0:blk_r0 + 512], f"uTB{(i // 4) % 3}")
            for n in ("Kf", "Qf", "Kb", "Qb", "V", "SB", "KFT", "SGR"):
                self.dma(t[n], S[n][i], f"ldB{n}{i % NLD}")

        def p1(i):
            t = T(i)
            for di, d in enumerate("fb"):
                for h in range(4):
                    self.mm(SC[di][:, h * 128:(h + 1) * 128], t["K" + d][:, h * 128:(h + 1) * 128],
                            t["Q" + d][:, h * 128:(h + 1) * 128], True, True)

        def p2(i):
            for di, d in enumerate("fb"):
                m = (maskF if d == "f" else maskB).re("p (o t) -> p o t", o=1).bcast([128, 4, 128])
                self.tt(sT[d][i % 2].re("p (h t) -> p h t", h=4), SC[di].re("p (h t) -> p h t", h=4), m, ALU.mult)
            self.tt(t1[i % 5].re("p (c t) -> p c t", c=8), T(i)["SGR"].re("p (c t) -> p c t", c=8),
                    gog.re("p (c o) -> p c o", o=1).bcast([128, 8, 128]), ALU.mult)

        def p3(i):
            t = T(i)
            if i % 16 == 0:
                self.memset(Sf, 0.0)
                self.memset(Sfb, 0.0)
            for fc in range(8):
                h = fc // 2
                ob = OB[fc // 4][:, (fc % 4) * 128:(fc % 4 + 1) * 128]
                vcol = t["V"][:, fc * 128:(fc + 1) * 128]
                self.mm(ob, vcol, sT["f"][i % 2][:, h * 128:(h + 1) * 128], True, False)
                self.mm(ob, vcol, sT["b"][i % 2][:, h * 128:(h + 1) * 128], False, False)
                self.mm(ob, Sfb[:, fc * 128:(fc + 1) * 128], t["Qf"][:, h * 128:(h + 1) * 128], False, False)
                self.mm(ob, t["SB"][:, fc * 128:(fc + 1) * 128], t["Qb"][:, h * 128:(h + 1) * 128], False, True)
            for h in range(4):
                self.mm(KVB[h // 2][:, (h % 2) * 256:(h % 2 + 1) * 256], t["KFT"][:, h * 128:(h + 1) * 128],
                        t["V"][:, h * 256:(h + 1) * 256], True, True)

        def p4(i):
            for hb in range(2):
                self.act(sq[i % 2][:, hb * 512:(hb + 1) * 512], OB[hb], AF.Square)
                self.act(oTs[i % 3][:, hb * 512:(hb + 1) * 512], OB[hb], AF.Copy)
            for h in range(4):
                sc = self.ebl[:, i * 4 + h:i * 4 + h + 1]
                self.ts(Sf[:, h * 256:(h + 1) * 256], Sf[:, h * 256:(h + 1) * 256], sc, ALU.mult)
                self.stt(Sf[:, h * 256:(h + 1) * 256], KVB[h // 2][:, (h % 2) * 256:(h % 2 + 1) * 256], sc,
                         Sf[:, h * 256:(h + 1) * 256], ALU.mult, ALU.add)
            self.copy(Sfb, Sf)

        def p5(i):
            for h in range(4):
                nbk = NB[:, h * 128:(h + 1) * 128]
                self.mm(nbk, self.ones_bf, sq[i % 2][:, (2 * h) * 128:(2 * h + 1) * 128], True, False)
                self.mm(nbk, self.ones_bf, sq[i % 2][:, (2 * h + 1) * 128:(2 * h + 2) * 128], False, True)

        def p6(i):
            ck = i % 4
            blk = i // 4
            self.act(lnv, NB, AF.Ln, scale=1.0 / 256, bias=eps_t)
            self.act(rstd, lnv, AF.Exp, scale=-0.5)
            tv = t1[i % 5].re("p (h e t) -> p h e t", h=4, e=2)
            self.tt(tv, tv, rstd.re("p (h o t) -> p h o t", h=4, o=1).bcast([128, 4, 2, 128]), ALU.mult)
            m_ = mT[blk % 2]
            self.tt(m_[:, :, ck * 128:(ck + 1) * 128], oTs[i % 3].re("p (c t) -> p c t", c=8),
                    t1[i % 5].re("p (c t) -> p c t", c=8), ALU.mult)
            if ck == 3:
                push_block(blk)
            drain(6)

        epi_q = []
        epc = {"n": 0}

        def nextbank():
            epc["n"] += 1
            return EP[epc["n"] % 2]

        sg_e0 = self.sb("sgeB0", 512, F32)
        sg_e = [sg_e0, sg_e0]

        def sigm(dst, src):
            e_ = sg_e[epc["n"] % 2]
            self.act(e_, src, AF.Exp, scale=-1.0)
            self.act(e_, e_, AF.Ln, bias=1.0)
            self.act(dst, e_, AF.Exp, scale=-1.0)

        def push_block(blk):
            r0 = blk * 512
            u_, m_, MA_ = uT[blk % 3], mT[blk % 2], MAb[blk % 2]

            def ga(o):
                def f():
                    bg = nextbank()
                    for c in range(8):
                        self.mm(bg, Wga[:, c, o * 128:(o + 1) * 128], u_[:, c, :], c == 0, c == 7)
                    sigm(sga[o % 2], bg)
                return f

            def ya(o):
                def f():
                    by = nextbank()
                    for c in range(8):
                        self.mm(by, Wa[:, c, o * 128:(o + 1) * 128], m_[:, c, :], c == 0, c == 7)
                    self.tt(MA_[:, o, :], by, sga[o % 2], ALU.mult)
                    if o == 7:
                        self.dma(MAv[:, :, r0:r0 + 512], MA_, f"MAst{blk % 2}", eng="pool")
                return f

            def gb(o):
                def f():
                    bg = nextbank()
                    for c in range(8):
                        self.mm(bg, Wgb[:, c, o * 128:(o + 1) * 128], u_[:, c, :], c == 0, c == 7)
                    sigm(SGBb[:, o, :], bg)
                    if o == 7:
                        self.dma(SGBv[:, :, r0:r0 + 512], SGBb, "SGBst", eng="pool")
                return f

            for o in range(8):
                epi_q.extend([ga(o), ya(o), gb(o)])

        def drain(n):
            for _ in range(n):
                if epi_q:
                    epi_q.pop(0)()

        self.pipeline(nch, [p0, p1, p2, p3, p4, p5, p6], order=(4, 6, 5, 3, 2, 1, 0))
        drain(10 ** 6)

    def rope_chain(self, src, W, gbc, Ct, St, tile, dst, T):
        nh = W // 128
        ta, ss, lnv, rs, qn, tb = T
        self.act(ta[:, 0:W], src, AF.Square)
        self.reduce(ss[:, 0:nh], ta[:, 0:W].re("p (h d) -> p h d", h=nh))
        self.act(lnv[:, 0:nh], ss[:, 0:nh], AF.Ln, scale=1.0 / 128, bias=self.eps_t)
        self.act(rs[:, 0:nh], lnv[:, 0:nh], AF.Exp, scale=-0.5)
        qv = qn[:, 0:W].re("p (h d) -> p h d", h=nh)
        self.tt(qv, src.re("p (h d) -> p h d", h=nh),
                rs[:, 0:nh].re("p (h o) -> p h o", o=1).bcast([128, nh, 128]), ALU.mult)
        self.tt(qv, qv, gbc.re("p (o d) -> p o d", o=1).bcast([128, nh, 128]), ALU.mult)
        self.tt(ta[:, 0:W].re("p (h d) -> p h d", h=nh), qv,
                Ct[:, tile, :].re("p (o d) -> p o d", o=1).bcast([128, nh, 128]), ALU.mult)
        q5 = qn[:, 0:W].re("p (h a j e) -> p h a j e", h=nh, a=2, j=2)
        t5 = tb[:, 0:W].re("p (h a j e) -> p h a j e", h=nh, a=2, j=2)
        S5 = St[:, tile, :].re("p (o a j e) -> p o a j e", o=1, a=2, j=2)
        for j in range(2):
            self.tt(t5[:, :, :, j, :], q5[:, :, :, 1 - j, :], S5[:, :, :, j, :].bcast([128, nh, 2, 32]), ALU.mult)
        self.tt(dst, ta[:, 0:W], tb[:, 0:W], ALU.add)

    def attn_phase(self, H1, H2, S, nseq=SEQ_PER_CORE):
        I, B = self.I, self.banks
        Wq = self.wload("Wq", "w_in", OFF_AQ, D, "Wq")
        Wkv = self.wload("Wkv", "w_in", OFF_AK, 512, "Wkv")
        Wb = self.wload("Wb", "w_branch_b", 0, D, "Wb")
        Wo = self.wload("Wo", "w_out", 0, D, "Wo")
        pg = self.sb("pgM", D, F32)
        self.dma(pg, Buf(I["mix_post_g"].ap.partition_broadcast(128), I["mix_post_g"].t), "smC")
        Ct = self.sb("ropeC", 16 * 128, F32).re("p (n d) -> p n d", n=16)
        St = self.sb("ropeS", 16 * 128, F32).re("p (n d) -> p n d", n=16)
        rv = I["rope"].re("(n p) d -> p n d", p=128)
        self.dma(Ct, rv[:, :, 0:128], "smC")
        self.dma(St, rv[:, :, 128:256], "smC")
        gq = self.sb("g_q", 128, F32)
        gk = self.sb("g_k", 128, F32)
        self.dma(gq, Buf(I["att_q_norm_g"].ap.partition_broadcast(128), I["att_q_norm_g"].t), "smC")
        self.dma(gk, Buf(I["att_k_norm_g"].ap.partition_broadcast(128), I["att_k_norm_g"].t), "smC")
        KT = self.sb("KT", 2 * L, BF16).re("p (h t) -> p h t", h=2)
        Vt = self.sb("Vt", 16 * 256, BF16).re("p (n d) -> p n d", n=16)
        uT = [self.sb(f"uTC{i}", 8 * 512, BF16).re("p (c t) -> p c t", c=8) for i in range(2)]
        Ts = []
        for i in range(3):
            tb_ = self.sb(f"tbC{i}", 512, F32)
            Ts.append((self.sb(f"taC{i}", 512, F32), self.sb(f"ssC{i}", 4, F32), self.sb(f"lnvC{i}", 4, F32),
                       self.sb(f"rsC{i}", 4, F32), self.sb(f"qnC{i}", 512, F32), tb_, tb_))
        qr = [self.sb(f"qrC{i}", D, BF16) for i in range(2)]
        kr = [self.sb(f"krC{i}", 256, BF16) for i in range(3)]
        qT = [self.sb(f"qTC{i}", 8 * 512, BF16).re("p (h t) -> p h t", h=8) for i in range(2)]
        obT = [self.sb(f"obT{i}", 8 * 512, BF16).re("p (h t) -> p h t", h=8) for i in range(2)]
        PT = [self.sb(f"PT{i}", 512, BF16) for i in range(4)]
        rden = self.sb("rdenC", 512, F32)
        lden = self.sb("ldenC", 512, F32)
        if DEN_QUADS:
            ots = [self.sb(f"otsC{i}", 512, F32) for i in range(2)]
            PA = [self.sb(f"PAC{i}", 512, BF16) for i in range(2)]
            PQ = [self.sb(f"PQC{i}", 512, BF16) for i in range(3)]
        else:
            ots = [self.sb(f"otsC{i}", 512, F32) for i in range(2)]
        t2 = self.sb("t2C0", 512, F32)
        MAl = self.sb("MAl", 8 * 512, BF16).re("p (c t) -> p c t", c=8)
        SGl = self.sb("SGl", 8 * 512, BF16).re("p (c t) -> p c t", c=8)
        junk = self.sb("junkC", 512, BF16)
        st2 = [self.sb(f"st2C{i}", 8, F32) for i in range(2)]
        xres = self.sb("xresC0", D, F32)
        ot = self.sb("otC0", D, F32)
        fs = ot
        UTv = S["UT"].re("(c p) t -> p c t", p=128)
        MAv = S["MA"].re("(c p) t -> p c t", p=128)
        SGv = S["SGB"].re("(c p) t -> p c t", p=128)
        sm_scale = 128.0 ** -0.5
        SIDE = [B[0], B[1], B[2]]
        OT, DEN = B[3], B[4]
        STB = [B[5], B[6], B[7]]
        cnt = {"side": 0, "chain": 0, "u": 0}

        def side_bank():
            cnt["side"] += 1
            return SIDE[cnt["side"] % 3]

        def chain_T():
            cnt["chain"] += 1
            return Ts[cnt["chain"] % 3][:6]

        def rope_stages(src_fn, W, gbc, tile_fn, dst_fn):
            nh = W // 128
            stt_ = {}

            def s1():
                T_ = Ts[cnt["chain"] % 3]
                cnt["chain"] += 1
                stt_["T"] = T_
                ta, ss, lnv, rs, qn, tb, qs = T_
                src = src_fn()
                self.act(ta[:, 0:W], src, AF.Square)
                self.act(qs[:, 0:W], src, AF.Copy)

            def s2():
                ta, ss, lnv, rs, qn, tb, qs = stt_["T"]
                self.reduce(ss[:, 0:nh], ta[:, 0:W].re("p (h d) -> p h d", h=nh))

            def s3():
                ta, ss, lnv, rs, qn, tb, qs = stt_["T"]
                self.act(lnv[:, 0:nh], ss[:, 0:nh], AF.Ln, scale=1.0 / 128, bias=self.eps_t)
                self.act(rs[:, 0:nh], lnv[:, 0:nh], AF.Exp, scale=-0.5)

            def s4():
                ta, ss, lnv, rs, qn, tb, qs = stt_["T"]
                tile = tile_fn()
                qv = qn[:, 0:W].re("p (h d) -> p h d", h=nh)
                self.tt(qv, qs[:, 0:W].re("p (h d) -> p h d", h=nh),
                        rs[:, 0:nh].re("p (h o) -> p h o", o=1).bcast([128, nh, 128]), ALU.mult)
                self.tt(qv, qv, gbc.re("p (o d) -> p o d", o=1).bcast([128, nh, 128]), ALU.mult)
                self.tt(ta[:, 0:W].re("p (h d) -> p h d", h=nh), qv,
                        Ct[:, tile, :].re("p (o d) -> p o d", o=1).bcast([128, nh, 128]), ALU.mult)

            def s5():
                ta, ss, lnv, rs, qn, tb, qs = stt_["T"]
                tile = tile_fn()
                q5 = qn[:, 0:W].re("p (h a j e) -> p h a j e", h=nh, a=2, j=2)
                t5 = tb[:, 0:W].re("p (h a j e) -> p h a j e", h=nh, a=2, j=2)
                S5 = St[:, tile, :].re("p (o a j e) -> p o a j e", o=1, a=2, j=2)
                for j in range(2):
                    self.tt(t5[:, :, :, j, :], q5[:, :, :, 1 - j, :], S5[:, :, :, j, :].bcast([128, nh, 2, 32]), ALU.mult)
                self.tt(dst_fn(), ta[:, 0:W], tb[:, 0:W], ALU.add)

            return [s1, s2, s3, s4, s5]

        def q_items(seq, blk, par):
            items = []
            st = {}

            def qp(tt, hh):
                def s0():
                    u = uT[1]
                    if tt == 0 and hh == 0:
                        self.dma(u, UTv[:, :, seq * L + blk * 512:seq * L + (blk + 1) * 512], "uTC1")
                    bank = side_bank()
                    st[("b", tt, hh)] = bank
                    for c in range(8):
                        self.mm(bank, u[:, c, tt * 128:(tt + 1) * 128], Wq[:, c, hh * 512:(hh + 1) * 512], c == 0, c == 7)
                return [s0] + rope_stages(lambda: st[("b", tt, hh)], 512, gq, lambda: blk * 4 + tt,
                                          lambda: qr[tt % 2][:, hh * 512:(hh + 1) * 512])

            def qt(tt):
                def s0():
                    trb = side_bank().bc(BF16)
                    st[("t", tt)] = trb
                    for h in range(8):
                        self.tr(trb[:, h * 128:(h + 1) * 128], qr[tt % 2][:, h * 128:(h + 1) * 128], self.ident)

                def s1():
                    self.copy(qT[par][:, :, tt * 128:(tt + 1) * 128], st[("t", tt)].re("p (h t) -> p h t", h=8))
                return [s0, s1]

            for tt in range(4):
                if tt >= 1:
                    items.append((qt(tt - 1), [8]))
                items.append((qp(tt, 0), []))
                items.append((qp(tt, 1), []))
            items.append((qt(3), [8]))
            return items

        def epi_items(seq, blk, par):
            items = []
            r0 = seq * L + blk * 512
            st = {}

            def yb(o):
                def s0():
                    if o == 0:
                        self.dma(MAl, MAv[:, :, r0:r0 + 512], "MAld")
                        self.dma(SGl, SGv[:, :, r0:r0 + 512], "SGld")
                    by = side_bank()
                    st[("y", o)] = by
                    for h in range(8):
                        self.mm(by, Wb[:, h, o * 128:(o + 1) * 128], obT[par][:, h, :], h == 0, h == 7)

                def s1():
                    self.tt(t2, st[("y", o)], SGl[:, o, :], ALU.mult)
                    self.tt(MAl[:, o, :], t2, MAl[:, o, :], ALU.add)
                return [s0, s1]

            def op(tt):
                row = r0 + tt * 128
                s2_ = st2[tt % 2]

                def s0():
                    banks2 = (side_bank(), side_bank())
                    st[("o", tt)] = banks2
                    for hh in range(2):
                        for o in range(8):
                            self.mm(banks2[hh], MAl[:, o, tt * 128:(tt + 1) * 128], Wo[:, o, hh * 512:(hh + 1) * 512], o == 0, o == 7)
                    self.dma(xres, H1[row:row + 128, :], "xresC0")

                def s1():
                    banks2 = st[("o", tt)]
                    for hh in range(2):
                        self.act(junk[:, 0:512], banks2[hh], AF.Square, accum=s2_[:, hh:hh + 1])
                        self.act(fs[:, hh * 512:(hh + 1) * 512], banks2[hh], AF.Copy)

                def s2():
                    self.tt(s2_[:, 2:3], s2_[:, 0:1], s2_[:, 1:2], ALU.add)

                def s3():
                    self.act(s2_[:, 3:4], s2_[:, 2:3], AF.Ln, scale=1.0 / D, bias=self.eps_t)
                    self.act(s2_[:, 4:5], s2_[:, 3:4], AF.Exp, scale=-0.5)

                def s4():
                    for hh in range(2):
                        self.stt(ot[:, hh * 512:(hh + 1) * 512], fs[:, hh * 512:(hh + 1) * 512], s2_[:, 4:5],
                                 pg[:, hh * 512:(hh + 1) * 512], ALU.mult, ALU.mult)
                    self.tt(ot, ot, xres, ALU.add)
                    self.dma(H2[row:row + 128, :], ot, "stC0", eng="pool")
                return [s0, s1, s2, s3, s4]

            for o in range(8):
                items.append((yb(o), []))
            for tt in range(4):
                items.append((op(tt), [6]))
            return items

        def kv_items(seq):
            items = []
            st = {}

            def kvp(t):
                blk, tt = divmod(t, 4)

                def s0():
                    u = uT[0]
                    if tt == 0:
                        self.dma(u, UTv[:, :, seq * L + blk * 512:seq * L + (blk + 1) * 512], "uTC0")
                    bank = side_bank()
                    st[("b", t)] = bank
                    for c in range(8):
                        self.mm(bank, u[:, c, tt * 128:(tt + 1) * 128], Wkv[:, c, :], c == 0, c == 7)

                def sv():
                    self.act(Vt[:, t, :], st[("b", t)][:, 256:512], AF.Copy)

                rs_ = rope_stages(lambda: st[("b", t)][:, 0:256], 256, gk, lambda: t, lambda: kr[t % 3])

                def s1():
                    sv()
                    rs_[0]()

                def s6():
                    trb = side_bank().bc(BF16)
                    st[("t", t)] = trb
                    for hk in range(2):
                        self.tr(trb[:, hk * 128:(hk + 1) * 128], kr[t % 3][:, hk * 128:(hk + 1) * 128], self.ident)

                def s7():
                    self.copy(KT[:, :, t * 128:(t + 1) * 128], st[("t", t)][:, 0:256].re("p (h t) -> p h t", h=2))
                return [s0, s1] + rs_[1:] + [s6, s7]

            for t in range(16):
                items.append((kvp(t), []))
            return items

        GAP = 2

        def plan_side(items, nslots, spacing):
            plan = {}
            start_prev, ends = -spacing, []
            for stages, deps in items:
                start = start_prev + spacing
                if deps and ends:
                    start = max(start, max(ends) + deps[0])
                for k, f in enumerate(stages):
                    plan.setdefault(start + GAP * k, []).append(f)
                ends.append(start + GAP * (len(stages) - 1))
                start_prev = start
            return plan

        def run_streams_standalone(streams):
            plan = {}
            for lst in streams:
                for sl, fs_ in plan_side(lst, 0, 4).items():
                    plan.setdefault(sl, []).extend(fs_)
            for sl in sorted(plan):
                for f in plan[sl]:
                    f()

        def head_loop(par, side):
            q_ = qT[par]
            ob = obT[par]
            its = [(h, kt) for h in range(8) for kt in range(16)]
            n = len(its)
            plan = {}
            for lst in side:
                for sl, fs_ in plan_side(lst, n, 4).items():
                    plan.setdefault(sl, []).extend(fs_)

            def issue_st(i):
                h, kt = its[i]
                self.mm(STB[i % 3], KT[:, h // 4, kt * 128:(kt + 1) * 128], q_[:, h, :], True, True)

            issue_st(0)
            issue_st(1)
            pend = []
            DLAG = 3

            def run_due(i):
                while pend and pend[0][0] <= i:
                    pend.pop(0)[1]()

            for i in range(n):
                h, kt = its[i]
                hk = h // 4
                pt = PT[i % 4]
                self.act(pt, STB[i % 3], AF.Exp, scale=sm_scale)
                if i + 2 < n:
                    issue_st(i + 2)
                self.mm(OT, Vt[:, kt, hk * 128:(hk + 1) * 128], pt, kt == 0, kt == 15)
                if not DEN_QUADS:
                    self.mm(DEN, self.ones_bf, pt, kt == 0, kt == 15)
                run_due(i)
                if DEN_QUADS:
                    if kt % 4 == 1:
                        self.tt(PA[0], PT[(i - 1) % 4], pt, ALU.add)
                    elif kt % 4 == 3:
                        self.tt(PA[1], PT[(i - 1) % 4], pt, ALU.add)
                        pq = PQ[(i // 4) % 3]
                        self.tt(pq, PA[0], PA[1], ALU.add)

                        def den_mm(g_=kt // 4, pq_=pq):
                            self.mm(DEN, self.ones_bf, pq_, g_ == 0, g_ == 3)
                        pend.append((i + DLAG, den_mm))
                if kt == 15:
                    ots_ = ots[h % 2]
                    self.copy(ots_, OT)
                    self.act(lden, DEN, AF.Ln)

                    def fin(h_=h, ots__=ots_):
                        self.act(rden, lden, AF.Exp, scale=-1.0)
                        self.tt(ob[:, h_, :], ots__, rden, ALU.mult)
                    pend.append((i + (DLAG if DEN_QUADS else 1), fin))
                for f in plan.get(i, ()):
                    f()
            run_due(n + DLAG)
            for sl in sorted(k for k in plan if k >= n):
                for f in plan[sl]:
                    f()

        for seq in range(nseq):
            kv_, q_ = kv_items(seq), q_items(seq, 0, 0)
            merged = []
            while kv_ or q_:
                if kv_:
                    merged.append(kv_.pop(0))
                if q_:
                    merged.append(q_.pop(0))
            run_streams_standalone([merged])
            for blk in range(4):
                par = blk % 2
                side = []
                if blk > 0:
                    side.append(epi_items(seq, blk - 1, 1 - par))
                if blk < 3:
                    side.append(q_items(seq, blk + 1, 1 - par))
                head_loop(par, side)
            run_streams_standalone([epi_items(seq, 3, 1)])

    def build(self, phases=("ffn1", "glaA", "glaB", "attn", "ffn2"), nblocks=None, nseq=SEQ_PER_CORE):
        self.nseq = nseq
        nc = self.nc
        I = {}
        I["x"] = self.dram_in("x", [NTOK, D])
        for n, shp in (("ffn1_pre_g", [D]), ("ffn1_w_in", [D, 2 * DFF]), ("ffn1_w_out", [DFF, D]),
                       ("ffn1_post_g", [D]), ("mix_pre_g", [D]), ("w_in", [D, N_IN]),
                       ("gla_decay_up_f", [16, 512]), ("gla_decay_bias_f", [512]),
                       ("gla_decay_up_b", [16, 512]), ("gla_decay_bias_b", [512]),
                       ("gla_out_g", [4, 256]), ("w_branch_a", [D, D]), ("att_q_norm_g", [128]),
                       ("att_k_norm_g", [128]), ("w_branch_b", [D, D]), ("w_out", [D, D]),
                       ("mix_post_g", [D]), ("ffn2_pre_g", [D]), ("ffn2_w_in", [D, 2 * DFF]),
                       ("ffn2_w_out", [DFF, D]), ("ffn2_post_g", [D])):
            I[n] = self.dram_in(n, shp)
        I["rope"] = self.dram_in("rope", [L, 256])
        out = self.dram_out("out", [NTOK, D])
        H1 = self.dram_scr("H1", [NTOK, D], F32)
        H2 = self.dram_scr("H2", [NTOK, D], F32)
        self.I = I
        self.setup()
        self.eps_t = self.sb("eps_t", 1, F32)
        self.memset(self.eps_t, EPS)
        self.eps4_t = self.sb("eps4_t", 1, F32)
        self.memset(self.eps4_t, 4 * EPS)
        self.arena_floor = self.arena_off
        nb = nblocks or NTOK // 512
        self.ebl = self.sb("ebl", 128, F32)
        self.arena_floor = self.arena_off
        S = {}
        for n in ("Qf", "Qb", "Kf", "Kb", "KFT"):
            S[n] = self.dram_scr("scr_" + n, [32, 128, 512], BF16)
        for n in ("V", "SGR", "SB"):
            S[n] = self.dram_scr("scr_" + n, [32, 128, 1024], BF16)
        S["UT"] = self.dram_scr("scr_UT", [D, NTOK], BF16)
        S["MA"] = self.dram_scr("scr_MA", [D, NTOK], BF16)
        S["SGB"] = self.dram_scr("scr_SGB", [D, NTOK], BF16)
        self.S = S
        self.WB = {}
        last = phases[-1]
        if "ffn1" in phases:
            self.ffn_phase(I["x"], out if last == "ffn1" else H1, I["ffn1_pre_g"], ("ffn1_w_in", "ffn1_w_out"),
                           I["ffn1_post_g"], "a", nblocks=nb,
                           after_weights=self.convert_weights if len(phases) > 1 else None)
            self.phase_reset()
        src_mix = H1 if "ffn1" in phases else I["x"]
        nseq = self.nseq
        if "glaA" in phases:
            self.gla_pass_a(src_mix, S, nseq)
            self.phase_reset()
        if "glaB" in phases:
            self.gla_pass_b(S, nseq)
            self.phase_reset()
        if "attn" in phases:
            self.attn_phase(src_mix, out if last == "attn" else H2, S, nseq)
            self.phase_reset()
        if "ffn2" in phases:
            self.ffn_phase(H2, out, I["ffn2_pre_g"], ("ffn2_w_in", "ffn2_w_out"), I["ffn2_post_g"], "b", nblocks=nb)
            self.phase_reset()
        self.p.final_wait()
        self.p.emit(nc, self.es)
        self.es.close()
        return nc


_ROPE = None


def rope_tables():
    global _ROPE
    if _ROPE is None:
        half = 64
        inv = 10000.0 ** (-np.arange(0, half, 2, dtype=np.float32) / half)
        t = np.arange(L)
        row = (t // 64).astype(np.float32)[:, None] * inv[None, :]
        col = (t % 64).astype(np.float32)[:, None] * inv[None, :]
        cr, sr, cc, sc = np.cos(row), np.sin(row), np.cos(col), np.sin(col)
        C = np.concatenate([cr, cr, cc, cc], axis=1)
        S = np.concatenate([-sr, sr, -sc, sc], axis=1)
        _ROPE = np.ascontiguousarray(np.concatenate([C, S], axis=1).astype(np.float32))
    return _ROPE


def make_in_maps(inputs):
    x = np.ascontiguousarray(np.asarray(inputs["x"], dtype=np.float32))
    maps = []
    shared = {}
    for k, v in inputs.items():
        if k == "x":
            continue
        a = np.asarray(v, dtype=np.float32)
        shared[k] = np.ascontiguousarray(a.reshape(a.shape[1:]))
    shared["rope"] = rope_tables()
    for c in range(NCORES):
        m = dict(shared)
        m["x"] = x[c * SEQ_PER_CORE:(c + 1) * SEQ_PER_CORE].reshape(NTOK, D)
        maps.append(m)
    return maps


_NC = None


def kernel(**inputs):
    global _NC
    if _NC is None:
        _NC = Builder().build()
    maps = make_in_maps(inputs)
    res = run_bass_kernel_spmd(_NC, maps, core_ids=list(range(NCORES)))
    outs = [np.asarray(r["out"]).reshape(SEQ_PER_CORE, L, D) for r in res.results]
    return np.concatenate(outs, axis=0).astype(np.float32)
```

```python
from contextlib import ExitStack

import numpy as np
import concourse.bass as bass
import concourse.mybir as mybir
from concourse.bass_utils import run_bass_kernel_spmd

F32 = mybir.dt.float32
BF16 = mybir.dt.bfloat16
AF = mybir.ActivationFunctionType
ALU = mybir.AluOpType
AX = mybir.AxisListType

NCORES = 8
DEN_QUADS = False
D = 1024
L = 2048
SEQ_PER_CORE = 2
NTOK = L * SEQ_PER_CORE
DFF = 2816
NFC = DFF // 128
EPS = 1e-6
N_IN = 6688
OFF_GQ, OFF_GK, OFF_GV, OFF_GR, OFF_ZF, OFF_ZB = 0, 512, 1024, 2048, 3072, 3088
OFF_AQ, OFF_AK, OFF_AV, OFF_GA, OFF_GB = 3104, 4128, 4384, 4640, 5664

ENGS = ("pe", "act", "dve", "pool", "sp")


class Tile:
    __slots__ = ("name", "w", "r", "rdma")

    def __init__(self, name):
        self.name = name
        self.w = None
        self.r = {}
        self.rdma = {}


class Buf:
    def __init__(self, ap, t):
        self.ap = ap
        self.t = t

    def __getitem__(self, k):
        return Buf(self.ap[k], self.t)

    def bc(self, dt):
        return Buf(self.ap.bitcast(dt), self.t)

    def re(self, pat, **kw):
        return Buf(self.ap.rearrange(pat, **kw), self.t)

    def bcast(self, shape):
        return Buf(self.ap.to_broadcast(list(shape)), self.t)


class OpRec:
    __slots__ = ("eng", "idx", "sig", "waits", "fn", "dma_inc", "rank")

    def __init__(self, eng, idx, fn):
        self.eng = eng
        self.idx = idx
        self.sig = False
        self.waits = []
        self.fn = fn
        self.dma_inc = None
        self.rank = 0


class Prog:
    def __init__(self):
        self.q = {e: [] for e in ENGS}
        self.lastc = {e: None for e in ENGS}
        self.seen = {e: {} for e in ENGS}
        self.dma_cnt = {}
        self.dma_owner = {}
        self.nbar = 0
        self.semmap = {}
        self.free_phys = []
        self.nphys = 0

    def _need(self, eng, need, marker, same_eng_ok):
        if marker is None:
            return
        if marker[0] == "op":
            op = marker[1]
            if op.eng == eng and (same_eng_ok or eng == "pe"):
                return
            s, v = "e_" + op.eng, op.idx
            cur = need.get(s)
            if cur is None or cur[0] < v:
                need[s] = (v, op)
        else:
            s, v = marker[1], marker[2]
            cur = need.get(s)
            if cur is None or cur[0] < v:
                need[s] = (v, None)

    def _commit(self, eng, need):
        out = []
        seen = self.seen[eng]
        for s, (v, op) in need.items():
            if seen.get(s, -1) < v:
                seen[s] = v
                if op is not None:
                    op.sig = True
                    out.append((s, op))
                else:
                    out.append((s, v))
        return out

    def _deps(self, eng, reads, writes):
        need = {}
        for t in reads:
            self._need(eng, need, t.w, False)
        for t in writes:
            self._need(eng, need, t.w, True)
            for op in t.r.values():
                self._need(eng, need, ("op", op), True)
            for s, v in t.rdma.items():
                self._need(eng, need, ("dma", s, v), True)
        return self._commit(eng, need)

    def op(self, eng, fn, reads=(), writes=()):
        rt = [b.t for b in reads]
        wt = [b.t for b in writes]
        waits = self._deps(eng, rt, wt)
        rec = OpRec(eng, len(self.q[eng]), fn)
        rec.waits = waits
        self.q[eng].append(rec)
        self.lastc[eng] = rec
        for t in rt:
            t.r[eng] = rec
        for t in wt:
            t.w = ("op", rec)
            t.r = {}
            t.rdma = {}
        return rec

    def dma(self, eng, out, in_, sem, after=(), **kw):
        if sem not in self.semmap:
            if self.free_phys:
                self.semmap[sem] = self.free_phys.pop(0)
            else:
                self.semmap[sem] = f"d{self.nphys}"
                self.nphys += 1
        sem = self.semmap[sem]
        rt, wt = [in_.t], [out.t]
        need = {}
        self._need(eng, need, in_.t.w, False)
        for b_ in after:
            self._need(eng, need, b_.t.w, False)
        self._need(eng, need, out.t.w, True)
        for op in out.t.r.values():
            self._need(eng, need, ("op", op), True)
        for s_, v_ in out.t.rdma.items():
            self._need(eng, need, ("dma", s_, v_), True)
        cur = self.dma_cnt.get(sem, 0)
        if self.dma_owner.get(sem) is not out.t and cur > 0:
            self._need(eng, need, ("dma", sem, cur), True)
        waits = self._commit(eng, need)
        self.dma_owner[sem] = out.t
        o_ap, i_ap = out.ap, in_.ap
        rec = OpRec(eng, len(self.q[eng]), lambda e: e.dma_start(out=o_ap, in_=i_ap, **kw))
        rec.waits = waits
        self.dma_cnt[sem] = cur + 16
        rec.dma_inc = sem
        self.q[eng].append(rec)
        in_.t.rdma[sem] = cur + 16
        out.t.w = ("dma", sem, cur + 16)
        out.t.r = {}
        out.t.rdma = {}
        return rec

    def _all_done_waits(self, eng):
        need = {}
        for e in ("pe", "act", "dve", "pool"):
            if self.lastc[e] is not None:
                self._need(eng, need, ("op", self.lastc[e]), False)
        for s, v in self.dma_cnt.items():
            self._need(eng, need, ("dma", s, v), False)
        return self._commit(eng, need)

    def barrier(self):
        waits = self._all_done_waits("sp")
        self.free_phys = sorted(set(self.free_phys) | set(self.semmap.values()), key=lambda n: int(n[1:]))
        self.semmap = {}
        self.nbar += 1
        nb = self.nbar
        rec = OpRec("sp", len(self.q["sp"]), ("bar", nb))
        rec.waits = waits
        self.q["sp"].append(rec)
        for e in ("pe", "act", "dve", "pool"):
            r2 = OpRec(e, len(self.q[e]), None)
            r2.waits = [("bar", nb)]
            self.q[e].append(r2)
            for s, v in self.seen["sp"].items():
                if self.seen[e].get(s, -1) < v:
                    self.seen[e][s] = v

    def final_wait(self):
        rec = OpRec("sp", len(self.q["sp"]), None)
        rec.waits = self._all_done_waits("sp")
        self.q["sp"].append(rec)

    def emit(self, nc, es):
        semnames = set(["bar"])
        for e in ENGS:
            semnames.add("e_" + e)
            k = 0
            for rec in self.q[e]:
                if rec.sig:
                    k += 1
                    rec.rank = k
                for s, _ in rec.waits:
                    semnames.add(s)
                if rec.dma_inc:
                    semnames.add(rec.dma_inc)
        sems = {s: es.enter_context(nc.semaphore(s)) for s in sorted(semnames)}
        block = es.enter_context(nc.Block())
        engmap = {"pe": block.tensor, "act": block.scalar, "dve": block.vector,
                  "pool": block.gpsimd, "sp": block.sync}

        def mk(ename):
            def body(eng):
                for rec in self.q[ename]:
                    for s, v in rec.waits:
                        eng.wait_ge(sems[s], v.rank if isinstance(v, OpRec) else v)
                    if rec.fn is None:
                        continue
                    if isinstance(rec.fn, tuple):
                        eng.sem_inc(sems["bar"], 1)
                        continue
                    ins = rec.fn(eng)
                    if rec.dma_inc:
                        ins.then_inc(sems[rec.dma_inc], 16)
                    elif rec.sig:
                        ins.then_inc(sems["e_" + ename], 1)
            return body

        for ename in ENGS:
            engmap[ename](mk(ename))


class Builder:
    def __init__(self, debug=()):
        self.debug = set(debug)
        self.nc = bass.Bass("TRN2", target_bir_lowering=False)
        self.p = Prog()
        self.es = ExitStack()
        self.arena_off = 0
        self.arena_floor = 0

    def dram_in(self, name, shape, dt=F32):
        h = self.nc.dram_tensor(name, list(shape), dt, kind="ExternalInput")
        return Buf(h.ap(), Tile(name))

    def dram_out(self, name, shape, dt=F32):
        h = self.nc.dram_tensor(name, list(shape), dt, kind="ExternalOutput")
        return Buf(h.ap(), Tile(name))

    def dram_scr(self, name, shape, dt):
        kind = "ExternalOutput" if name in self.debug else "Internal"
        h = self.nc.dram_tensor(name, list(shape), dt, kind=kind)
        return Buf(h.ap(), Tile(name))

    def sb(self, name, cols, dt=F32):
        nby = cols * (4 if dt == F32 else 2)
        n4 = (nby + 31) // 32 * 8
        off = self.arena_off
        assert off + n4 <= self.arena_cols, f"SBUF arena overflow at {name}: {(off + n4) * 4} B"
        self.arena_off += n4
        ap = self.arena[:, off:off + n4]
        if dt != F32:
            ap = ap.bitcast(dt)[:, 0:cols]
        else:
            ap = ap[:, 0:cols]
        return Buf(ap, Tile(name))

    def phase_reset(self):
        self.p.barrier()
        self.arena_off = self.arena_floor

    def mm(self, out, lhsT, rhs, start, stop, extra_reads=()):
        o, a, b = out.ap, lhsT.ap, rhs.ap
        self.p.op("pe", lambda e: e.matmul(o, lhsT=a, rhs=b, start=start, stop=stop),
                  reads=[lhsT, rhs, *extra_reads], writes=[out])

    def tr(self, out, in_, ident):
        o, a, i = out.ap, in_.ap, ident.ap
        self.p.op("pe", lambda e: e.transpose(o, a, i), reads=[in_, ident], writes=[out])

    def act(self, out, in_, func, scale=1.0, bias=0.0, accum=None, eng="act"):
        o, a = out.ap, in_.ap
        reads = [in_]
        writes = [out]
        kw = {}
        if isinstance(scale, Buf):
            reads.append(scale)
            kw["scale"] = scale.ap
        else:
            kw["scale"] = float(scale)
        if isinstance(bias, Buf):
            reads.append(bias)
            kw["bias"] = bias.ap
        elif bias != 0.0:
            kw["bias"] = float(bias)
        if accum is not None:
            writes.append(accum)
            kw["accum_out"] = accum.ap
        self.p.op(eng, lambda e: e.activation(out=o, in_=a, func=func, **kw), reads=reads, writes=writes)

    def tt(self, out, in0, in1, op, eng="dve"):
        o, a, b = out.ap, in0.ap, in1.ap
        self.p.op(eng, lambda e: e.tensor_tensor(out=o, in0=a, in1=b, op=op), reads=[in0, in1], writes=[out])

    def ts(self, out, in0, s1, op0, s2=None, op1=None, eng="dve"):
        o, a = out.ap, in0.ap
        reads = [in0]
        if isinstance(s1, Buf):
            reads.append(s1)
            s1v = s1.ap
        else:
            s1v = float(s1)
        if isinstance(s2, Buf):
            reads.append(s2)
            s2v = s2.ap
        elif s2 is None:
            s2v = None
        else:
            s2v = float(s2)
        if op1 is None:
            self.p.op(eng, lambda e: e.tensor_scalar(out=o, in0=a, scalar1=s1v, scalar2=None, op0=op0),
                      reads=reads, writes=[out])
        else:
            self.p.op(eng, lambda e: e.tensor_scalar(out=o, in0=a, scalar1=s1v, scalar2=s2v, op0=op0, op1=op1),
                      reads=reads, writes=[out])

    def stt(self, out, in0, scalar, in1, op0, op1, eng="dve"):
        o, a, b = out.ap, in0.ap, in1.ap
        reads = [in0, in1]
        if isinstance(scalar, Buf):
            reads.append(scalar)
            sv = scalar.ap
        else:
            sv = float(scalar)
        self.p.op(eng, lambda e: e.scalar_tensor_tensor(out=o, in0=a, scalar=sv, in1=b, op0=op0, op1=op1),
                  reads=reads, writes=[out])

    def copy(self, out, in_, eng="dve"):
        o, a = out.ap, in_.ap
        self.p.op(eng, lambda e: e.tensor_copy(out=o, in_=a), reads=[in_], writes=[out])

    def recip(self, out, in_):
        o, a = out.ap, in_.ap
        self.p.op("dve", lambda e: e.reciprocal(out=o, in_=a), reads=[in_], writes=[out])

    def reduce(self, out, in_, op=ALU.add, eng="dve"):
        o, a = out.ap, in_.ap
        self.p.op(eng, lambda e: e.tensor_reduce(out=o, in_=a, axis=AX.X, op=op), reads=[in_], writes=[out])

    def memset(self, out, val, eng="pool"):
        o = out.ap
        self.p.op(eng, lambda e: e.memset(o, val), reads=[], writes=[out])

    def dma(self, out, in_, sem, eng="sp", **kw):
        self.p.dma(eng, out, in_, sem, **kw)

    def setup(self):
        nc, es = self.nc, self.es
        self.arena_cols = 52 * 1024 - 512
        self.arena = es.enter_context(nc.sbuf_tensor("arena", [128, self.arena_cols], F32))
        self.banks = []
        for i in range(8):
            ps = es.enter_context(nc.psum_tensor(f"ps{i}", [128, 512], F32))
            self.banks.append(Buf(ps[:, :], Tile(f"ps{i}")))
        self.identf = self.sb("identf", 128, F32)
        self.ident = self.sb("ident", 128, BF16)
        self.ones_bf = self.sb("ones_bf", 128, BF16)
        self.ones_f = self.sb("ones_f", 128, F32)
        self.memset(self.identf, 0.0)
        ia = self.identf.ap
        self.p.op("pool", lambda e: e.affine_select(out=ia, in_=ia, pattern=[[-1, 128]], compare_op=ALU.not_equal,
                                                   fill=1.0, base=0, channel_multiplier=1),
                  reads=[self.identf], writes=[self.identf])
        self.copy(self.ident, self.identf)
        self.memset(self.ones_f, 1.0)
        self.copy(self.ones_bf, self.ones_f)
        self.arena_floor = self.arena_off


    def norm_scale(self, xi, xsb, stb, junk=None, lnexp=False):
        self.act(xsb, xi, AF.Square, accum=stb[:, 0:1])
        if lnexp:
            self.act(stb[:, 1:2], stb[:, 0:1], AF.Ln, scale=1.0 / D, bias=self.eps_t)
            self.act(stb[:, 2:3], stb[:, 1:2], AF.Exp, scale=-0.5)
        else:
            self.act(stb[:, 1:2], stb[:, 0:1], AF.Sqrt, scale=1.0 / D, bias=self.eps_t)
            self.recip(stb[:, 2:3], stb[:, 1:2])
        self.act(xsb, xi, AF.Copy, scale=stb[:, 2:3])

    def transpose_gain(self, xsb, bank, gT, dstv):
        bv = bank.bc(BF16).re("p (c t) -> p c t", c=8)
        for c in range(8):
            self.tr(bv[:, c, :], xsb[:, c * 128:(c + 1) * 128], self.ident)
        self.tt(dstv, bv, gT.re("p (c o) -> p c o", o=1).bcast([128, 8, 128]), ALU.mult)

    def post_norm_residual(self, banks2, s2, junk, pg, o, xr, half, lnexp=False):
        for hh in range(2):
            self.act(junk[:, 0:512], banks2[hh], AF.Square, accum=s2[:, hh:hh + 1])
        self.tt(s2[:, 2:3], s2[:, 0:1], s2[:, 1:2], ALU.add)
        if lnexp:
            assert not half
            self.act(s2[:, 3:4], s2[:, 2:3], AF.Ln, scale=1.0 / D, bias=self.eps_t)
            self.act(s2[:, 4:5], s2[:, 3:4], AF.Exp, scale=-0.5)
        elif half:
            self.act(s2[:, 3:4], s2[:, 2:3], AF.Sqrt, scale=4.0 / D, bias=self.eps4_t)
        else:
            self.act(s2[:, 3:4], s2[:, 2:3], AF.Sqrt, scale=1.0 / D, bias=self.eps_t)
        if not lnexp:
            self.recip(s2[:, 4:5], s2[:, 3:4])
        for hh in range(2):
            self.stt(o[:, hh * 512:(hh + 1) * 512], banks2[hh], s2[:, 4:5], pg[:, hh * 512:(hh + 1) * 512],
                     ALU.mult, ALU.mult)
        self.tt(o, o, xr, ALU.add)

    def ffn_phase(self, src, dst, pre_g, wkeys, post_g, tag, nblocks=NTOK // 512, after_weights=None):
        p = self.p
        k_in, k_out = wkeys
        pre = k_in in self.WB
        w_in = self.WB[k_in] if pre else self.I[k_in]
        w_out = self.WB[k_out] if pre else self.I[k_out]
        weng = "sp" if pre else "pool"
        w_in_v = w_in.re("(c p) f -> p c f", p=128)
        w_out_v = w_out.re("(c p) f -> p c f", p=128)
        gT = self.sb(f"gT{tag}", 8, F32)
        self.dma(gT, pre_g.re("(c p) -> p c", p=128), f"sm{tag}", allow_slow_non_contiguous=True)
        pg = self.sb(f"pg{tag}", D, F32)
        self.dma(pg, Buf(post_g.ap.partition_broadcast(128), post_g.t), f"sm{tag}")
        xin = [self.sb(f"xin{i}{tag}", D, F32) for i in range(2)]
        xs = [self.sb(f"xs{i}{tag}", D, BF16) for i in range(4)]
        junk = self.sb(f"junk{tag}", 512, BF16)
        xT = self.sb(f"xT{tag}", 8 * 512, BF16)
        xTv = xT.re("p (c t) -> p c t", c=8)
        actT = self.sb(f"actT{tag}", NFC * 512, BF16)
        actTv = actT.re("p (c t) -> p c t", c=NFC)
        sg = [self.sb(f"sg{i}{tag}", 512, F32) for i in range(2)]
        st = [self.sb(f"st{i}{tag}", 8, F32) for i in range(2)]
        st2 = [self.sb(f"st2{i}{tag}", 8, F32) for i in range(2)]
        xres = [self.sb(f"xres{i}{tag}", D, F32) for i in range(2)]
        ot = [self.sb(f"ot{i}{tag}", D, F32) for i in range(2)]
        B = self.banks

        def front_elem(b, tt):
            k = b * 4 + tt
            xi = xin[k % 2]
            self.dma(xi, src[k * 128:(k + 1) * 128, :], f"xin{k % 2}{tag}")
            self.norm_scale(xi, xs[k % 4], st[k % 2], junk)

        def front_pe(b, tt):
            k = b * 4 + tt
            self.transpose_gain(xs[k % 4], B[k % 2], gT, xTv[:, :, tt * 128:(tt + 1) * 128])

        for tt in range(4):
            front_elem(0, tt)
        gb_ = [0, 6, 12, 17, 22]
        W1g, jgrp = [], {}
        for g in range(4):
            j0, j1 = gb_[g], gb_[g + 1]
            gw = (j1 - j0) * 128
            Wt = self.sb(f"W1{tag}g{g}", 8 * 2 * gw, BF16).re("p (c f) -> p c f", c=8)
            self.dma(Wt[:, :, 0:gw], w_in_v[:, :, j0 * 128:j1 * 128], f"W1{tag}g{g}", eng=weng)
            self.dma(Wt[:, :, gw:2 * gw], w_in_v[:, :, DFF + j0 * 128:DFF + j1 * 128], f"W1{tag}g{g}", eng=weng)
            W1g.append(Wt)
            for j in range(j0, j1):
                jgrp[j] = (g, (j - j0) * 128, gw)
        W2 = self.sb(f"W2{tag}", NFC * D, BF16)
        W2v = W2.re("p (c f) -> p c f", c=NFC)
        for c in range(0, NFC, 11):
            self.dma(W2v[:, c:c + 11, :], w_out_v[:, c:c + 11, :], f"W2{tag}", eng=weng)
        if after_weights is not None:
            after_weights(after=[W2])
        for tt in range(4):
            front_pe(0, tt)
        for b in range(nblocks):
            for j in range(NFC):
                pg_, pu_ = B[(j % 2) * 2], B[(j % 2) * 2 + 1]
                g_, lo_, gw_ = jgrp[j]
                for c in range(8):
                    self.mm(pg_, W1g[g_][:, c, lo_:lo_ + 128], xTv[:, c, :], c == 0, c == 7)
                for c in range(8):
                    self.mm(pu_, W1g[g_][:, c, gw_ + lo_:gw_ + lo_ + 128], xTv[:, c, :], c == 0, c == 7)
                s = sg[j % 2]
                self.act(s, pg_, AF.Silu)
                self.tt(actTv[:, j, :], s, pu_, ALU.mult)
                if b + 1 < nblocks and j in (2, 7, 12, 17):
                    front_elem(b + 1, (j - 2) // 5)
            if b + 1 < nblocks:
                for tt in range(4):
                    front_pe(b + 1, tt)
            for tt in range(4):
                k = b * 4 + tt
                banks2 = (B[4 + (tt % 2) * 2], B[5 + (tt % 2) * 2])
                for hh in range(2):
                    for j in range(NFC):
                        self.mm(banks2[hh], actTv[:, j, tt * 128:(tt + 1) * 128], W2v[:, j, hh * 512:(hh + 1) * 512],
                                j == 0, j == NFC - 1)
                s2 = st2[k % 2]
                xr = xres[k % 2]
                self.dma(xr, src[k * 128:(k + 1) * 128, :], f"xres{k % 2}{tag}")
                o = ot[k % 2]
                self.post_norm_residual(banks2, s2, junk, pg, o, xr, half=True)
                self.dma(dst[k * 128:(k + 1) * 128, :], o, f"st{k % 4}{tag}")


    def wload(self, name, srckey, col0, ncols, sem):
        W = self.sb(name, 8 * ncols, BF16)
        Wv = W.re("p (c f) -> p c f", c=8)
        if srckey in self.WB:
            sv = self.WB[srckey].re("(c p) f -> p c f", p=128)
            for c0 in range(0, 8, 4):
                self.dma(Wv[:, c0:c0 + 4, :], sv[:, c0:c0 + 4, col0:col0 + ncols], sem, eng="sp")
        else:
            sv = self.I[srckey].re("(c p) f -> p c f", p=128)
            for c0 in range(0, 8, 2):
                self.dma(Wv[:, c0:c0 + 2, :], sv[:, c0:c0 + 2, col0:col0 + ncols], sem, eng="pool")
        return Wv

    def convert_weights(self, after=()):
        for key, rows, cols in (("w_in", D, N_IN), ("w_branch_a", D, D), ("w_branch_b", D, D), ("w_out", D, D),
                                ("ffn2_w_in", D, 2 * DFF), ("ffn2_w_out", DFF, D)):
            dst = self.dram_scr("wb_" + key, [rows, cols], BF16)
            dv = dst.re("(c p) f -> p c f", p=128)
            sv = self.I[key].re("(c p) f -> p c f", p=128)
            nchunk = rows // 128
            step = 2
            for c0 in range(0, nchunk, step):
                self.dma(dv[:, c0:c0 + step, :], sv[:, c0:c0 + step, :], "wb_" + key, eng="pool", after=after)
            self.WB[key] = dst

    def aff(self, buf, val, pattern, cm, cmp):
        self.memset(buf, val)
        a = buf.ap
        self.p.op("pool", lambda e: e.affine_select(out=a, in_=a, pattern=pattern, compare_op=cmp, fill=0.0,
                                                   base=0, channel_multiplier=cm), reads=[buf], writes=[buf])

    def pipeline(self, n_items, stages, order=None):
        ns = len(stages)
        for step in range(n_items + ns - 1):
            for si in (order if order is not None else range(ns - 1, -1, -1)):
                i = step - si
                if 0 <= i < n_items:
                    stages[si](i)

    def gla_pass_a(self, H1, S, nseq=SEQ_PER_CORE):
        I, B = self.I, self.banks
        Wg = self.wload("Wg", "w_in", 0, 3104, "Wg")
        gT = self.sb("gTm", 8, F32)
        self.dma(gT, I["mix_pre_g"].re("(c p) -> p c", p=128), "smA", allow_slow_non_contiguous=True)
        up = {"f": self.sb("upf", 512, F32), "b": self.sb("upb", 512, F32)}
        for d in "fb":
            self.memset(up[d][0:64, :], 0.0)
            self.dma(up[d][0:16, :], I["gla_decay_up_" + d], "smA")
            self.dma(up[d][32:33, :], I["gla_decay_bias_" + d].re("(o f) -> o f", o=1), "smA")
        tri = {"f": self.sb("triF", 128, F32), "b": self.sb("triB", 128, F32)}
        self.aff(tri["f"], -1.0 / 16, [[1, 128]], -1, ALU.is_ge)
        self.aff(tri["b"], -1.0 / 16, [[-1, 128]], 1, ALU.is_ge)
        xin = [self.sb(f"xinA{i}", D, F32) for i in range(2)]
        xs = [self.sb(f"xsA{i}", D, BF16) for i in range(4)]
        st = [self.sb(f"stA{i}", 8, F32) for i in range(2)]
        uT = self.sb("uTA", 8 * 512, BF16)
        uTv = uT.re("p (c t) -> p c t", c=8)
        qraw = self.sb("qraw", 4 * 512, F32).re("p (h t) -> p h t", h=4)
        kraw = self.sb("kraw", 4 * 512, F32).re("p (h t) -> p h t", h=4)
        sgr = self.sb("sgrA", 8 * 512, BF16).re("p (c t) -> p c t", c=8)
        z = {"f": self.sb("zf", 512, F32), "b": self.sb("zb", 512, F32)}
        for d in "fb":
            self.memset(z[d][0:64, :], 0.0)
            self.memset(z[d][32:33, :], 1.0)
        vtm = [self.sb(f"vtm{i}", D, BF16) for i in range(4)]
        e1 = [self.sb(f"e1_{i}", 512, F32) for i in range(2)]
        sp = [self.sb(f"sp_{i}", 512, F32) for i in range(2)]
        eb = [self.sb(f"eb_{i}", 512, F32) for i in range(2)]
        enb = [self.sb(f"enb_{i}", 512, F32) for i in range(2)]
        eblb = [self.sb(f"eblb{i}", 4, F32) for i in range(4)]
        qt = [self.sb(f"qt{i}", 512, BF16) for i in range(4)]
        kt = [self.sb(f"kt{i}", 512, BF16) for i in range(4)]
        ktm = [self.sb(f"ktm{i}", 512, BF16) for i in range(4)]
        Sb = self.sb("SbA", 1024, F32)
        Sbb = [self.sb(f"SbbA{i}", 1024, BF16) for i in range(2)]
        UTv = S["UT"].re("(c p) t -> p c t", p=128)
        qscale = 128.0 ** -0.5
        ZB, BTB, TRB, KVB = (B[0], B[1]), (B[2], B[3]), (B[4], B[5]), (B[6], B[7])
        blocks = [(seq, blk) for seq in range(nseq) for blk in range(3, -1, -1)]

        def front_elem(bi, tt):
            seq, blk = blocks[bi]
            r0 = seq * L + blk * 512
            k = bi * 4 + tt
            self.dma(xin[k % 2], H1[r0 + tt * 128:r0 + (tt + 1) * 128, :], f"xinA{k % 2}")
            self.norm_scale(xin[k % 2], xs[k % 4], st[k % 2], None, lnexp=True)

        qraw2 = [qraw, self.sb("qrawB2", 4 * 512, F32).re("p (h t) -> p h t", h=4)]
        kraw2 = [kraw, self.sb("krawB2", 4 * 512, F32).re("p (h t) -> p h t", h=4)]

        def pre_block(bj, part):
            seq_, blk_ = blocks[bj]
            r0_ = seq_ * L + blk_ * 512
            if part == 0:
                for tt in range(4):
                    k = bj * 4 + tt
                    self.transpose_gain(xs[k % 4], B[k % 2], gT, uTv[:, :, tt * 128:(tt + 1) * 128])
                self.dma(UTv[:, :, r0_:r0_ + 512], uTv, "UTst", eng="pool")
            elif part in (1, 2):
                nb_ = 0
                for h in ((0, 1) if part == 1 else (2, 3)):
                    for (off, dstb) in ((OFF_GQ, qraw2[bj % 2]), (OFF_GK, kraw2[bj % 2])):
                        bank = B[nb_ % 2]
                        nb_ += 1
                        for c in range(8):
                            self.mm(bank, Wg[:, c, off + h * 128:off + (h + 1) * 128], uTv[:, c, :], c == 0, c == 7)
                        if off == OFF_GQ:
                            self.act(dstb[:, h, :], bank, AF.Copy)
                        else:
                            self.copy(dstb[:, h, :], bank)
            else:
                for nb_, (d, off) in enumerate((("f", OFF_ZF), ("b", OFF_ZB))):
                    bank = B[nb_ % 2]
                    for c in range(8):
                        self.mm(bank[0:16, :], Wg[:, c, off:off + 16], uTv[:, c, :], c == 0, c == 7)
                    self.copy(z[d][0:16, :], bank[0:16, :])

        for tt in range(4):
            front_elem(0, tt)
        for part in range(4):
            pre_block(0, part)
        for bi, (seq, blk) in enumerate(blocks):
            r0 = seq * L + blk * 512
            has_next = bi + 1 < len(blocks)
            if blk == 3:
                self.memset(Sb, 0.0)
            nb = 0
            for fc in range(8):
                bank = B[4 + fc % 2]
                for c in range(8):
                    self.mm(bank, Wg[:, c, OFF_GR + fc * 128:OFF_GR + (fc + 1) * 128], uTv[:, c, :], c == 0, c == 7)
                self.act(sgr[:, fc, :], bank, AF.Silu)
            for ck in range(4):
                gc = seq * 16 + blk * 4 + ck
                for hh in range(2):
                    bank = B[2 + hh]
                    for c in range(8):
                        self.mm(bank, uTv[:, c, ck * 128:(ck + 1) * 128],
                                Wg[:, c, OFF_GV + hh * 512:OFF_GV + (hh + 1) * 512], c == 0, c == 7)
                    if hh == 0:
                        self.act(vtm[ck][:, 0:512], bank, AF.Copy)
                    else:
                        self.copy(vtm[ck][:, 512:1024], bank)
                self.dma(S["V"][gc], vtm[ck], f"Vst{ck}", eng="pool")
            for ck in range(4):
                gc = seq * 16 + blk * 4 + ck
                self.dma(S["SGR"][gc].re("p (c t) -> p c t", c=8), sgr[:, :, ck * 128:(ck + 1) * 128], "SGRst", eng="pool")
            items = [(ck, d) for ck in range(3, -1, -1) for d in "bf"]

            def s0(i):
                ck, d = items[i]
                self.mm(ZB[i % 2], z[d][0:33, ck * 128:(ck + 1) * 128], up[d][0:33, :], True, True)
                if has_next and i % 2 == 0:
                    front_elem(bi + 1, i // 2)

            def s1(i):
                self.act(e1[i % 2], ZB[i % 2], AF.Exp, scale=-1.0)
                self.act(sp[i % 2], e1[i % 2], AF.Ln, bias=1.0)
                if has_next and i == 7:
                    pre_block(bi + 1, 0)

            def s2(i):
                ck, d = items[i]
                for h in range(4):
                    self.mm(BTB[i % 2][:, h * 128:(h + 1) * 128], sp[i % 2][:, h * 128:(h + 1) * 128], tri[d], True, True)
                if has_next and i == 7:
                    pre_block(bi + 1, 1)

            def s3(i):
                ck, d = items[i]
                gc = seq * 16 + blk * 4 + ck
                e_, en_ = eb[i % 2], enb[i % 2]
                self.act(e_, BTB[i % 2], AF.Exp)
                self.act(en_, BTB[i % 2], AF.Exp, scale=-1.0)
                q_, k_ = qt[i % 4], kt[i % 4]
                self.stt(q_.re("p (h t) -> p h t", h=4), qraw2[bi % 2][:, :, ck * 128:(ck + 1) * 128], qscale,
                         e_.re("p (h t) -> p h t", h=4), ALU.mult, ALU.mult)
                self.tt(k_.re("p (h t) -> p h t", h=4), kraw2[bi % 2][:, :, ck * 128:(ck + 1) * 128],
                        en_.re("p (h t) -> p h t", h=4), ALU.mult)
                ev = e_.re("p (h t) -> p h t", h=4)
                if d == "f":
                    self.copy(self.ebl[:, gc * 4:(gc + 1) * 4], ev[:, :, 127])
                else:
                    self.copy(eblb[i % 4], ev[:, :, 0])
                self.dma(S["Q" + d][gc], q_, f"Qst{i % 4}", eng="pool")
                self.dma(S["K" + d][gc], k_, f"Kst{i % 4}", eng="pool")
                if has_next and i == 7:
                    pre_block(bi + 1, 2)

            def s4(i):
                trb = TRB[i % 2].bc(BF16)
                for h in range(4):
                    self.tr(trb[:, h * 128:(h + 1) * 128], kt[i % 4][:, h * 128:(h + 1) * 128], self.ident)

            def s5(i):
                ck, d = items[i]
                gc = seq * 16 + blk * 4 + ck
                km_ = ktm[i % 4]
                self.copy(km_, TRB[i % 2].bc(BF16)[:, 0:512])
                if d == "f":
                    self.dma(S["KFT"][gc], km_, f"KFTst{i % 4}", eng="pool")
                else:
                    sbb = Sbb[(i // 2) % 2]
                    self.act(sbb, Sb, AF.Copy)
                    self.dma(S["SB"][gc], sbb, f"SBst{(i // 2) % 2}", eng="pool")
                    for h in range(4):
                        self.mm(KVB[h // 2][:, (h % 2) * 256:(h % 2 + 1) * 256], km_[:, h * 128:(h + 1) * 128],
                                vtm[ck][:, h * 256:(h + 1) * 256], True, True)
                    for h in range(4):
                        sc = eblb[i % 4][:, h:h + 1]
                        self.ts(Sb[:, h * 256:(h + 1) * 256], Sb[:, h * 256:(h + 1) * 256], sc, ALU.mult)
                        self.stt(Sb[:, h * 256:(h + 1) * 256], KVB[h // 2][:, (h % 2) * 256:(h % 2 + 1) * 256], sc,
                                 Sb[:, h * 256:(h + 1) * 256], ALU.mult, ALU.add)
                if has_next and i == 7:
                    pre_block(bi + 1, 3)

            self.pipeline(len(items), [s0, s1, s2, s3, s4, s5])

    def gla_pass_b(self, S, nseq=SEQ_PER_CORE):
        I, B = self.I, self.banks
        Wa = self.wload("Wa", "w_branch_a", 0, D, "Wa")
        Wga = self.wload("Wga", "w_in", OFF_GA, D, "Wga")
        Wgb = self.wload("WgbB", "w_in", OFF_GB, D, "WgbB")
        SGBb = self.sb("SGBb", 8 * 512, BF16).re("p (c t) -> p c t", c=8)
        SGBv = S["SGB"].re("(c p) t -> p c t", p=128)
        gog = self.sb("gog", 8, F32)
        self.dma(gog, I["gla_out_g"].re("h (e p) -> p (h e)", p=128), "smB", allow_slow_non_contiguous=True)
        maskF = self.sb("maskF", 128, F32)
        maskB = self.sb("maskB", 128, F32)
        self.aff(maskF, 1.0, [[1, 128]], -1, ALU.is_ge)
        self.aff(maskB, 1.0, [[-1, 128]], 1, ALU.is_gt)
        NLD = 3
        names = ("Qf", "Qb", "Kf", "Kb", "KFT")
        ld = {n: [self.sb(f"ld{n}{i}", 512, BF16) for i in range(NLD)] for n in names}
        for n in ("V", "SGR", "SB"):
            ld[n] = [self.sb(f"ld{n}{i}", 1024, BF16) for i in range(NLD)]
        sT = {d: [self.sb(f"sT{d}{i}", 512, BF16) for i in range(2)] for d in "fb"}
        Sf = self.sb("SfB", 1024, F32)
        Sfb = self.sb("SfbB", 1024, BF16)
        sq = [self.sb(f"sqB{i}", 1024, BF16) for i in range(2)]
        oTs = [self.sb(f"oTsB{i}", 1024, F32) for i in range(3)]
        lnv = self.sb("lnvB", 512, F32)
        rstd = self.sb("rstdB", 512, F32)
        t1 = [self.sb(f"t1B{i}", 1024, F32) for i in range(5)]
        mT = [self.sb(f"mTB{i}", 8 * 512, BF16).re("p (c t) -> p c t", c=8) for i in range(2)]
        uT = [self.sb(f"uTB{i}", 8 * 512, BF16).re("p (c t) -> p c t", c=8) for i in range(3)]
        sga = [self.sb(f"sga{i}", 512, F32) for i in range(2)]
        MAb = [self.sb(f"MAb{i}", 8 * 512, BF16).re("p (c t) -> p c t", c=8) for i in range(2)]
        UTv = S["UT"].re("(c p) t -> p c t", p=128)
        MAv = S["MA"].re("(c p) t -> p c t", p=128)
        eps_t = self.eps_t
        SC = (B[0], B[1])
        OB = (B[2], B[3])
        KVB = (B[4], B[5])
        NB = B[6]
        EP = (B[7], B[6])
        nch = nseq * 16

        def T(i):
            return {n: ld[n][i % NLD] for n in ld}

        def p0(i):
            t = T(i)
            if i % 4 == 0:
                blk_r0 = (i // 4) * 512
                self.dma(uT[(i // 4) % 3], UTv[:, :, blk_r0:blk_r0 + 512], f"uTB{(i // 4) % 3}")
            for n in ("Kf", "Qf", "Kb", "Qb", "V", "SB", "KFT", "SGR"):
                self.dma(t[n], S[n][i], f"ldB{n}{i % NLD}")

        def p1(i):
            t = T(i)
            for di, d in enumerate("fb"):
                for h in range(4):
                    self.mm(SC[di][:, h * 128:(h + 1) * 128], t["K" + d][:, h * 128:(h + 1) * 128],
                            t["Q" + d][:, h * 128:(h + 1) * 128], True, True)

        def p2(i):
            for di, d in enumerate("fb"):
                m = (maskF if d == "f" else maskB).re("p (o t) -> p o t", o=1).bcast([128, 4, 128])
                self.tt(sT[d][i % 2].re("p (h t) -> p h t", h=4), SC[di].re("p (h t) -> p h t", h=4), m, ALU.mult)
            self.tt(t1[i % 5].re("p (c t) -> p c t", c=8), T(i)["SGR"].re("p (c t) -> p c t", c=8),
                    gog.re("p (c o) -> p c o", o=1).bcast([128, 8, 128]), ALU.mult)

        def p3(i):
            t = T(i)
            if i % 16 == 0:
                self.memset(Sf, 0.0)
                self.memset(Sfb, 0.0)
            for fc in range(8):
                h = fc // 2
                ob = OB[fc // 4][:, (fc % 4) * 128:(fc % 4 + 1) * 128]
                vcol = t["V"][:, fc * 128:(fc + 1) * 128]
                self.mm(ob, vcol, sT["f"][i % 2][:, h * 128:(h + 1) * 128], True, False)
                self.mm(ob, vcol, sT["b"][i % 2][:, h * 128:(h + 1) * 128], False, False)
                self.mm(ob, Sfb[:, fc * 128:(fc + 1) * 128], t["Qf"][:, h * 128:(h + 1) * 128], False, False)
                self.mm(ob, t["SB"][:, fc * 128:(fc + 1) * 128], t["Qb"][:, h * 128:(h + 1) * 128], False, True)
            for h in range(4):
                self.mm(KVB[h // 2][:, (h % 2) * 256:(h % 2 + 1) * 256], t["KFT"][:, h * 128:(h + 1) * 128],
                        t["V"][:, h * 256:(h + 1) * 256], True, True)

        def p4(i):
            for hb in range(2):
                self.act(sq[i % 2][:, hb * 512:(hb + 1) * 512], OB[hb], AF.Square)
                self.act(oTs[i % 3][:, hb * 512:(hb + 1) * 512], OB[hb], AF.Copy)
            for h in range(4):
                sc = self.ebl[:, i * 4 + h:i * 4 + h + 1]
                self.ts(Sf[:, h * 256:(h + 1) * 256], Sf[:, h * 256:(h + 1) * 256], sc, ALU.mult)
                self.stt(Sf[:, h * 256:(h + 1) * 256], KVB[h // 2][:, (h % 2) * 256:(h % 2 + 1) * 256], sc,
                         Sf[:, h * 256:(h + 1) * 256], ALU.mult, ALU.add)
            self.copy(Sfb, Sf)

        def p5(i):
            for h in range(4):
                nbk = NB[:, h * 128:(h + 1) * 128]
                self.mm(nbk, self.ones_bf, sq[i % 2][:, (2 * h) * 128:(2 * h + 1) * 128], True, False)
                self.mm(nbk, self.ones_bf, sq[i % 2][:, (2 * h + 1) * 128:(2 * h + 2) * 128], False, True)

        def p6(i):
            ck = i % 4
            blk = i // 4
            self.act(lnv, NB, AF.Ln, scale=1.0 / 256, bias=eps_t)
            self.act(rstd, lnv, AF.Exp, scale=-0.5)
            tv = t1[i % 5].re("p (h e t) -> p h e t", h=4, e=2)
            self.tt(tv, tv, rstd.re("p (h o t) -> p h o t", h=4, o=1).bcast([128, 4, 2, 128]), ALU.mult)
            m_ = mT[blk % 2]
            self.tt(m_[:, :, ck * 128:(ck + 1) * 128], oTs[i % 3].re("p (c t) -> p c t", c=8),
                    t1[i % 5].re("p (c t) -> p c t", c=8), ALU.mult)
            if ck == 3:
                push_block(blk)
            drain(6)

        epi_q = []
        epc = {"n": 0}

        def nextbank():
            epc["n"] += 1
            return EP[epc["n"] % 2]

        sg_e0 = self.sb("sgeB0", 512, F32)
        sg_e = [sg_e0, sg_e0]

        def sigm(dst, src):
            e_ = sg_e[epc["n"] % 2]
            self.act(e_, src, AF.Exp, scale=-1.0)
            self.act(e_, e_, AF.Ln, bias=1.0)
            self.act(dst, e_, AF.Exp, scale=-1.0)

        def push_block(blk):
            r0 = blk * 512
            u_, m_, MA_ = uT[blk % 3], mT[blk % 2], MAb[blk % 2]

            def ga(o):
                def f():
                    bg = nextbank()
                    for c in range(8):
                        self.mm(bg, Wga[:, c, o * 128:(o + 1) * 128], u_[:, c, :], c == 0, c == 7)
                    sigm(sga[o % 2], bg)
                return f

            def ya(o):
                def f():
                    by = nextbank()
                    for c in range(8):
                        self.mm(by, Wa[:, c, o * 128:(o + 1) * 128], m_[:, c, :], c == 0, c == 7)
                    self.tt(MA_[:, o, :], by, sga[o % 2], ALU.mult)
                    if o == 7:
                        self.dma(MAv[:, :, r0:r0 + 512], MA_, f"MAst{blk % 2}", eng="pool")
                return f

            def gb(o):
                def f():
                    bg = nextbank()
                    for c in range(8):
                        self.mm(bg, Wgb[:, c, o * 128:(o + 1) * 128], u_[:, c, :], c == 0, c == 7)
                    sigm(SGBb[:, o, :], bg)
                    if o == 7:
                        self.dma(SGBv[:, :, r0:r0 + 512], SGBb, "SGBst", eng="pool")
                return f

            for o in range(8):
                epi_q.extend([ga(o), ya(o), gb(o)])

        def drain(n):
            for _ in range(n):
                if epi_q:
                    epi_q.pop(0)()

        self.pipeline(nch, [p0, p1, p2, p3, p4, p5, p6], order=(4, 6, 5, 3, 2, 1, 0))
        drain(10 ** 6)

    def rope_chain(self, src, W, gbc, Ct, St, tile, dst, T):
        nh = W // 128
        ta, ss, lnv, rs, qn, tb = T
        self.act(ta[:, 0:W], src, AF.Square)
        self.reduce(ss[:, 0:nh], ta[:, 0:W].re("p (h d) -> p h d", h=nh))
        self.act(lnv[:, 0:nh], ss[:, 0:nh], AF.Ln, scale=1.0 / 128, bias=self.eps_t)
        self.act(rs[:, 0:nh], lnv[:, 0:nh], AF.Exp, scale=-0.5)
        qv = qn[:, 0:W].re("p (h d) -> p h d", h=nh)
        self.tt(qv, src.re("p (h d) -> p h d", h=nh),
                rs[:, 0:nh].re("p (h o) -> p h o", o=1).bcast([128, nh, 128]), ALU.mult)
        self.tt(qv, qv, gbc.re("p (o d) -> p o d", o=1).bcast([128, nh, 128]), ALU.mult)
        self.tt(ta[:, 0:W].re("p (h d) -> p h d", h=nh), qv,
                Ct[:, tile, :].re("p (o d) -> p o d", o=1).bcast([128, nh, 128]), ALU.mult)
        q5 = qn[:, 0:W].re("p (h a j e) -> p h a j e", h=nh, a=2, j=2)
        t5 = tb[:, 0:W].re("p (h a j e) -> p h a j e", h=nh, a=2, j=2)
        S5 = St[:, tile, :].re("p (o a j e) -> p o a j e", o=1, a=2, j=2)
        for j in range(2):
            self.tt(t5[:, :, :, j, :], q5[:, :, :, 1 - j, :], S5[:, :, :, j, :].bcast([128, nh, 2, 32]), ALU.mult)
        self.tt(dst, ta[:, 0:W], tb[:, 0:W], ALU.add)

    def attn_phase(self, H1, H2, S, nseq=SEQ_PER_CORE):
        I, B = self.I, self.banks
        Wq = self.wload("Wq", "w_in", OFF_AQ, D, "Wq")
        Wkv = self.wload("Wkv", "w_in", OFF_AK, 512, "Wkv")
        Wb = self.wload("Wb", "w_branch_b", 0, D, "Wb")
        Wo = self.wload("Wo", "w_out", 0, D, "Wo")
        pg = self.sb("pgM", D, F32)
        self.dma(pg, Buf(I["mix_post_g"].ap.partition_broadcast(128), I["mix_post_g"].t), "smC")
        Ct = self.sb("ropeC", 16 * 128, F32).re("p (n d) -> p n d", n=16)
        St = self.sb("ropeS", 16 * 128, F32).re("p (n d) -> p n d", n=16)
        rv = I["rope"].re("(n p) d -> p n d", p=128)
        self.dma(Ct, rv[:, :, 0:128], "smC")
        self.dma(St, rv[:, :, 128:256], "smC")
        gq = self.sb("g_q", 128, F32)
        gk = self.sb("g_k", 128, F32)
        self.dma(gq, Buf(I["att_q_norm_g"].ap.partition_broadcast(128), I["att_q_norm_g"].t), "smC")
        self.dma(gk, Buf(I["att_k_norm_g"].ap.partition_broadcast(128), I["att_k_norm_g"].t), "smC")
        KT = self.sb("KT", 2 * L, BF16).re("p (h t) -> p h t", h=2)
        Vt = self.sb("Vt", 16 * 256, BF16).re("p (n d) -> p n d", n=16)
        uT = [self.sb(f"uTC{i}", 8 * 512, BF16).re("p (c t) -> p c t", c=8) for i in range(2)]
        Ts = []
        for i in range(3):
            tb_ = self.sb(f"tbC{i}", 512, F32)
            Ts.append((self.sb(f"taC{i}", 512, F32), self.sb(f"ssC{i}", 4, F32), self.sb(f"lnvC{i}", 4, F32),
                       self.sb(f"rsC{i}", 4, F32), self.sb(f"qnC{i}", 512, F32), tb_, tb_))
        qr = [self.sb(f"qrC{i}", D, BF16) for i in range(2)]
        kr = [self.sb(f"krC{i}", 256, BF16) for i in range(3)]
        qT = [self.sb(f"qTC{i}", 8 * 512, BF16).re("p (h t) -> p h t", h=8) for i in range(2)]
        obT = [self.sb(f"obT{i}", 8 * 512, BF16).re("p (h t) -> p h t", h=8) for i in range(2)]
        PT = [self.sb(f"PT{i}", 512, BF16) for i in range(4)]
        rden = self.sb("rdenC", 512, F32)
        lden = self.sb("ldenC", 512, F32)
        if DEN_QUADS:
            ots = [self.sb(f"otsC{i}", 512, F32) for i in range(2)]
            PA = [self.sb(f"PAC{i}", 512, BF16) for i in range(2)]
            PQ = [self.sb(f"PQC{i}", 512, BF16) for i in range(3)]
        else:
            ots = [self.sb(f"otsC{i}", 512, F32) for i in range(2)]
        t2 = self.sb("t2C0", 512, F32)
        MAl = self.sb("MAl", 8 * 512, BF16).re("p (c t) -> p c t", c=8)
        SGl = self.sb("SGl", 8 * 512, BF16).re("p (c t) -> p c t", c=8)
        junk = self.sb("junkC", 512, BF16)
        st2 = [self.sb(f"st2C{i}", 8, F32) for i in range(2)]
        xres = self.sb("xresC0", D, F32)
        ot = self.sb("otC0", D, F32)
        fs = ot
        UTv = S["UT"].re("(c p) t -> p c t", p=128)
        MAv = S["MA"].re("(c p) t -> p c t", p=128)
        SGv = S["SGB"].re("(c p) t -> p c t", p=128)
        sm_scale = 128.0 ** -0.5
        SIDE = [B[0], B[1], B[2]]
        OT, DEN = B[3], B[4]
        STB = [B[5], B[6], B[7]]
        cnt = {"side": 0, "chain": 0, "u": 0}

        def side_bank():
            cnt["side"] += 1
            return SIDE[cnt["side"] % 3]

        def chain_T():
            cnt["chain"] += 1
            return Ts[cnt["chain"] % 3][:6]

        def rope_stages(src_fn, W, gbc, tile_fn, dst_fn):
            nh = W // 128
            stt_ = {}

            def s1():
                T_ = Ts[cnt["chain"] % 3]
                cnt["chain"] += 1
                stt_["T"] = T_
                ta, ss, lnv, rs, qn, tb, qs = T_
                src = src_fn()
                self.act(ta[:, 0:W], src, AF.Square)
                self.act(qs[:, 0:W], src, AF.Copy)

            def s2():
                ta, ss, lnv, rs, qn, tb, qs = stt_["T"]
                self.reduce(ss[:, 0:nh], ta[:, 0:W].re("p (h d) -> p h d", h=nh))

            def s3():
                ta, ss, lnv, rs, qn, tb, qs = stt_["T"]
                self.act(lnv[:, 0:nh], ss[:, 0:nh], AF.Ln, scale=1.0 / 128, bias=self.eps_t)
                self.act(rs[:, 0:nh], lnv[:, 0:nh], AF.Exp, scale=-0.5)

            def s4():
                ta, ss, lnv, rs, qn, tb, qs = stt_["T"]
                tile = tile_fn()
                qv = qn[:, 0:W].re("p (h d) -> p h d", h=nh)
                self.tt(qv, qs[:, 0:W].re("p (h d) -> p h d", h=nh),
                        rs[:, 0:nh].re("p (h o) -> p h o", o=1).bcast([128, nh, 128]), ALU.mult)
                self.tt(qv, qv, gbc.re("p (o d) -> p o d", o=1).bcast([128, nh, 128]), ALU.mult)
                self.tt(ta[:, 0:W].re("p (h d) -> p h d", h=nh), qv,
                        Ct[:, tile, :].re("p (o d) -> p o d", o=1).bcast([128, nh, 128]), ALU.mult)

            def s5():
                ta, ss, lnv, rs, qn, tb, qs = stt_["T"]
                tile = tile_fn()
                q5 = qn[:, 0:W].re("p (h a j e) -> p h a j e", h=nh, a=2, j=2)
                t5 = tb[:, 0:W].re("p (h a j e) -> p h a j e", h=nh, a=2, j=2)
                S5 = St[:, tile, :].re("p (o a j e) -> p o a j e", o=1, a=2, j=2)
                for j in range(2):
                    self.tt(t5[:, :, :, j, :], q5[:, :, :, 1 - j, :], S5[:, :, :, j, :].bcast([128, nh, 2, 32]), ALU.mult)
                self.tt(dst_fn(), ta[:, 0:W], tb[:, 0:W], ALU.add)

            return [s1, s2, s3, s4, s5]

        def q_items(seq, blk, par):
            items = []
            st = {}

            def qp(tt, hh):
                def s0():
                    u = uT[1]
                    if tt == 0 and hh == 0:
                        self.dma(u, UTv[:, :, seq * L + blk * 512:seq * L + (blk + 1) * 512], "uTC1")
                    bank = side_bank()
                    st[("b", tt, hh)] = bank
                    for c in range(8):
                        self.mm(bank, u[:, c, tt * 128:(tt + 1) * 128], Wq[:, c, hh * 512:(hh + 1) * 512], c == 0, c == 7)
                return [s0] + rope_stages(lambda: st[("b", tt, hh)], 512, gq, lambda: blk * 4 + tt,
                                          lambda: qr[tt % 2][:, hh * 512:(hh + 1) * 512])

            def qt(tt):
                def s0():
                    trb = side_bank().bc(BF16)
                    st[("t", tt)] = trb
                    for h in range(8):
                        self.tr(trb[:, h * 128:(h + 1) * 128], qr[tt % 2][:, h * 128:(h + 1) * 128], self.ident)

                def s1():
                    self.copy(qT[par][:, :, tt * 128:(tt + 1) * 128], st[("t", tt)].re("p (h t) -> p h t", h=8))
                return [s0, s1]

            for tt in range(4):
                if tt >= 1:
                    items.append((qt(tt - 1), [8]))
                items.append((qp(tt, 0), []))
                items.append((qp(tt, 1), []))
            items.append((qt(3), [8]))
            return items

        def epi_items(seq, blk, par):
            items = []
            r0 = seq * L + blk * 512
            st = {}

            def yb(o):
                def s0():
                    if o == 0:
                        self.dma(MAl, MAv[:, :, r0:r0 + 512], "MAld")
                        self.dma(SGl, SGv[:, :, r0:r0 + 512], "SGld")
                    by = side_bank()
                    st[("y", o)] = by
                    for h in range(8):
                        self.mm(by, Wb[:, h, o * 128:(o + 1) * 128], obT[par][:, h, :], h == 0, h == 7)

                def s1():
                    self.tt(t2, st[("y", o)], SGl[:, o, :], ALU.mult)
                    self.tt(MAl[:, o, :], t2, MAl[:, o, :], ALU.add)
                return [s0, s1]

            def op(tt):
                row = r0 + tt * 128
                s2_ = st2[tt % 2]

                def s0():
                    banks2 = (side_bank(), side_bank())
                    st[("o", tt)] = banks2
                    for hh in range(2):
                        for o in range(8):
                            self.mm(banks2[hh], MAl[:, o, tt * 128:(tt + 1) * 128], Wo[:, o, hh * 512:(hh + 1) * 512], o == 0, o == 7)
                    self.dma(xres, H1[row:row + 128, :], "xresC0")

                def s1():
                    banks2 = st[("o", tt)]
                    for hh in range(2):
                        self.act(junk[:, 0:512], banks2[hh], AF.Square, accum=s2_[:, hh:hh + 1])
                        self.act(fs[:, hh * 512:(hh + 1) * 512], banks2[hh], AF.Copy)

                def s2():
                    self.tt(s2_[:, 2:3], s2_[:, 0:1], s2_[:, 1:2], ALU.add)

                def s3():
                    self.act(s2_[:, 3:4], s2_[:, 2:3], AF.Ln, scale=1.0 / D, bias=self.eps_t)
                    self.act(s2_[:, 4:5], s2_[:, 3:4], AF.Exp, scale=-0.5)

                def s4():
                    for hh in range(2):
                        self.stt(ot[:, hh * 512:(hh + 1) * 512], fs[:, hh * 512:(hh + 1) * 512], s2_[:, 4:5],
                                 pg[:, hh * 512:(hh + 1) * 512], ALU.mult, ALU.mult)
                    self.tt(ot, ot, xres, ALU.add)
                    self.dma(H2[row:row + 128, :], ot, "stC0", eng="pool")
                return [s0, s1, s2, s3, s4]

            for o in range(8):
                items.append((yb(o), []))
            for tt in range(4):
                items.append((op(tt), [6]))
            return items

        def kv_items(seq):
            items = []
            st = {}

            def kvp(t):
                blk, tt = divmod(t, 4)

                def s0():
                    u = uT[0]
                    if tt == 0:
                        self.dma(u, UTv[:, :, seq * L + blk * 512:seq * L + (blk + 1) * 512], "uTC0")
                    bank = side_bank()
                    st[("b", t)] = bank
                    for c in range(8):
                        self.mm(bank, u[:, c, tt * 128:(tt + 1) * 128], Wkv[:, c, :], c == 0, c == 7)

                def sv():
                    self.act(Vt[:, t, :], st[("b", t)][:, 256:512], AF.Copy)

                rs_ = rope_stages(lambda: st[("b", t)][:, 0:256], 256, gk, lambda: t, lambda: kr[t % 3])

                def s1():
                    sv()
                    rs_[0]()

                def s6():
                    trb = side_bank().bc(BF16)
                    st[("t", t)] = trb
                    for hk in range(2):
                        self.tr(trb[:, hk * 128:(hk + 1) * 128], kr[t % 3][:, hk * 128:(hk + 1) * 128], self.ident)

                def s7():
                    self.copy(KT[:, :, t * 128:(t + 1) * 128], st[("t", t)][:, 0:256].re("p (h t) -> p h t", h=2))
                return [s0, s1] + rs_[1:] + [s6, s7]

            for t in range(16):
                items.append((kvp(t), []))
            return items

        GAP = 2

        def plan_side(items, nslots, spacing):
            plan = {}
            start_prev, ends = -spacing, []
            for stages, deps in items:
                start = start_prev + spacing
                if deps and ends:
                    start = max(start, max(ends) + deps[0])
                for k, f in enumerate(stages):
                    plan.setdefault(start + GAP * k, []).append(f)
                ends.append(start + GAP * (len(stages) - 1))
                start_prev = start
            return plan

        def run_streams_standalone(streams):
            plan = {}
            for lst in streams:
                for sl, fs_ in plan_side(lst, 0, 4).items():
                    plan.setdefault(sl, []).extend(fs_)
            for sl in sorted(plan):
                for f in plan[sl]:
                    f()

        def head_loop(par, side):
            q_ = qT[par]
            ob = obT[par]
            its = [(h, kt) for h in range(8) for kt in range(16)]
            n = len(its)
            plan = {}
            for lst in side:
                for sl, fs_ in plan_side(lst, n, 4).items():
                    plan.setdefault(sl, []).extend(fs_)

            def issue_st(i):
                h, kt = its[i]
                self.mm(STB[i % 3], KT[:, h // 4, kt * 128:(kt + 1) * 128], q_[:, h, :], True, True)

            issue_st(0)
            issue_st(1)
            pend = []
            DLAG = 3

            def run_due(i):
                while pend and pend[0][0] <= i:
                    pend.pop(0)[1]()

            for i in range(n):
                h, kt = its[i]
                hk = h // 4
                pt = PT[i % 4]
                self.act(pt, STB[i % 3], AF.Exp, scale=sm_scale)
                if i + 2 < n:
                    issue_st(i + 2)
                self.mm(OT, Vt[:, kt, hk * 128:(hk + 1) * 128], pt, kt == 0, kt == 15)
                if not DEN_QUADS:
                    self.mm(DEN, self.ones_bf, pt, kt == 0, kt == 15)
                run_due(i)
                if DEN_QUADS:
                    if kt % 4 == 1:
                        self.tt(PA[0], PT[(i - 1) % 4], pt, ALU.add)
                    elif kt % 4 == 3:
                        self.tt(PA[1], PT[(i - 1) % 4], pt, ALU.add)
                        pq = PQ[(i // 4) % 3]
                        self.tt(pq, PA[0], PA[1], ALU.add)

                        def den_mm(g_=kt // 4, pq_=pq):
                            self.mm(DEN, self.ones_bf, pq_, g_ == 0, g_ == 3)
                        pend.append((i + DLAG, den_mm))
                if kt == 15:
                    ots_ = ots[h % 2]
                    self.copy(ots_, OT)
                    self.act(lden, DEN, AF.Ln)

                    def fin(h_=h, ots__=ots_):
                        self.act(rden, lden, AF.Exp, scale=-1.0)
                        self.tt(ob[:, h_, :], ots__, rden, ALU.mult)
                    pend.append((i + (DLAG if DEN_QUADS else 1), fin))
                for f in plan.get(i, ()):
                    f()
            run_due(n + DLAG)
            for sl in sorted(k for k in plan if k >= n):
                for f in plan[sl]:
                    f()

        for seq in range(nseq):
            kv_, q_ = kv_items(seq), q_items(seq, 0, 0)
            merged = []
            while kv_ or q_:
                if kv_:
                    merged.append(kv_.pop(0))
                if q_:
                    merged.append(q_.pop(0))
            run_streams_standalone([merged])
            for blk in range(4):
                par = blk % 2
                side = []
                if blk > 0:
                    side.append(epi_items(seq, blk - 1, 1 - par))
                if blk < 3:
                    side.append(q_items(seq, blk + 1, 1 - par))
                head_loop(par, side)
            run_streams_standalone([epi_items(seq, 3, 1)])

    def build(self, phases=("ffn1", "glaA", "glaB", "attn", "ffn2"), nblocks=None, nseq=SEQ_PER_CORE):
        self.nseq = nseq
        nc = self.nc
        I = {}
        I["x"] = self.dram_in("x", [NTOK, D])
        for n, shp in (("ffn1_pre_g", [D]), ("ffn1_w_in", [D, 2 * DFF]), ("ffn1_w_out", [DFF, D]),
                       ("ffn1_post_g", [D]), ("mix_pre_g", [D]), ("w_in", [D, N_IN]),
                       ("gla_decay_up_f", [16, 512]), ("gla_decay_bias_f", [512]),
                       ("gla_decay_up_b", [16, 512]), ("gla_decay_bias_b", [512]),
                       ("gla_out_g", [4, 256]), ("w_branch_a", [D, D]), ("att_q_norm_g", [128]),
                       ("att_k_norm_g", [128]), ("w_branch_b", [D, D]), ("w_out", [D, D]),
                       ("mix_post_g", [D]), ("ffn2_pre_g", [D]), ("ffn2_w_in", [D, 2 * DFF]),
                       ("ffn2_w_out", [DFF, D]), ("ffn2_post_g", [D])):
            I[n] = self.dram_in(n, shp)
        I["rope"] = self.dram_in("rope", [L, 256])
        out = self.dram_out("out", [NTOK, D])
        H1 = self.dram_scr("H1", [NTOK, D], F32)
        H2 = self.dram_scr("H2", [NTOK, D], F32)
        self.I = I
        self.setup()
        self.eps_t = self.sb("eps_t", 1, F32)
        self.memset(self.eps_t, EPS)
        self.eps4_t = self.sb("eps4_t", 1, F32)
        self.memset(self.eps4_t, 4 * EPS)
        self.arena_floor = self.arena_off
        nb = nblocks or NTOK // 512
        self.ebl = self.sb("ebl", 128, F32)
        self.arena_floor = self.arena_off
        S = {}
        for n in ("Qf", "Qb", "Kf", "Kb", "KFT"):
            S[n] = self.dram_scr("scr_" + n, [32, 128, 512], BF16)
        for n in ("V", "SGR", "SB"):
            S[n] = self.dram_scr("scr_" + n, [32, 128, 1024], BF16)
        S["UT"] = self.dram_scr("scr_UT", [D, NTOK], BF16)
        S["MA"] = self.dram_scr("scr_MA", [D, NTOK], BF16)
        S["SGB"] = self.dram_scr("scr_SGB", [D, NTOK], BF16)
        self.S = S
        self.WB = {}
        last = phases[-1]
        if "ffn1" in phases:
            self.ffn_phase(I["x"], out if last == "ffn1" else H1, I["ffn1_pre_g"], ("ffn1_w_in", "ffn1_w_out"),
                           I["ffn1_post_g"], "a", nblocks=nb,
                           after_weights=self.convert_weights if len(phases) > 1 else None)
            self.phase_reset()
        src_mix = H1 if "ffn1" in phases else I["x"]
        nseq = self.nseq
        if "glaA" in phases:
            self.gla_pass_a(src_mix, S, nseq)
            self.phase_reset()
        if "glaB" in phases:
            self.gla_pass_b(S, nseq)
            self.phase_reset()
        if "attn" in phases:
            self.attn_phase(src_mix, out if last == "attn" else H2, S, nseq)
            self.phase_reset()
        if "ffn2" in phases:
            self.ffn_phase(H2, out, I["ffn2_pre_g"], ("ffn2_w_in", "ffn2_w_out"), I["ffn2_post_g"], "b", nblocks=nb)
            self.phase_reset()
        self.p.final_wait()
        self.p.emit(nc, self.es)
        self.es.close()
        return nc


_ROPE = None


def rope_tables():
    global _ROPE
    if _ROPE is None:
        half = 64
        inv = 10000.0 ** (-np.arange(0, half, 2, dtype=np.float32) / half)
        t = np.arange(L)
        row = (t // 64).astype(np.float32)[:, None] * inv[None, :]
        col = (t % 64).astype(np.float32)[:, None] * inv[None, :]
        cr, sr, cc, sc = np.cos(row), np.sin(row), np.cos(col), np.sin(col)
        C = np.concatenate([cr, cr, cc, cc], axis=1)
        S = np.concatenate([-sr, sr, -sc, sc], axis=1)
        _ROPE = np.ascontiguousarray(np.concatenate([C, S], axis=1).astype(np.float32))
    return _ROPE


def make_in_maps(inputs):
    x = np.ascontiguousarray(np.asarray(inputs["x"], dtype=np.float32))
    maps = []
    shared = {}
    for k, v in inputs.items():
        if k == "x":
            continue
        a = np.asarray(v, dtype=np.float32)
        shared[k] = np.ascontiguousarray(a.reshape(a.shape[1:]))
    shared["rope"] = rope_tables()
    for c in range(NCORES):
        m = dict(shared)
        m["x"] = x[c * SEQ_PER_CORE:(c + 1) * SEQ_PER_CORE].reshape(NTOK, D)
        maps.append(m)
    return maps


_NC = None


def kernel(**inputs):
    global _NC
    if _NC is None:
        _NC = Builder().build()
    maps = make_in_maps(inputs)
    res = run_bass_kernel_spmd(_NC, maps, core_ids=list(range(NCORES)))
    outs = [np.asarray(r["out"]).reshape(SEQ_PER_CORE, L, D) for r in res.results]
    return np.concatenate(outs, axis=0).astype(np.float32)
```
